# Optimizing a Trainium2 kernel written in Bass

```python
import math
import jax, jax.numpy as jnp
from jax import lax
import numpy as np

D_MODEL = 1024
BATCH = 32
SEQ = 2048
DEPTH = 1
DEC_BATCH = 32
DEC_SEQ = 64
PAST_LEN = 4096

CHUNK = 64
Q_BLOCK = 128
W_A = 1024
N_LRU_BLOCKS = 8
LRU_BLOCK = W_A // N_LRU_BLOCKS
CONV_W = 4
LRU_C = 8.0
H_B = 8
DK = 64
DV = 128
W_B = H_B * DV
NUM_BUCKETS = 32
MAX_DISTANCE = 128
EPS = 1e-6
IN_COLS = 2 * W_A + 2 * H_B * 2 * DK + 2 * W_B + 2 * D_MODEL

kernel_name = "hawk_diffattn_parallel_stream_step"


def rms_norm(x, g):
    xf = x.astype(jnp.float32)
    y = xf * lax.rsqrt(jnp.mean(xf * xf, axis=-1, keepdims=True) + EPS)
    return (y * g.astype(jnp.float32)).astype(x.dtype)


def split_in(u):
    sizes = [W_A, W_A, H_B * 2 * DK, H_B * 2 * DK, W_B, W_B, D_MODEL, D_MODEL]
    idx = [int(v) for v in np.cumsum(sizes)[:-1]]
    return jnp.split(u, idx, axis=-1)


def rel_bucket(rel):
    nb = NUM_BUCKETS // 2
    max_exact = nb // 2
    n = jnp.abs(rel)
    large = max_exact + (jnp.log(jnp.maximum(n, 1).astype(jnp.float32) / max_exact)
                         / math.log(MAX_DISTANCE / max_exact) * (nb - max_exact)).astype(jnp.int32)
    large = jnp.minimum(large, nb - 1)
    return jnp.where(rel > 0, nb, 0) + jnp.where(n < max_exact, n, large)


def diff_attn(q, k, v, q_pos, k_pos, rel_bias, lam):
    s = jnp.einsum('bqhmd,bkhmd->bhmqk', q.astype(jnp.float32), k.astype(jnp.float32)) * (DK ** -0.5)
    bias = rel_bias.astype(jnp.float32)[rel_bucket(k_pos[None, :] - q_pos[:, None])]
    bias = jnp.transpose(bias, (2, 0, 1))[None, :, None]
    visible = (k_pos[None, :] // CHUNK) <= (q_pos[:, None] // CHUNK)
    s = jnp.where(visible, s + bias, -jnp.inf)
    p = jax.nn.softmax(s, axis=-1)
    w = p[:, :, 0] - lam * p[:, :, 1]
    return jnp.einsum('bhqk,bkhd->bqhd', w, v.astype(jnp.float32))


def rglru(xc, h0, is_first, w_rg, b_rg, w_ig, b_ig, lru_lambda):
    B, T, _ = xc.shape
    xf = xc.astype(jnp.float32)
    xb = xf.reshape(B, T, N_LRU_BLOCKS, LRU_BLOCK)
    r = jax.nn.sigmoid(jnp.einsum('btnc,ncd->btnd', xb, w_rg.astype(jnp.float32)).reshape(B, T, W_A)
                       + b_rg.astype(jnp.float32))
    i = jax.nn.sigmoid(jnp.einsum('btnc,ncd->btnd', xb, w_ig.astype(jnp.float32)).reshape(B, T, W_A)
                       + b_ig.astype(jnp.float32))
    log_a = -LRU_C * r * jax.nn.softplus(-lru_lambda.astype(jnp.float32))
    a = jnp.exp(log_a)
    mult = jnp.sqrt(-jnp.expm1(2.0 * log_a))
    if is_first:
        mult = mult.at[:, 0].set(1.0)
    b = mult * i * xf
    b = b.at[:, 0].add(a[:, 0] * h0.astype(jnp.float32))

    def comb(left, right):
        return (left[0] * right[0], right[0] * left[1] + right[1])

    _, h = lax.associative_scan(comb, (a, b), axis=1)
    return h, h[:, -1]


def mixer_layer(x, conv_prev, h0, k_past, v_past, is_first, lam_init, rel_bias,
                norm_gain, w_in, conv_w, conv_b, w_rg, b_rg, w_ig, b_ig, lru_lambda,
                q_norm_gain, k_norm_gain, lam_q1, lam_k1, lam_q2, lam_k2, subln_gain,
                w_proj_a, w_proj_b, w_out):
    B, T, _ = x.shape
    xn = rms_norm(x, norm_gain)
    u = xn @ w_in
    xa, za, q, k, v, zb, ga, gb = split_in(u)

    xpad = jnp.concatenate([conv_prev.astype(xa.dtype), xa], axis=1)
    new_conv = xpad[:, -(CONV_W - 1):]
    xc = conv_b + sum(xpad[:, j:j + T] * conv_w[j] for j in range(CONV_W))
    h, h_last = rglru(xc, h0, is_first, w_rg, b_rg, w_ig, b_ig, lru_lambda)
    ya = h.astype(x.dtype) * jax.nn.silu(za)

    q = rms_norm(q.reshape(B, T, H_B, 2, DK), q_norm_gain)
    k = rms_norm(k.reshape(B, T, H_B, 2, DK), k_norm_gain)
    v = v.reshape(B, T, H_B, DV)
    lf = lambda t: t.astype(jnp.float32)
    lam = (jnp.exp(jnp.sum(lf(lam_q1) * lf(lam_k1))) - jnp.exp(jnp.sum(lf(lam_q2) * lf(lam_k2)))
           + lam_init)
    if is_first:
        pos = jnp.arange(T)
        outs = []
        for blk in range(T // Q_BLOCK):
            s0, e0 = blk * Q_BLOCK, (blk + 1) * Q_BLOCK
            outs.append(diff_attn(q[:, s0:e0], k[:, :e0], v[:, :e0], pos[s0:e0], pos[:e0], rel_bias, lam))
        o = jnp.concatenate(outs, axis=1)
    else:
        P = k_past.shape[1]
        k_all = jnp.concatenate([k_past.reshape(B, P, H_B, 2, DK).astype(k.dtype), k], axis=1)
        v_all = jnp.concatenate([v_past.astype(v.dtype), v], axis=1)
        pos_all = jnp.arange(P + T)
        o = diff_attn(q, k_all, v_all, pos_all[P:], pos_all, rel_bias, lam)
    o = rms_norm(o, subln_gain) * (1.0 - lam_init)
    yb = o.reshape(B, T, W_B).astype(x.dtype) * jax.nn.silu(zb)

    m = jax.nn.sigmoid(ga) * (ya @ w_proj_a) + jax.nn.sigmoid(gb) * (yb @ w_proj_b)
    y = x + m @ w_out
    return y, k.reshape(B, T, H_B, 2 * DK), v, new_conv, h_last


def setup_inputs(seed: int = 0) -> dict:
    key = jax.random.key(seed)
    ks = jax.random.split(key, 32)
    nrm = lambda kk, shape, s: jax.random.normal(kk, shape, jnp.float32) * s
    a0 = jax.random.uniform(ks[10], (DEPTH, W_A), jnp.float32, 0.9, 0.999)
    sig = a0 ** (1.0 / LRU_C)
    lru_lambda = jnp.log(sig) - jnp.log1p(-sig)
    return {
        "x_prompt": nrm(ks[0], (BATCH, SEQ, D_MODEL), 1.0),
        "x_sample": nrm(ks[1], (DEC_BATCH, DEC_SEQ, D_MODEL), 1.0),
        "cache_k": nrm(ks[2], (DEPTH, DEC_BATCH, PAST_LEN, H_B, 2 * DK), 1.0),
        "cache_v": nrm(ks[3], (DEPTH, DEC_BATCH, PAST_LEN, H_B, DV), 1.0),
        "state_conv": nrm(ks[4], (DEPTH, DEC_BATCH, CONV_W - 1, W_A), 1.0),
        "state_lru": nrm(ks[5], (DEPTH, DEC_BATCH, W_A), 0.5),
        "norm_gain": 1.0 + nrm(ks[6], (DEPTH, D_MODEL), 0.05),
        "w_in": nrm(ks[7], (DEPTH, D_MODEL, IN_COLS), D_MODEL ** -0.5),
        "conv_w": nrm(ks[8], (DEPTH, CONV_W, W_A), CONV_W ** -0.5),
        "conv_b": nrm(ks[9], (DEPTH, W_A), 0.01),
        "w_rg": nrm(ks[11], (DEPTH, N_LRU_BLOCKS, LRU_BLOCK, LRU_BLOCK), LRU_BLOCK ** -0.5),
        "b_rg": nrm(ks[12], (DEPTH, W_A), 0.01),
        "w_ig": nrm(ks[13], (DEPTH, N_LRU_BLOCKS, LRU_BLOCK, LRU_BLOCK), LRU_BLOCK ** -0.5),
        "b_ig": nrm(ks[14], (DEPTH, W_A), 0.01),
        "lru_lambda": lru_lambda,
        "q_norm_gain": 1.0 + nrm(ks[15], (DEPTH, DK), 0.05),
        "k_norm_gain": 1.0 + nrm(ks[16], (DEPTH, DK), 0.05),
        "rel_bias": nrm(ks[17], (NUM_BUCKETS, H_B), 0.2),
        "lam_q1": nrm(ks[18], (DEPTH, DK), 0.1),
        "lam_k1": nrm(ks[19], (DEPTH, DK), 0.1),
        "lam_q2": nrm(ks[20], (DEPTH, DK), 0.1),
        "lam_k2": nrm(ks[21], (DEPTH, DK), 0.1),
        "subln_gain": 1.0 + nrm(ks[22], (DEPTH, DV), 0.05),
        "w_proj_a": nrm(ks[23], (DEPTH, W_A, D_MODEL), W_A ** -0.5),
        "w_proj_b": nrm(ks[24], (DEPTH, W_B, D_MODEL), W_B ** -0.5),
        "w_out": nrm(ks[25], (DEPTH, D_MODEL, D_MODEL), D_MODEL ** -0.5),
    }


def reference(x_prompt, x_sample, cache_k, cache_v, state_conv, state_lru,
              norm_gain, w_in, conv_w, conv_b, w_rg, b_rg, w_ig, b_ig, lru_lambda,
              q_norm_gain, k_norm_gain, rel_bias, lam_q1, lam_k1, lam_q2, lam_k2,
              subln_gain, w_proj_a, w_proj_b, w_out):
    hp, hs = x_prompt, x_sample
    Bp = x_prompt.shape[0]
    kp_l, vp_l, cp_l, sp_l, ks_l, vs_l, cs_l, ss_l = [], [], [], [], [], [], [], []
    for l in range(DEPTH):
        lam_init = 0.8 - 0.6 * math.exp(-0.3 * l)
        lw = (norm_gain[l], w_in[l], conv_w[l], conv_b[l], w_rg[l], b_rg[l], w_ig[l], b_ig[l],
              lru_lambda[l], q_norm_gain[l], k_norm_gain[l], lam_q1[l], lam_k1[l], lam_q2[l],
              lam_k2[l], subln_gain[l], w_proj_a[l], w_proj_b[l], w_out[l])
        conv0 = jnp.zeros((Bp, CONV_W - 1, W_A), hp.dtype)
        h0 = jnp.zeros((Bp, W_A), jnp.float32)
        hp, kp, vp, cp, sp = mixer_layer(hp, conv0, h0, None, None, True, lam_init, rel_bias, *lw)
        hs, kk, vv, cs, ss = mixer_layer(hs, state_conv[l], state_lru[l], cache_k[l], cache_v[l],
                                         False, lam_init, rel_bias, *lw)
        kp_l.append(kp); vp_l.append(vp); cp_l.append(cp); sp_l.append(sp.astype(hp.dtype))
        ks_l.append(kk); vs_l.append(vv); cs_l.append(cs); ss_l.append(ss.astype(state_lru.dtype))
    return (hp, hs,
            jnp.stack(kp_l), jnp.stack(vp_l), jnp.stack(cp_l), jnp.stack(sp_l),
            jnp.stack(ks_l), jnp.stack(vs_l), jnp.stack(cs_l), jnp.stack(ss_l))
```

```python
import numpy as np
from collections import deque
from contextlib import ExitStack
import concourse.bass as bass
import concourse.mybir as mybir
from concourse.bass_utils import run_bass_kernel_spmd

F32 = mybir.dt.float32
BF16 = mybir.dt.bfloat16
AF = mybir.ActivationFunctionType
ALU = mybir.AluOpType
AX = mybir.AxisListType

NCORES = 8
D = 1024
T = 2048
BL = 4
TS = 64
PAST = 4096
EPS = 1e-6
LAM_INIT = 0.2
NEG = -30000.0


class Ev:
    __slots__ = ("sem", "val", "key", "opid")

    def __init__(self, sem, val, key, opid=None):
        self.sem, self.val, self.key, self.opid = sem, val, key, opid


class Buf:
    def __init__(self, name):
        self.name = name
        self.w = None
        self.r = {}
        self.dsem = None
        self.psum = name.startswith("PS")


class Eng:
    def __init__(self, name, h, is_pe=False):
        self.name, self.h, self.is_pe = name, h, is_pe
        self.sem = None
        self.key = None
        self.cnt = 0
        self.seq = 0
        self.seen = {}
        self.used = None

    def wait(self, ev):
        if self.seen.get(ev.key, 0) >= ev.val:
            return
        self.h.wait_ge(ev.sem, ev.val)
        self.seen[ev.key] = ev.val
        if ev.opid is not None and self.used is not None:
            self.used.add(ev.opid)


class K:
    def __init__(self, nc, stack, needed=None):
        self.nc = nc
        self.stack = stack
        self.needed = needed
        self.used = set()
        self.pe = Eng("pe", nc.tensor, True)
        self.act = Eng("act", nc.scalar)
        self.dve = Eng("dve", nc.vector)
        self.pool = Eng("pool", nc.gpsimd)
        self.sp = Eng("sp", nc.sync)
        for e in (self.pe, self.act, self.dve, self.pool, self.sp):
            e.used = self.used
        self.nsem = 0
        self.out_events = []
        self.new_epoch()

    def sem(self, name):
        self.nsem += 1
        return self.stack.enter_context(self.nc.semaphore(f"{name}_{self.nsem}"))

    def new_epoch(self):
        for e in (self.pe, self.act, self.dve, self.pool):
            e.sem = self.sem("e" + e.name)
            e.key = id(e.sem)
            e.cnt = 0

    def _deps(self, eng, reads, writes, own_key):
        for b in reads:
            if b.w is not None:
                ev = b.w
                if ev.key == own_key and eng.is_pe:
                    continue
                if ev.key == own_key and own_key != eng.key:
                    continue
                eng.wait(ev)
            if b.psum:
                for ev in list(b.r.values()):
                    if ev.key != own_key:
                        eng.wait(ev)
        for b in writes:
            evs = list(b.r.values())
            if b.w is not None:
                evs.append(b.w)
            for ev in evs:
                if ev.key == own_key:
                    continue
                eng.wait(ev)

    def op(self, eng, fn, reads=(), writes=()):
        self._deps(eng, reads, writes, eng.key)
        ins = fn()
        eng.seq += 1
        opid = (eng.name, eng.seq)
        if self.needed is None or opid in self.needed:
            eng.cnt += 1
            ins.then_inc(eng.sem, 1)
        ev = Ev(eng.sem, eng.cnt, eng.key, opid)
        for b in writes:
            b.w = ev
            b.r = {}
        for b in reads:
            b.r[ev.key] = ev
        return ev

    def dma(self, q, out, in_, reads=(), writes=(), track=None, is_output=False):
        tb = track if track is not None else (writes[0] if writes else reads[0])
        if tb.dsem is None:
            tb.dsem = {}
        if q.name not in tb.dsem:
            s = self.sem("d" + tb.name + q.name)
            tb.dsem[q.name] = [s, id(s), 0]
        ds = tb.dsem[q.name]
        s, key = ds[0], ds[1]
        self._deps(q, reads, writes, key)
        ds[2] += 1
        q.h.dma_start(out=out, in_=in_).then_inc(s, 16)
        ev = Ev(s, 16 * ds[2], key)
        for b in writes:
            b.w = ev
            b.r = {}
        for b in reads:
            b.r[ev.key] = ev
        if is_output:
            self.out_events.append(ev)
        return ev


class Ring:
    def __init__(self, items):
        self.items = items
        self.i = 0

    def next(self):
        it = self.items[self.i % len(self.items)]
        self.i += 1
        return it


def _bucket_table():
    s = np.arange(384)
    rel = (127 - s).astype(np.int64)
    nb, max_exact = 16, 8
    n = np.abs(rel)
    nf = np.maximum(n, 1).astype(np.float32)
    large = max_exact + (np.log(nf / np.float32(max_exact)) / np.float32(np.log(128 / max_exact))
                         * np.float32(nb - max_exact)).astype(np.int32)
    large = np.minimum(large, nb - 1)
    b = np.where(rel > 0, nb, 0) + np.where(n < max_exact, n, large)
    oh = np.zeros((32, 384), np.float32)
    oh[b, s] = 1.0
    return oh


class _Stop(Exception):
    pass


def build_program(jobs=(0, 1, 2, 3), do_sample=True, heads=tuple(range(8)), stop=None, mini=False):
    _, used = _build(jobs, do_sample, heads, stop, None, mini)
    nc, _ = _build(jobs, do_sample, heads, stop, used, mini)
    return nc


def _build(jobs, do_sample, heads, stop, needed, mini=False):
    BLx = 1 if mini else BL
    PASTx = 128 if mini else PAST
    nc = bass.Bass("TRN2", target_bir_lowering=False)
    din = {}

    def inp(name, shape):
        din[name] = nc.dram_tensor(name, list(shape), F32, kind="ExternalInput").ap()
        return din[name]

    def outp(name, shape):
        return nc.dram_tensor(name, list(shape), F32, kind="ExternalOutput").ap()

    x_p = inp("x_prompt", (BLx, T, D))
    x_s = inp("x_sample", (BL * TS, D))
    ck = inp("cache_k", (BLx, PASTx, 1024))
    cv = inp("cache_v", (BLx, PASTx, 1024))
    st_conv = inp("state_conv", (BL, 3, 1024))
    st_lru = inp("state_lru", (BL, 1024))
    norm_gain = inp("norm_gain", (1, 1024))
    w_in = inp("w_in", (1024, 8192))
    conv_w = inp("conv_w", (4, 1024))
    conv_b = inp("conv_b", (1, 1024))
    w_rg = inp("w_rg", (8, 128, 128))
    b_rg = inp("b_rg", (1, 1024))
    w_ig = inp("w_ig", (8, 128, 128))
    b_ig = inp("b_ig", (1, 1024))
    lru_lambda = inp("lru_lambda", (1, 1024))
    q_gain = inp("q_norm_gain", (1, 64))
    k_gain = inp("k_norm_gain", (1, 64))
    rel_bias = inp("rel_bias", (32, 8))
    lam_q1 = inp("lam_q1", (1, 64))
    lam_k1 = inp("lam_k1", (1, 64))
    lam_q2 = inp("lam_q2", (1, 64))
    lam_k2 = inp("lam_k2", (1, 64))
    subln_gain = inp("subln_gain", (128, 1))
    w_pa = inp("w_proj_a", (1024, 1024))
    w_pb = inp("w_proj_b", (1024, 1024))
    w_out = inp("w_out", (1024, 1024))
    c_oh = inp("c_onehot", (32, 384))
    c_ident = inp("c_ident", (128, 128))
    c_bones = inp("c_blockones", (128, 128))

    y_p = outp("y_prompt", (BLx, T, D))
    y_s = outp("y_sample", (BL * TS, D))
    k_p = outp("k_prompt", (BLx, T, 1024))
    v_p = outp("v_prompt", (BLx, T, 1024))
    conv_p = outp("conv_prompt", (BL, 3, 1024))
    lru_p = outp("lru_prompt", (BL, 1024))
    k_s = outp("k_sample", (BL * TS, 1024))
    v_s = outp("v_sample", (BL * TS, 1024))
    conv_s = outp("conv_sample", (BL, 3, 1024))
    lru_s = outp("lru_sample", (BL, 1024))

    WF = nc.dram_tensor("WF", [48, 128, 8, 128], BF16, kind="Internal").ap()
    WKV = nc.dram_tensor("WKV", [8, 128, 8, 256], BF16, kind="Internal").ap()
    WP = nc.dram_tensor("WP", [2, 8, 128, 8, 128], BF16, kind="Internal").ap()
    WO = nc.dram_tensor("WO", [2, 128, 8, 512], BF16, kind="Internal").ap()
    REP_t = nc.dram_tensor("REP", [8, 128, 384], F32, kind="Internal")
    REP = REP_t.ap()

    with ExitStack() as stack:
        kb = K(nc, stack, needed)
        PE, ACT, DVE, POOL, SP = kb.pe, kb.act, kb.dve, kb.pool, kb.sp
        op, dma = kb.op, kb.dma
        stack.enter_context(nc.allow_non_contiguous_dma(reason="small param / state layout DMAs"))

        def sb(name, shape, dt):
            return stack.enter_context(nc.sbuf_tensor(name, list(shape), dt))

        XT = sb("XT", [128, 8, T], BF16)
        XTb = Buf("XT")
        YB = sb("YB", [128, 8, T], BF16)
        YBb = Buf("YB")
        QT = sb("QT", [128, 2, T], BF16)
        QTb = Buf("QT")
        KT = sb("KT", [128, PAST + 128], BF16)
        KTb = Buf("KT")
        NVS = 1
        VS = [sb(f"VS{i}", [128, 33, 130], BF16) for i in range(NVS)]
        VSb = [Buf(f"VS{i}") for i in range(NVS)]
        KC = [sb(f"KC{i}", [128, 32, 128], BF16) for i in range(NVS)]
        KCb = [Buf(f"KC{i}") for i in range(NVS)]
        ZB = sb("ZB", [128, T], BF16)
        ZBb = Buf("ZB")
        YA = sb("YA", [128, 8, 512], BF16)
        YAb = Buf("YA")
        MM = sb("MM", [128, 8, 512], BF16)
        MMb = Buf("MM")
        XAw = sb("XAw", [128, 640], F32)
        XAwb = Buf("XAw")
        HALO = sb("HALO", [128, 8, 4, 3], F32)
        HALOb = Buf("HALO")
        HC = sb("HC", [128, 8, 4], F32)
        HCb = Buf("HC")
        NG = 12
        G = [sb(f"G{i}", [128, 512], F32) for i in range(NG)]
        Gb = [Buf(f"G{i}") for i in range(NG)]
        gfree = deque(range(NG))
        WR = [sb(f"WR{i}", [128, 4096], BF16) for i in range(2)]
        WRb = [Buf(f"WR{i}") for i in range(2)]
        wring = Ring([0, 1])
        ET = [sb(f"ET{i}", [128, 512], BF16) for i in range(3)]
        ETb = [Buf(f"ET{i}") for i in range(3)]
        ering = Ring([0, 1, 2])
        XS = [sb(f"XS{i}", [128, 1024], F32) for i in range(2)]
        XSb = [Buf(f"XS{i}") for i in range(2)]
        xsring = Ring([0, 1])
        XNB = [sb(f"XNB{i}", [128, 1024], BF16) for i in range(2)]
        XNBb = [Buf(f"XNB{i}") for i in range(2)]
        SM = [sb(f"SM{i}", [128, 8], F32) for i in range(24)]
        SMb = [Buf(f"SM{i}") for i in range(24)]
        smring = Ring(list(range(24)))
        g_row = sb("g_row", [128, 1024], F32)
        gk_rep = sb("gk_rep", [128, 128], F32)
        gq2 = sb("gq2", [128, 1], F32)
        sg8 = sb("sg8", [128, 1], F32)
        lamt = sb("lamt", [128, 4, 64], F32)
        lamc = sb("lamc", [128, 8], F32)
        c15 = sb("c15", [128, 8], F32)
        rb = sb("rb", [32, 8], F32)
        ohr = sb("ohr", [32, 384], F32)
        ident = sb("ident", [128, 128], BF16)
        bones = sb("bones", [128, 128], BF16)
        BDb = sb("BDb", [128, 8, 128], BF16)
        BSb = sb("BSb", [128, 8, 128], BF16)
        cw = sb("cw", [128, 8, 4], F32)
        cbias = sb("cbias", [128, 8], F32)
        brg = sb("brg", [128, 8], F32)
        big = sb("big", [128, 8], F32)
        lam_l = sb("lam_l", [128, 8], F32)
        nsp = sb("nsp", [128, 8], F32)
        WRG = sb("WRG", [128, 8, 128], BF16)
        WIG = sb("WIG", [128, 8, 128], BF16)
        SCV = sb("SCV", [128, 4, 8, 3], F32)
        LR0 = sb("LR0", [128, 4, 8], F32)
        CONVST = sb("CONVST", [128, 4, 8, 3], F32)
        CST = Buf("consts")
        CB = {}

        def cb(name):
            if name not in CB:
                CB[name] = Buf("c" + name)
            return CB[name]
        CONVSTb = Buf("CONVST")

        PS = [stack.enter_context(nc.psum_tensor(f"PS{i}", [128, 512], F32)) for i in range(8)]
        PSb = [Buf(f"PS{i}") for i in range(8)]
        ringA = Ring([0, 1, 2, 3])
        ringB = Ring([4, 5, 6, 7])
        ringAll = Ring([0, 1, 2, 3, 4, 5, 6, 7])

        def galloc():
            return gfree.popleft()

        def grel(*ids):
            for i in ids:
                gfree.append(i)

        if True:
            def body():
                wf_b = [Buf(f"WFd{g}") for g in range(6)]
                srcbase = [0, 1024, 2048, 5120, 6144, 7168]
                for g in range(6):
                    for cc in range(8):
                        c0 = srcbase[g] + cc * 128
                        dma(POOL, WF[g * 8 + cc], w_in[:, c0:c0 + 128].rearrange("(kc p) col -> p kc col", p=128),
                            writes=[wf_b[g]])
                wkv_b = Buf("WKVd")
                for h in range(8):
                    dma(POOL, WKV[h, :, :, 0:128],
                        w_in[:, 3072 + h * 128:3072 + (h + 1) * 128].rearrange("(kc p) c -> p kc c", p=128), writes=[wkv_b])
                    dma(POOL, WKV[h, :, :, 128:256],
                        w_in[:, 4096 + h * 128:4096 + (h + 1) * 128].rearrange("(kc p) c -> p kc c", p=128), writes=[wkv_b])
                wp_b = Buf("WPd")
                for j in range(8):
                    dma(POOL, WP[0, j], w_pa[:, j * 128:(j + 1) * 128].rearrange("(c p) col -> p c col", p=128), writes=[wp_b])
                    dma(POOL, WP[1, j], w_pb[:, j * 128:(j + 1) * 128].rearrange("(c p) col -> p c col", p=128), writes=[wp_b])
                wo_b = Buf("WOd")
                for n in range(2):
                    dma(POOL, WO[n], w_out[:, n * 512:(n + 1) * 512].rearrange("(j p) col -> p j col", p=128), writes=[wo_b])
                dma(POOL, WRG[:], w_rg.rearrange("n c d -> c n d"), writes=[cb("WRG")])
                dma(POOL, WIG[:], w_ig.rearrange("n c d -> c n d"), writes=[cb("WIG")])
                dma(POOL, ident[:], c_ident, writes=[cb("ident")])
                dma(POOL, bones[:], c_bones, writes=[cb("bones")])
                dma(SP, g_row[:], norm_gain.to_broadcast([128, 1024]), writes=[cb("g_row")])
                dma(SP, gk_rep[:, 0:64], k_gain.to_broadcast([128, 64]), writes=[cb("gk_rep")])
                dma(SP, gk_rep[:, 64:128], k_gain.to_broadcast([128, 64]), writes=[cb("gk_rep")])
                dma(SP, gq2[0:64, :], q_gain.rearrange("o d -> d o"), writes=[cb("gq2")])
                dma(SP, gq2[64:128, :], q_gain.rearrange("o d -> d o"), writes=[cb("gq2")])
                dma(SP, sg8[:], subln_gain, writes=[cb("sg8")])
                for i, lv in enumerate((lam_q1, lam_k1, lam_q2, lam_k2)):
                    dma(SP, lamt[:, i, :], lv.to_broadcast([128, 64]), writes=[cb("lam")])
                dma(SP, c15[:], rel_bias[15:16, :].to_broadcast([128, 8]), writes=[cb("c15")])
                dma(SP, rb[:], rel_bias, writes=[cb("rb")])
                dma(SP, ohr[:], c_oh, writes=[cb("ohr")])
                for j in range(4):
                    dma(SP, cw[:, :, j], conv_w[j].rearrange("(c p) -> p c", p=128), writes=[cb("cw")])
                dma(SP, cbias[:], conv_b.rearrange("o (c p) -> p (o c)", p=128), writes=[cb("cbias")])
                dma(SP, brg[:], b_rg.rearrange("o (c p) -> p (o c)", p=128), writes=[cb("brg")])
                dma(SP, big[:], b_ig.rearrange("o (c p) -> p (o c)", p=128), writes=[cb("big")])
                dma(SP, lam_l[:], lru_lambda.rearrange("o (c p) -> p (o c)", p=128), writes=[cb("nsp")])
                for b in range(BL):
                    for j in range(3):
                        dma(SP, SCV[:, b, :, j], st_conv[b, j].rearrange("(c p) -> p c", p=128), writes=[cb("SCV")])
                dma(SP, LR0[:], st_lru.rearrange("b (c p) -> p b c", p=128), writes=[cb("LR0")])

                op(DVE, lambda: nc.vector.tensor_tensor(out=lamt[:, 0, :], in0=lamt[:, 0, :], in1=lamt[:, 1, :], op=ALU.mult),
                   reads=[cb("lam")], writes=[cb("lam")])
                op(DVE, lambda: nc.vector.tensor_tensor(out=lamt[:, 2, :], in0=lamt[:, 2, :], in1=lamt[:, 3, :], op=ALU.mult),
                   reads=[cb("lam")], writes=[cb("lam")])
                op(DVE, lambda: nc.vector.tensor_reduce(out=lamc[:, 2:3], in_=lamt[:, 0, :], axis=AX.X, op=ALU.add),
                   reads=[cb("lam")], writes=[cb("lamc")])
                op(DVE, lambda: nc.vector.tensor_reduce(out=lamc[:, 3:4], in_=lamt[:, 2, :], axis=AX.X, op=ALU.add),
                   reads=[cb("lam")], writes=[cb("lamc")])
                op(ACT, lambda: nc.scalar.activation(out=lamc[:, 4:6], in_=lamc[:, 2:4], func=AF.Exp),
                   reads=[cb("lamc")], writes=[cb("lamc")])
                op(DVE, lambda: nc.vector.tensor_tensor(out=lamc[:, 6:7], in0=lamc[:, 4:5], in1=lamc[:, 5:6], op=ALU.subtract),
                   reads=[cb("lamc")], writes=[cb("lamc")])
                op(DVE, lambda: nc.vector.tensor_scalar(out=lamc[:, 0:1], in0=lamc[:, 6:7], scalar1=LAM_INIT, scalar2=None,
                                                        op0=ALU.add), reads=[cb("lamc")], writes=[cb("lamc")])
                op(DVE, lambda: nc.vector.tensor_scalar(out=lamc[:, 1:2], in0=lamc[:, 0:1], scalar1=-1.0, scalar2=None,
                                                        op0=ALU.mult), reads=[cb("lamc")], writes=[cb("lamc")])
                op(DVE, lambda: nc.vector.tensor_scalar(out=sg8[:], in0=sg8[:], scalar1=1.0 - LAM_INIT, scalar2=None,
                                                        op0=ALU.mult), reads=[cb("sg8")], writes=[cb("sg8")])
                op(ACT, lambda: nc.scalar.activation(out=nsp[:], in_=lam_l[:], func=AF.Exp, scale=-1.0),
                   reads=[cb("nsp")], writes=[cb("nsp")])
                op(ACT, lambda: nc.scalar.activation(out=nsp[:], in_=nsp[:], func=AF.Ln, bias=1.0),
                   reads=[cb("nsp")], writes=[cb("nsp")])
                op(DVE, lambda: nc.vector.tensor_scalar(out=nsp[:], in0=nsp[:], scalar1=-8.0, scalar2=None, op0=ALU.mult),
                   reads=[cb("nsp")], writes=[cb("nsp")])
                repb = Buf("REPd")
                grb = galloc()
                rbh = G[grb][0:32, :].bitcast(F32)
                for hh in range(2):
                    op(DVE, lambda: nc.vector.tensor_copy(out=rbh[:, 0:512].rearrange("p (h r) -> p h r", h=4),
                                                          in_=rb[:, hh * 4:(hh + 1) * 4].unsqueeze(2).to_broadcast([32, 4, 128])),
                       reads=[cb("rb")], writes=[Gb[grb]])
                    for h4 in range(4):
                        h = hh * 4 + h4
                        bk = ringA.next()
                        gi = galloc()
                        op(PE, lambda: nc.tensor.matmul(PS[bk][:, 0:384], lhsT=rbh[:, h4 * 128:(h4 + 1) * 128], rhs=ohr[:],
                                                        start=True, stop=True),
                           reads=[cb("ohr"), Gb[grb]], writes=[PSb[bk]])
                        op(DVE, lambda: nc.vector.tensor_scalar(out=G[gi][:, 0:384], in0=PS[bk][:, 0:384],
                                                                scalar1=c15[:, h:h + 1], scalar2=8.0,
                                                                op0=ALU.subtract, op1=ALU.mult),
                           reads=[PSb[bk], cb("c15")], writes=[Gb[gi]])
                        dma(SP, REP[h], G[gi][:, 0:384], reads=[Gb[gi]], writes=[repb], track=Gb[gi])
                        grel(gi)
                grel(grb)
                for hh in range(2):
                    gd, gs_ = galloc(), galloc()
                    bdv = G[gd][:].rearrange("p (h q) -> p h q", h=4)
                    bsv = G[gs_][:].rearrange("p (h q) -> p h q", h=4)
                    skd = bass.AP(tensor=REP_t, offset=hh * 4 * 128 * 384 + 127, ap=[[383, 128], [128 * 384, 4], [1, 128]])
                    sks = bass.AP(tensor=REP_t, offset=hh * 4 * 128 * 384 + 255, ap=[[383, 128], [128 * 384, 4], [1, 128]])
                    dma(SP, bdv, skd, reads=[repb], writes=[Gb[gd]])
                    dma(SP, bsv, sks, reads=[repb], writes=[Gb[gs_]])
                    op(POOL, lambda: nc.gpsimd.memset(bdv[64:128, :, 0:64], NEG), writes=[Gb[gd]])
                    op(DVE, lambda: nc.vector.tensor_copy(out=BDb[:, hh * 4:(hh + 1) * 4, :], in_=bdv), reads=[Gb[gd]], writes=[cb("BDb")])
                    op(DVE, lambda: nc.vector.tensor_copy(out=BSb[:, hh * 4:(hh + 1) * 4, :], in_=bsv), reads=[Gb[gs_]], writes=[cb("BSb")])
                    grel(gd, gs_)
                for i in range(NVS):
                    op(POOL, lambda: nc.gpsimd.memset(VS[i][:, :, 128:130], 1.0), writes=[VSb[i]])
                op(POOL, lambda: nc.gpsimd.memset(QT[:], 0.0), writes=[QTb])

                for e_ in (PE, ACT, DVE, POOL, SP):
                    for c_ in CB.values():
                        if c_.w is not None:
                            e_.wait(c_.w)
                for c_ in CB.values():
                    c_.w = None
                    c_.r = {}
                def chk(name):
                    if stop == name:
                        raise _Stop()

                def wload(parts):
                    wi = wring.next()
                    for (off, n, src, sbuf) in parts:
                        dma(SP, WR[wi][:, off:off + n], src, reads=[sbuf], writes=[WRb[wi]])
                    return wi

                def small():
                    i = smring.next()
                    return SM[i], SMb[i]

                def rstd_from_ss(ss_ap, ssb, n, inv_n, np_=128):
                    t1, t1b = small()
                    op(DVE, lambda: nc.vector.tensor_scalar(out=t1[0:np_, 0:n], in0=ss_ap, scalar1=inv_n, scalar2=EPS,
                                                            op0=ALU.mult, op1=ALU.add), reads=[ssb], writes=[t1b])
                    op(ACT, lambda: nc.scalar.activation(out=t1[0:np_, 0:n], in_=t1[0:np_, 0:n], func=AF.Sqrt),
                       reads=[t1b], writes=[t1b])
                    t2, t2b = small()
                    op(DVE, lambda: nc.vector.reciprocal(out=t2[0:np_, 0:n], in_=t1[0:np_, 0:n]), reads=[t1b], writes=[t2b])
                    return t2, t2b

                def run_job(sample, b_idx):
                    if not sample:
                        ntok = T
                        tiles = [(i * 512, 512) for i in range(4)]
                        subs = [(i * 128, 128) for i in range(16)]
                        x_src = x_p[b_idx]
                        k_dst, v_dst, y_dst = k_p[b_idx], v_p[b_idx], y_p[b_idx]
                        nseg, L = 1, 512
                    else:
                        ntok = BL * TS
                        tiles = [(0, 256)]
                        subs = [(i * 64, 64) for i in range(4)]
                        x_src = x_s
                        k_dst, v_dst, y_dst = k_s, v_s, y_s
                        nseg, L = 4, 64

                    if len(heads) < 8:
                        op(POOL, lambda: nc.gpsimd.memset(YB[:], 0.0), writes=[YBb])
                    for s0 in range(0, ntok, 128):
                        xi = xsring.next()
                        dma(SP, XS[xi][:], x_src[s0:s0 + 128, :], writes=[XSb[xi]])
                        ss, ssb = small()
                        op(ACT, lambda: nc.scalar.activation(out=XNB[xi][:], in_=XS[xi][:], func=AF.Square,
                                                             accum_out=ss[:, 0:1]),
                           reads=[XSb[xi]], writes=[XNBb[xi], ssb])
                        r, rb_ = rstd_from_ss(ss[:, 0:1], ssb, 1, 1.0 / D)
                        op(DVE, lambda: nc.vector.scalar_tensor_tensor(out=XNB[xi][:], in0=XS[xi][:], scalar=r[:, 0:1],
                                                                       in1=g_row[:], op0=ALU.mult, op1=ALU.mult),
                           reads=[XSb[xi], rb_, CST], writes=[XNBb[xi]])
                        bk = ringA.next()
                        pbf = PS[bk][:].bitcast(BF16)

                        def tr():
                            ins = None
                            for kc in range(8):
                                ins = nc.tensor.transpose(pbf[:, kc * 128:(kc + 1) * 128],
                                                          XNB[xi][:, kc * 128:(kc + 1) * 128], ident[:])
                            return ins
                        op(PE, tr, reads=[XNBb[xi], CST], writes=[PSb[bk]])
                        op(ACT, lambda: nc.scalar.copy(out=XT[:, :, s0:s0 + 128],
                                                       in_=pbf.rearrange("p (k t) -> p k t", k=8)),
                           reads=[PSb[bk]], writes=[XTb])

                    chk("s0")

                    def proj_fm(wi, woff, tcol, tw, bk):
                        def f():
                            ins = None
                            for kc in range(8):
                                ins = nc.tensor.matmul(PS[bk][:, 0:tw], lhsT=WR[wi][:, woff + kc * 128: woff + (kc + 1) * 128],
                                                       rhs=XT[:, kc, tcol:tcol + tw], start=(kc == 0), stop=(kc == 7))
                            return ins
                        op(PE, f, reads=[WRb[wi], XTb], writes=[PSb[bk]])

                    for h in heads:
                        wi = wload([(0, 1024, WF[16 + h].rearrange("p kc c -> p (kc c)"), wf_b[2]),
                                    (1024, 1024, WF[24 + h].rearrange("p kc c -> p (kc c)"), wf_b[3]),
                                    (2048, 2048, WKV[h].rearrange("p kc c -> p (kc c)"), wkv_b)])
                        for (tc0, tw) in tiles:
                            bk = ringA.next()
                            proj_fm(wi, 0, tc0, tw, bk)
                            gq, gs = galloc(), galloc()
                            op(DVE, lambda: nc.vector.tensor_copy(out=G[gq][:, 0:tw], in_=PS[bk][:, 0:tw]),
                               reads=[PSb[bk]], writes=[Gb[gq]])
                            sqb = G[gs][:].bitcast(BF16)
                            op(ACT, lambda: nc.scalar.activation(out=sqb[:, 0:tw], in_=PS[bk][:, 0:tw], func=AF.Square),
                               reads=[PSb[bk]], writes=[Gb[gs]])
                            bk2 = ringB.next()
                            op(PE, lambda: nc.tensor.matmul(PS[bk2][:, 0:tw], lhsT=bones[:], rhs=sqb[:, 0:tw],
                                                            start=True, stop=True),
                               reads=[Gb[gs], CST], writes=[PSb[bk2]])
                            gr = galloc()
                            op(ACT, lambda: nc.scalar.activation(out=G[gr][:, 0:tw], in_=PS[bk2][:, 0:tw], func=AF.Sqrt,
                                                                 scale=1.0 / 64, bias=EPS),
                               reads=[PSb[bk2]], writes=[Gb[gr]])
                            op(DVE, lambda: nc.vector.reciprocal(out=G[gr][:, 0:tw], in_=G[gr][:, 0:tw]),
                               reads=[Gb[gr]], writes=[Gb[gr]])
                            for m_ in range(2):
                                ps_ = slice(64 * m_, 64 * m_ + 64)
                                op(DVE, lambda: nc.vector.scalar_tensor_tensor(out=QT[ps_, m_, tc0:tc0 + tw],
                                                                               in0=G[gq][ps_, 0:tw],
                                                                               scalar=gq2[ps_, 0:1], in1=G[gr][ps_, 0:tw],
                                                                               op0=ALU.mult, op1=ALU.mult),
                                   reads=[Gb[gq], Gb[gr], CST], writes=[QTb])
                            grel(gq, gs, gr)
                            bk = ringA.next()
                            proj_fm(wi, 1024, tc0, tw, bk)
                            op(ACT, lambda: nc.scalar.activation(out=ZB[:, tc0:tc0 + tw], in_=PS[bk][:, 0:tw], func=AF.Silu),
                               reads=[PSb[bk]], writes=[ZBb])
                        chk("s1")
                        if not sample:
                            s2_prompt_kv(h, wi, subs, k_dst, v_dst)
                            chk("kv")
                            s2_prompt_attn(h)
                            chk("attn")
                        else:
                            s2_sample(h, wi, k_dst, v_dst)

                    for ti, (tc0, tw) in enumerate(tiles):
                        chk("heads")
                        s3(sample, ti, tc0, tw, nseg, L, len(tiles))
                        chk("s3")
                        s4(tc0, tw)
                        chk("s4")
                        s5(tc0, tw, x_src, y_dst)
                        chk("s5")
                    cdst = conv_s if sample else conv_p[b_idx:b_idx + 1]
                    ldst = lru_s if sample else lru_p[b_idx:b_idx + 1]
                    for sg in range(nseg):
                        for j in range(3):
                            dma(SP, cdst[sg, j].rearrange("(c p) -> p c", p=128), CONVST[:, sg, :, j],
                                reads=[CONVSTb], track=CONVSTb, is_output=True)
                    for sg in range(nseg):
                        dma(SP, ldst[sg].rearrange("(c p) -> p c", p=128), HC[:, :, sg],
                            reads=[HCb], track=HCb, is_output=True)

                def kv_subtile(h, wi, s0, sn, vsi, kt_idx, k_dst, v_dst, ktcol, pbf, pcol, trb):
                    bk = ringA.next()

                    def f():
                        ins = None
                        for kc in range(8):
                            ins = nc.tensor.matmul(PS[bk][0:sn, 0:256], lhsT=XT[:, kc, s0:s0 + sn],
                                                   rhs=WR[wi][:, 2048 + kc * 256: 2048 + (kc + 1) * 256],
                                                   start=(kc == 0), stop=(kc == 7))
                        return ins
                    op(PE, f, reads=[WRb[wi], XTb], writes=[PSb[bk]])
                    gsq, gkv = galloc(), galloc()
                    op(ACT, lambda: nc.scalar.activation(out=G[gsq][0:sn, 0:128], in_=PS[bk][0:sn, 0:128], func=AF.Square),
                       reads=[PSb[bk]], writes=[Gb[gsq]])
                    ss, ssb = small()
                    op(DVE, lambda: nc.vector.tensor_reduce(out=ss[0:sn, 0:2],
                                                            in_=G[gsq][0:sn, 0:128].rearrange("p (g d) -> p g d", g=2),
                                                            axis=AX.X, op=ALU.add),
                       reads=[Gb[gsq]], writes=[ssb])
                    rk, rkb = rstd_from_ss(ss[0:sn, 0:2], ssb, 2, 1.0 / 64, np_=sn)
                    op(DVE, lambda: nc.vector.tensor_tensor(out=G[gkv][0:sn, 0:128].rearrange("p (g d) -> p g d", g=2),
                                                            in0=PS[bk][0:sn, 0:128].rearrange("p (g d) -> p g d", g=2),
                                                            in1=rk[0:sn, 0:2].unsqueeze(2).to_broadcast([sn, 2, 64]),
                                                            op=ALU.mult),
                       reads=[PSb[bk], rkb], writes=[Gb[gkv]])
                    op(DVE, lambda: nc.vector.tensor_tensor(out=G[gkv][0:sn, 0:128], in0=G[gkv][0:sn, 0:128],
                                                            in1=gk_rep[0:sn, :], op=ALU.mult),
                       reads=[Gb[gkv], CST], writes=[Gb[gkv]])
                    op(ACT, lambda: nc.scalar.copy(out=G[gkv][0:sn, 128:256], in_=PS[bk][0:sn, 128:256]),
                       reads=[PSb[bk]], writes=[Gb[gkv]])
                    dma(SP, k_dst[s0:s0 + sn, h * 128:(h + 1) * 128], G[gkv][0:sn, 0:128], reads=[Gb[gkv]],
                        track=Gb[gkv], is_output=True)
                    dma(SP, v_dst[s0:s0 + sn, h * 128:(h + 1) * 128], G[gkv][0:sn, 128:256], reads=[Gb[gkv]],
                        track=Gb[gkv], is_output=True)
                    knb = G[gsq][:].bitcast(BF16)
                    op(POOL, lambda: nc.gpsimd.tensor_copy(out=knb[0:sn, 0:128], in_=G[gkv][0:sn, 0:128]),
                       reads=[Gb[gkv]], writes=[Gb[gsq]])
                    op(POOL, lambda: nc.gpsimd.tensor_copy(out=VS[vsi][0:sn, kt_idx, 0:128], in_=G[gkv][0:sn, 128:256]),
                       reads=[Gb[gkv]], writes=[VSb[vsi]])
                    op(PE, lambda: nc.tensor.transpose(pbf[:, pcol:pcol + sn], knb[0:sn, 0:128], ident[0:sn, 0:sn]),
                       reads=[Gb[gsq], CST], writes=[trb])
                    grel(gsq, gkv)

                def s2_prompt_kv(h, wi, subs, k_dst, v_dst):
                    for g4 in range(0, 16, 4):
                        bkt = ringB.next()
                        pbf = PS[bkt][:].bitcast(BF16)
                        for j in range(4):
                            s0, sn = subs[g4 + j]
                            kv_subtile(h, wi, s0, sn, 0, g4 + j, k_dst, v_dst, None, pbf, j * 128, PSb[bkt])
                        op(DVE, lambda: nc.vector.tensor_copy(out=KT[:, g4 * 128:g4 * 128 + 512], in_=pbf[:, 0:512]),
                           reads=[PSb[bkt]], writes=[KTb])

                def attn_post(h, obank, nq, ycol):
                    ov = PS[obank][0:nq, 0:258].rearrange("p (m d) -> p m d", m=2)
                    rz, rzb = small()
                    op(DVE, lambda: nc.vector.reciprocal(out=rz[0:nq, 0:2], in_=ov[:, :, 128]),
                       reads=[PSb[obank]], writes=[rzb])
                    op(DVE, lambda: nc.vector.tensor_tensor(out=rz[0:nq, 2:3], in0=rz[0:nq, 1:2], in1=lamc[0:nq, 1:2],
                                                            op=ALU.mult), reads=[rzb, CST], writes=[rzb])
                    go = galloc()
                    op(DVE, lambda: nc.vector.tensor_scalar(out=G[go][0:nq, 0:128], in0=ov[:, 0, 0:128],
                                                            scalar1=rz[0:nq, 0:1], scalar2=None, op0=ALU.mult),
                       reads=[PSb[obank], rzb], writes=[Gb[go]])
                    op(DVE, lambda: nc.vector.scalar_tensor_tensor(out=G[go][0:nq, 0:128], in0=ov[:, 1, 0:128],
                                                                   scalar=rz[0:nq, 2:3], in1=G[go][0:nq, 0:128],
                                                                   op0=ALU.mult, op1=ALU.add),
                       reads=[PSb[obank], rzb, Gb[go]], writes=[Gb[go]])
                    ss, ssb = small()
                    op(ACT, lambda: nc.scalar.activation(out=G[go][0:nq, 128:256], in_=G[go][0:nq, 0:128], func=AF.Square,
                                                         accum_out=ss[0:nq, 0:1]),
                       reads=[Gb[go]], writes=[Gb[go], ssb])
                    rs, rsb = rstd_from_ss(ss[0:nq, 0:1], ssb, 1, 1.0 / 128, np_=nq)
                    onb = G[go][:].bitcast(BF16)
                    op(ACT, lambda: nc.scalar.activation(out=onb[0:nq, 512:640], in_=G[go][0:nq, 0:128], func=AF.Copy,
                                                         scale=rs[0:nq, 0:1]),
                       reads=[Gb[go], rsb], writes=[Gb[go]])
                    bk = ringA.next()
                    pbf = PS[bk][:].bitcast(BF16)
                    op(PE, lambda: nc.tensor.transpose(pbf[:, 0:nq], onb[0:nq, 512:640], ident[0:nq, 0:nq]),
                       reads=[Gb[go], CST], writes=[PSb[bk]])
                    op(DVE, lambda: nc.vector.scalar_tensor_tensor(out=YB[:, h, ycol:ycol + nq], in0=pbf[:, 0:nq],
                                                                   scalar=sg8[:, 0:1], in1=ZB[:, ycol:ycol + nq],
                                                                   op0=ALU.mult, op1=ALU.mult),
                       reads=[PSb[bk], ZBb, CST], writes=[YBb])
                    grel(go)

                def s2_prompt_attn(h):
                    for t in range(8):
                        ob = [ringB.next(), ringB.next()]
                        nkt = 2 * t + 2
                        for kt in range(nkt):
                            c0 = 0 if kt <= 2 * t else 128
                            bk = ringA.next()
                            sv = PS[bk][:].rearrange("p (m q) -> p m q", m=2)

                            def sc():
                                blocks = [ql for ql in range(2) if 0 <= (2 * t + ql) - kt <= 1]
                                ins = nc.tensor.matmul(sv[:, :, c0:256], lhsT=KT[:, kt * 128:(kt + 1) * 128],
                                                       rhs=QT[:, :, t * 256 + c0:t * 256 + 256],
                                                       start=True, stop=(len(blocks) == 0), skip_group_check=True)
                                for bi, ql in enumerate(blocks):
                                    bt = BDb if (2 * t + ql) == kt else BSb
                                    for m in range(2):
                                        ins = nc.tensor.matmul(sv[:, m, ql * 128:(ql + 1) * 128], lhsT=ident[:],
                                                               rhs=bt[:, h, :], start=False,
                                                               stop=(bi == len(blocks) - 1 and m == 1),
                                                               skip_group_check=True)
                                return ins
                            op(PE, sc, reads=[KTb, QTb, CST], writes=[PSb[bk]])
                            ei = ering.next()
                            ev = ET[ei][:].rearrange("p (m q) -> p m q", m=2)
                            op(ACT, lambda: nc.scalar.activation(out=ev[:, :, c0:256], in_=sv[:, :, c0:256], func=AF.Exp,
                                                                 scale=0.125, bias=c15[:, h:h + 1]),
                               reads=[PSb[bk], CST], writes=[ETb[ei]])

                            def pv():
                                ins = None
                                for ql in range(2):
                                    if kt > 2 * t + ql:
                                        continue
                                    last = (kt == 2 * t + ql)
                                    for m in range(2):
                                        ins = nc.tensor.matmul(PS[ob[ql]][:, m * 129:(m + 1) * 129],
                                                               lhsT=ev[:, m, ql * 128:(ql + 1) * 128],
                                                               rhs=VS[0][:, kt, 0:129], start=(kt == 0 and m == 0), stop=last,
                                                               skip_group_check=True)
                                return ins
                            op(PE, pv, reads=[ETb[ei], VSb[0]], writes=[PSb[ob[0]], PSb[ob[1]]])
                        for ql in range(2):
                            attn_post(h, ob[ql], 128, t * 256 + ql * 128)

                def s2_sample(h, wi, k_dst, v_dst):
                    for b in range(BL):
                        u = h * BL + b
                        vsi = u % NVS
                        kci = u % NVS
                        dma(POOL, KC[kci][:], ck[b, :, h * 128:(h + 1) * 128].rearrange("(kt p) c -> p kt c", p=128),
                            writes=[KCb[kci]])
                        dma(POOL, VS[vsi][:, 0:32, 0:128],
                            cv[b, :, h * 128:(h + 1) * 128].rearrange("(kt p) c -> p kt c", p=128), writes=[VSb[vsi]])
                        for g8 in range(0, 32, 8):
                            bkt = ringB.next()
                            pbf = PS[bkt][:].bitcast(BF16)

                            def tr():
                                ins = None
                                for j in range(8):
                                    ins = nc.tensor.transpose(pbf[:, j * 128:(j + 1) * 128], KC[kci][:, g8 + j, :], ident[:])
                                return ins
                            op(PE, tr, reads=[KCb[kci], CST], writes=[PSb[bkt]])
                            op(DVE, lambda: nc.vector.tensor_copy(out=KT[:, g8 * 128:(g8 + 8) * 128], in_=pbf[:, 0:1024]),
                               reads=[PSb[bkt]], writes=[KTb])
                        bkt = ringB.next()
                        pbf = PS[bkt][:].bitcast(BF16)
                        kv_subtile(h, wi, b * TS, TS, vsi, 32, k_dst, v_dst, None, pbf, 0, PSb[bkt])
                        op(DVE, lambda: nc.vector.tensor_copy(out=KT[:, PAST:PAST + TS], in_=pbf[:, 0:TS]),
                           reads=[PSb[bkt]], writes=[KTb])
                        obank = ringB.next()
                        for g4 in range(0, 33, 4):
                            kts = list(range(g4, min(g4 + 4, 33)))
                            bk = ringA.next()
                            sv = PS[bk][:].rearrange("p (j m q) -> p j m q", j=4, m=2)

                            def sc():
                                ins = None
                                for j, kt in enumerate(kts):
                                    nk = 128 if kt < 32 else TS
                                    nb = kt >= 31
                                    ins = nc.tensor.matmul(sv[0:nk, j, :, :], lhsT=KT[:, kt * 128:kt * 128 + nk],
                                                           rhs=QT[:, :, b * TS:(b + 1) * TS],
                                                           start=True, stop=not nb, skip_group_check=True)
                                    for m in range(2):
                                        if kt == 31:
                                            ins = nc.tensor.matmul(sv[:, j, m, :], lhsT=ident[:], rhs=BSb[:, h, 0:TS],
                                                                   start=False, stop=(m == 1), skip_group_check=True)
                                        if kt == 32:
                                            ins = nc.tensor.matmul(sv[0:TS, j, m, :], lhsT=ident[0:TS, 0:TS],
                                                                   rhs=BDb[0:TS, h, 0:TS], start=False, stop=(m == 1),
                                                                   skip_group_check=True)
                                return ins
                            op(PE, sc, reads=[KTb, QTb, CST], writes=[PSb[bk]])
                            ei = ering.next()
                            ev = ET[ei][:].rearrange("p (j m q) -> p j m q", j=4, m=2)
                            nfull = len([kt for kt in kts if kt < 32])
                            if nfull:
                                op(ACT, lambda: nc.scalar.activation(out=ev[:, 0:nfull], in_=sv[:, 0:nfull], func=AF.Exp,
                                                                     scale=0.125, bias=c15[:, h:h + 1]),
                                   reads=[PSb[bk], CST], writes=[ETb[ei]])
                            if kts[-1] == 32:
                                j = len(kts) - 1
                                op(ACT, lambda: nc.scalar.activation(out=ev[0:TS, j], in_=sv[0:TS, j], func=AF.Exp,
                                                                     scale=0.125, bias=c15[0:TS, h:h + 1]),
                                   reads=[PSb[bk], CST], writes=[ETb[ei]])

                            def pv():
                                ins = None
                                for j, kt in enumerate(kts):
                                    nk = 128 if kt < 32 else TS
                                    for m in range(2):
                                        ins = nc.tensor.matmul(PS[obank][0:TS, m * 129:(m + 1) * 129],
                                                               lhsT=ev[0:nk, j, m, :], rhs=VS[vsi][0:nk, kt, 0:129],
                                                               start=(kt == 0 and m == 0), stop=(kt == 32), skip_group_check=True)
                                return ins
                            op(PE, pv, reads=[ETb[ei], VSb[vsi]], writes=[PSb[obank]])
                        attn_post(h, obank, TS, b * TS)

                def s3(sample, ti, tc0, tw, nseg, L, ntiles):
                    for c in range(8):
                        wi = wload([(0, 1024, WF[c].rearrange("p kc c -> p (kc c)"), wf_b[0]),
                                    (1024, 1024, WF[8 + c].rearrange("p kc c -> p (kc c)"), wf_b[1])])
                        xav = XAw[:, 0:nseg * (L + 3)].rearrange("p (s l) -> p s l", s=nseg)
                        if sample:
                            op(DVE, lambda: nc.vector.tensor_copy(out=xav[:, :, 0:3], in_=SCV[:, :, c, :]),
                               reads=[CST], writes=[XAwb])
                        elif ti == 0:
                            op(DVE, lambda: nc.vector.memset(xav[:, :, 0:3], 0.0), writes=[XAwb])
                        else:
                            op(DVE, lambda: nc.vector.tensor_copy(out=xav[:, :, 0:3], in_=HALO[:, c, 0:nseg, :]),
                               reads=[HALOb], writes=[XAwb])
                        bk = ringAll.next()
                        proj_fm2(wi, 0, tc0, tw, bk)
                        op(ACT, lambda: nc.scalar.copy(out=xav[:, :, 3:3 + L],
                                                       in_=PS[bk][:, 0:tw].rearrange("p (s l) -> p s l", s=nseg)),
                           reads=[PSb[bk]], writes=[XAwb])
                        op(POOL, lambda: nc.gpsimd.tensor_copy(out=HALO[:, c, 0:nseg, :], in_=xav[:, :, L:L + 3]),
                           reads=[XAwb], writes=[HALOb])
                        if sample or ti == ntiles - 1:
                            op(POOL, lambda: nc.gpsimd.tensor_copy(out=CONVST[:, 0:nseg, c, :], in_=xav[:, :, L:L + 3]),
                               reads=[XAwb], writes=[CONVSTb])
                        gx = galloc()
                        xcv = G[gx][:, 0:tw].rearrange("p (s l) -> p s l", s=nseg)
                        op(DVE, lambda: nc.vector.tensor_scalar(out=xcv, in0=xav[:, :, 0:L], scalar1=cw[:, c, 0:1],
                                                                scalar2=cbias[:, c:c + 1], op0=ALU.mult, op1=ALU.add),
                           reads=[XAwb, CST], writes=[Gb[gx]])
                        for j in range(1, 4):
                            op(DVE, lambda: nc.vector.scalar_tensor_tensor(out=xcv, in0=xav[:, :, j:j + L],
                                                                           scalar=cw[:, c, j:j + 1], in1=xcv,
                                                                           op0=ALU.mult, op1=ALU.add),
                               reads=[XAwb, Gb[gx], CST], writes=[Gb[gx]])
                        gxb = galloc()
                        xcb = G[gxb][:].bitcast(BF16)
                        op(ACT, lambda: nc.scalar.copy(out=xcb[:, 0:tw], in_=G[gx][:, 0:tw]), reads=[Gb[gx]], writes=[Gb[gxb]])
                        bkr, bki = ringAll.next(), ringAll.next()
                        op(PE, lambda: nc.tensor.matmul(PS[bkr][:, 0:tw], lhsT=WRG[:, c, :], rhs=xcb[:, 0:tw],
                                                        start=True, stop=True), reads=[Gb[gxb], CST], writes=[PSb[bkr]])
                        op(PE, lambda: nc.tensor.matmul(PS[bki][:, 0:tw], lhsT=WIG[:, c, :], rhs=xcb[:, 0:tw],
                                                        start=True, stop=True), reads=[Gb[gxb], CST], writes=[PSb[bki]])
                        gr_, gi_, ga_ = galloc(), galloc(), galloc()
                        op(ACT, lambda: nc.scalar.activation(out=G[gr_][:, 0:tw], in_=PS[bkr][:, 0:tw], func=AF.Sigmoid,
                                                             bias=brg[:, c:c + 1]), reads=[PSb[bkr], CST], writes=[Gb[gr_]])
                        op(ACT, lambda: nc.scalar.activation(out=G[gi_][:, 0:tw], in_=PS[bki][:, 0:tw], func=AF.Sigmoid,
                                                             bias=big[:, c:c + 1]), reads=[PSb[bki], CST], writes=[Gb[gi_]])
                        op(ACT, lambda: nc.scalar.activation(out=G[ga_][:, 0:tw], in_=G[gr_][:, 0:tw], func=AF.Exp,
                                                             scale=nsp[:, c:c + 1]), reads=[Gb[gr_], CST], writes=[Gb[ga_]])
                        op(DVE, lambda: nc.vector.tensor_tensor(out=G[gr_][:, 0:tw], in0=G[ga_][:, 0:tw], in1=G[ga_][:, 0:tw],
                                                                op=ALU.mult), reads=[Gb[ga_]], writes=[Gb[gr_]])
                        op(ACT, lambda: nc.scalar.activation(out=G[gr_][:, 0:tw], in_=G[gr_][:, 0:tw], func=AF.Sqrt,
                                                             scale=-1.0, bias=1.0), reads=[Gb[gr_]], writes=[Gb[gr_]])
                        if (not sample) and ti == 0:
                            op(DVE, lambda: nc.vector.memset(G[gr_][:, 0:1], 1.0), writes=[Gb[gr_]])
                        op(DVE, lambda: nc.vector.tensor_tensor(out=G[gi_][:, 0:tw], in0=G[gi_][:, 0:tw], in1=G[gx][:, 0:tw],
                                                                op=ALU.mult), reads=[Gb[gi_], Gb[gx]], writes=[Gb[gi_]])
                        op(DVE, lambda: nc.vector.tensor_tensor(out=G[gi_][:, 0:tw], in0=G[gi_][:, 0:tw], in1=G[gr_][:, 0:tw],
                                                                op=ALU.mult), reads=[Gb[gi_], Gb[gr_]], writes=[Gb[gi_]])
                        for sg in range(nseg):
                            if sample:
                                init = LR0[:, sg, c:c + 1]
                                rdi = [CST]
                            elif ti == 0:
                                init = 0.0
                                rdi = []
                            else:
                                init = HC[:, c, sg:sg + 1]
                                rdi = [HCb]
                            op(DVE, lambda: nc.vector.tensor_tensor_scan(out=G[gx][:, sg * L:(sg + 1) * L],
                                                                         data0=G[ga_][:, sg * L:(sg + 1) * L],
                                                                         data1=G[gi_][:, sg * L:(sg + 1) * L],
                                                                         initial=init, op0=ALU.mult, op1=ALU.add),
                               reads=[Gb[ga_], Gb[gi_]] + rdi, writes=[Gb[gx]])
                        hv = G[gx][:, 0:tw].rearrange("p (s l) -> p s l", s=nseg)
                        op(POOL, lambda: nc.gpsimd.tensor_copy(out=HC[:, c, 0:nseg], in_=hv[:, :, L - 1]),
                           reads=[Gb[gx]], writes=[HCb])
                        bkz = ringAll.next()
                        proj_fm2(wi, 1024, tc0, tw, bkz)
                        op(ACT, lambda: nc.scalar.activation(out=G[ga_][:, 0:tw], in_=PS[bkz][:, 0:tw], func=AF.Silu),
                           reads=[PSb[bkz]], writes=[Gb[ga_]])
                        op(DVE, lambda: nc.vector.tensor_tensor(out=YA[:, c, 0:tw], in0=G[gx][:, 0:tw], in1=G[ga_][:, 0:tw],
                                                                op=ALU.mult), reads=[Gb[gx], Gb[ga_]], writes=[YAb])
                        grel(gx, gxb, gr_, gi_, ga_)

                def proj_fm2(wi, woff, tcol, tw, bk):
                    def f():
                        ins = None
                        for kc in range(8):
                            ins = nc.tensor.matmul(PS[bk][:, 0:tw], lhsT=WR[wi][:, woff + kc * 128: woff + (kc + 1) * 128],
                                                   rhs=XT[:, kc, tcol:tcol + tw], start=(kc == 0), stop=(kc == 7))
                        return ins
                    op(PE, f, reads=[WRb[wi], XTb], writes=[PSb[bk]])

                def s4(tc0, tw):
                    for j in range(8):
                        wi = wload([(0, 1024, WP[0, j].rearrange("p c col -> p (c col)"), wp_b),
                                    (1024, 1024, WP[1, j].rearrange("p c col -> p (c col)"), wp_b),
                                    (2048, 1024, WF[32 + j].rearrange("p kc c -> p (kc c)"), wf_b[4]),
                                    (3072, 1024, WF[40 + j].rearrange("p kc c -> p (kc c)"), wf_b[5])])
                        bpa, bpb, bga, bgb = ringAll.next(), ringAll.next(), ringAll.next(), ringAll.next()

                        def mk(bk, woff, src, srcb):
                            def f():
                                ins = None
                                for cc in range(8):
                                    rhs = src[:, cc, 0:tw] if src is YA else src[:, cc, tc0:tc0 + tw]
                                    ins = nc.tensor.matmul(PS[bk][:, 0:tw],
                                                           lhsT=WR[wi][:, woff + cc * 128: woff + (cc + 1) * 128],
                                                           rhs=rhs, start=(cc == 0), stop=(cc == 7))
                                return ins
                            op(PE, f, reads=[WRb[wi], srcb], writes=[PSb[bk]])
                        mk(bga, 2048, XT, XTb)
                        mk(bgb, 3072, XT, XTb)
                        mk(bpa, 0, YA, YAb)
                        mk(bpb, 1024, YB, YBb)
                        g1, g2 = galloc(), galloc()
                        op(ACT, lambda: nc.scalar.activation(out=G[g1][:, 0:tw], in_=PS[bga][:, 0:tw], func=AF.Sigmoid),
                           reads=[PSb[bga]], writes=[Gb[g1]])
                        op(ACT, lambda: nc.scalar.activation(out=G[g2][:, 0:tw], in_=PS[bgb][:, 0:tw], func=AF.Sigmoid),
                           reads=[PSb[bgb]], writes=[Gb[g2]])
                        op(DVE, lambda: nc.vector.tensor_tensor(out=G[g1][:, 0:tw], in0=PS[bpa][:, 0:tw], in1=G[g1][:, 0:tw],
                                                                op=ALU.mult), reads=[PSb[bpa], Gb[g1]], writes=[Gb[g1]])
                        op(DVE, lambda: nc.vector.tensor_tensor(out=G[g2][:, 0:tw], in0=PS[bpb][:, 0:tw], in1=G[g2][:, 0:tw],
                                                                op=ALU.mult), reads=[PSb[bpb], Gb[g2]], writes=[Gb[g2]])
                        op(POOL, lambda: nc.gpsimd.tensor_tensor(out=MM[:, j, 0:tw], in0=G[g1][:, 0:tw], in1=G[g2][:, 0:tw],
                                                                 op=ALU.add), reads=[Gb[g1], Gb[g2]], writes=[MMb])
                        grel(g1, g2)

                def s5(tc0, tw, x_src, y_dst):
                    wis = [wload([(0, 4096, WO[n].rearrange("p j col -> p (j col)"), wo_b)]) for n in range(2)]
                    for s0 in range(0, tw, 128):
                        for n in range(2):
                            wi = wis[n]
                            bk = ringAll.next()

                            def f():
                                ins = None
                                for j in range(8):
                                    ins = nc.tensor.matmul(PS[bk][:, 0:512], lhsT=MM[:, j, s0:s0 + 128],
                                                           rhs=WR[wi][:, j * 512:(j + 1) * 512],
                                                           start=(j == 0), stop=(j == 7))
                                return ins
                            op(PE, f, reads=[WRb[wi], MMb], writes=[PSb[bk]])
                            gx_ = galloc()
                            r0 = tc0 + s0
                            dma(SP, G[gx_][:, :], x_src[r0:r0 + 128, n * 512:(n + 1) * 512], writes=[Gb[gx_]])
                            op(DVE, lambda: nc.vector.tensor_tensor(out=G[gx_][:, :], in0=PS[bk][:, 0:512], in1=G[gx_][:, :],
                                                                    op=ALU.add), reads=[PSb[bk], Gb[gx_]], writes=[Gb[gx_]])
                            dma(SP, y_dst[r0:r0 + 128, n * 512:(n + 1) * 512], G[gx_][:, :], reads=[Gb[gx_]],
                                track=Gb[gx_], is_output=True)
                            grel(gx_)

                try:
                    chk("pro")
                    for b in jobs:
                        run_job(False, b)
                        kb.new_epoch()
                    if do_sample:
                        run_job(True, None)
                except _Stop:
                    pass
                last = {}
                for ev in kb.out_events:
                    if ev.key not in last or last[ev.key].val < ev.val:
                        last[ev.key] = ev
                for ev in last.values():
                    SP.wait(ev)

            body()

    return nc, kb.used


_CONSTS = None


def _consts():
    global _CONSTS
    if _CONSTS is None:
        bo = np.zeros((128, 128), np.float32)
        bo[0:64, 0:64] = 1.0
        bo[64:128, 64:128] = 1.0
        _CONSTS = {"c_onehot": _bucket_table(), "c_ident": np.eye(128, dtype=np.float32), "c_blockones": bo}
    return _CONSTS


def kernel(**inputs):
    f = lambda a: np.ascontiguousarray(np.asarray(a, dtype=np.float32))
    x_prompt = f(inputs["x_prompt"])
    x_sample = f(inputs["x_sample"])
    cache_k = f(inputs["cache_k"])[0].reshape(32, PAST, 1024)
    cache_v = f(inputs["cache_v"])[0].reshape(32, PAST, 1024)
    state_conv = f(inputs["state_conv"])[0]
    state_lru = f(inputs["state_lru"])[0]
    shared = {
        "norm_gain": f(inputs["norm_gain"]), "w_in": f(inputs["w_in"])[0], "conv_w": f(inputs["conv_w"])[0],
        "conv_b": f(inputs["conv_b"]), "w_rg": f(inputs["w_rg"])[0], "b_rg": f(inputs["b_rg"]),
        "w_ig": f(inputs["w_ig"])[0], "b_ig": f(inputs["b_ig"]), "lru_lambda": f(inputs["lru_lambda"]),
        "q_norm_gain": f(inputs["q_norm_gain"]), "k_norm_gain": f(inputs["k_norm_gain"]),
        "rel_bias": f(inputs["rel_bias"]), "lam_q1": f(inputs["lam_q1"]), "lam_k1": f(inputs["lam_k1"]),
        "lam_q2": f(inputs["lam_q2"]), "lam_k2": f(inputs["lam_k2"]),
        "subln_gain": f(inputs["subln_gain"]).reshape(128, 1),
        "w_proj_a": f(inputs["w_proj_a"])[0], "w_proj_b": f(inputs["w_proj_b"])[0], "w_out": f(inputs["w_out"])[0],
    }
    shared.update(_consts())
    in_maps = []
    for c in range(NCORES):
        sl = slice(c * BL, (c + 1) * BL)
        m = dict(shared)
        m["x_prompt"] = x_prompt[sl]
        m["x_sample"] = x_sample[sl].reshape(BL * TS, D)
        m["cache_k"] = cache_k[sl]
        m["cache_v"] = cache_v[sl]
        m["state_conv"] = state_conv[sl]
        m["state_lru"] = state_lru[sl]
        in_maps.append(m)
    nc = build_program()
    res = run_bass_kernel_spmd(nc, in_maps, core_ids=list(range(NCORES)))
    R = res.results
    cat = lambda name: np.concatenate([np.asarray(r[name], dtype=np.float32) for r in R], axis=0)
    y_prompt = cat("y_prompt")
    y_sample = cat("y_sample").reshape(32, TS, D)
    k_prompt = cat("k_prompt").reshape(1, 32, T, 8, 128)
    v_prompt = cat("v_prompt").reshape(1, 32, T, 8, 128)
    conv_prompt = cat("conv_prompt").reshape(1, 32, 3, 1024)
    lru_prompt = cat("lru_prompt").reshape(1, 32, 1024)
    k_sample = cat("k_sample").reshape(1, 32, TS, 8, 128)
    v_sample = cat("v_sample").reshape(1, 32, TS, 8, 128)
    conv_sample = cat("conv_sample").reshape(1, 32, 3, 1024)
    lru_sample = cat("lru_sample").reshape(1, 32, 1024)
    return (y_prompt, y_sample, k_prompt, v_prompt, conv_prompt, lru_prompt, k_sample, v_sample, conv_sample, lru_sample)
```

```python
import numpy as np
from collections import deque
from contextlib import ExitStack
import concourse.bass as bass
import concourse.mybir as mybir
from concourse.bass_utils import run_bass_kernel_spmd

F32 = mybir.dt.float32
BF16 = mybir.dt.bfloat16
AF = mybir.ActivationFunctionType
ALU = mybir.AluOpType
AX = mybir.AxisListType

NCORES = 8
D = 1024
T = 2048
BL = 4
TS = 64
PAST = 4096
EPS = 1e-6
LAM_INIT = 0.2
NEG = -30000.0


class Ev:
    __slots__ = ("sem", "val", "key", "opid")

    def __init__(self, sem, val, key, opid=None):
        self.sem, self.val, self.key, self.opid = sem, val, key, opid


class Buf:
    def __init__(self, name):
        self.name = name
        self.w = None
        self.r = {}
        self.dsem = None
        self.psum = name.startswith("PS")


class Eng:
    def __init__(self, name, h, is_pe=False):
        self.name, self.h, self.is_pe = name, h, is_pe
        self.sem = None
        self.key = None
        self.cnt = 0
        self.seq = 0
        self.seen = {}
        self.used = None

    def wait(self, ev):
        if self.seen.get(ev.key, 0) >= ev.val:
            return
        self.h.wait_ge(ev.sem, ev.val)
        self.seen[ev.key] = ev.val
        if ev.opid is not None and self.used is not None:
            self.used.add(ev.opid)


class K:
    def __init__(self, nc, stack, needed=None):
        self.nc = nc
        self.stack = stack
        self.needed = needed
        self.used = set()
        self.pe = Eng("pe", nc.tensor, True)
        self.act = Eng("act", nc.scalar)
        self.dve = Eng("dve", nc.vector)
        self.pool = Eng("pool", nc.gpsimd)
        self.sp = Eng("sp", nc.sync)
        for e in (self.pe, self.act, self.dve, self.pool, self.sp):
            e.used = self.used
        self.nsem = 0
        self.out_events = []
        self.new_epoch()

    def sem(self, name):
        self.nsem += 1
        return self.stack.enter_context(self.nc.semaphore(f"{name}_{self.nsem}"))

    def new_epoch(self):
        for e in (self.pe, self.act, self.dve, self.pool):
            e.sem = self.sem("e" + e.name)
            e.key = id(e.sem)
            e.cnt = 0

    def _deps(self, eng, reads, writes, own_key):
        for b in reads:
            if b.w is not None:
                ev = b.w
                if ev.key == own_key and eng.is_pe:
                    continue
                if ev.key == own_key and own_key != eng.key:
                    continue
                eng.wait(ev)
            if b.psum:
                for ev in list(b.r.values()):
                    if ev.key != own_key:
                        eng.wait(ev)
        for b in writes:
            evs = list(b.r.values())
            if b.w is not None:
                evs.append(b.w)
            for ev in evs:
                if ev.key == own_key:
                    continue
                eng.wait(ev)

    def op(self, eng, fn, reads=(), writes=()):
        self._deps(eng, reads, writes, eng.key)
        ins = fn()
        eng.seq += 1
        opid = (eng.name, eng.seq)
        if self.needed is None or opid in self.needed:
            eng.cnt += 1
            ins.then_inc(eng.sem, 1)
        ev = Ev(eng.sem, eng.cnt, eng.key, opid)
        for b in writes:
            b.w = ev
            b.r = {}
        for b in reads:
            b.r[ev.key] = ev
        return ev

    def dma(self, q, out, in_, reads=(), writes=(), track=None, is_output=False):
        tb = track if track is not None else (writes[0] if writes else reads[0])
        if tb.dsem is None:
            tb.dsem = {}
        if q.name not in tb.dsem:
            s = self.sem("d" + tb.name + q.name)
            tb.dsem[q.name] = [s, id(s), 0]
        ds = tb.dsem[q.name]
        s, key = ds[0], ds[1]
        self._deps(q, reads, writes, key)
        ds[2] += 1
        q.h.dma_start(out=out, in_=in_).then_inc(s, 16)
        ev = Ev(s, 16 * ds[2], key)
        for b in writes:
            b.w = ev
            b.r = {}
        for b in reads:
            b.r[ev.key] = ev
        if is_output:
            self.out_events.append(ev)
        return ev


class Ring:
    def __init__(self, items):
        self.items = items
        self.i = 0

    def next(self):
        it = self.items[self.i % len(self.items)]
        self.i += 1
        return it


def _bucket_table():
    s = np.arange(384)
    rel = (127 - s).astype(np.int64)
    nb, max_exact = 16, 8
    n = np.abs(rel)
    nf = np.maximum(n, 1).astype(np.float32)
    large = max_exact + (np.log(nf / np.float32(max_exact)) / np.float32(np.log(128 / max_exact))
                         * np.float32(nb - max_exact)).astype(np.int32)
    large = np.minimum(large, nb - 1)
    b = np.where(rel > 0, nb, 0) + np.where(n < max_exact, n, large)
    oh = np.zeros((32, 384), np.float32)
    oh[b, s] = 1.0
    return oh


class _Stop(Exception):
    pass


def build_program(jobs=(0, 1, 2, 3), do_sample=True, heads=tuple(range(8)), stop=None, mini=False):
    _, used = _build(jobs, do_sample, heads, stop, None, mini)
    nc, _ = _build(jobs, do_sample, heads, stop, used, mini)
    return nc


def _build(jobs, do_sample, heads, stop, needed, mini=False):
    BLx = 1 if mini else BL
    PASTx = 128 if mini else PAST
    nc = bass.Bass("TRN2", target_bir_lowering=False)
    din = {}

    def inp(name, shape):
        din[name] = nc.dram_tensor(name, list(shape), F32, kind="ExternalInput").ap()
        return din[name]

    def outp(name, shape):
        return nc.dram_tensor(name, list(shape), F32, kind="ExternalOutput").ap()

    x_p = inp("x_prompt", (BLx, T, D))
    x_s = inp("x_sample", (BL * TS, D))
    ck = inp("cache_k", (BLx, PASTx, 1024))
    cv = inp("cache_v", (BLx, PASTx, 1024))
    st_conv = inp("state_conv", (BL, 3, 1024))
    st_lru = inp("state_lru", (BL, 1024))
    norm_gain = inp("norm_gain", (1, 1024))
    w_in = inp("w_in", (1024, 8192))
    conv_w = inp("conv_w", (4, 1024))
    conv_b = inp("conv_b", (1, 1024))
    w_rg = inp("w_rg", (8, 128, 128))
    b_rg = inp("b_rg", (1, 1024))
    w_ig = inp("w_ig", (8, 128, 128))
    b_ig = inp("b_ig", (1, 1024))
    lru_lambda = inp("lru_lambda", (1, 1024))
    q_gain = inp("q_norm_gain", (1, 64))
    k_gain = inp("k_norm_gain", (1, 64))
    rel_bias = inp("rel_bias", (32, 8))
    lam_q1 = inp("lam_q1", (1, 64))
    lam_k1 = inp("lam_k1", (1, 64))
    lam_q2 = inp("lam_q2", (1, 64))
    lam_k2 = inp("lam_k2", (1, 64))
    subln_gain = inp("subln_gain", (128, 1))
    w_pa = inp("w_proj_a", (1024, 1024))
    w_pb = inp("w_proj_b", (1024, 1024))
    w_out = inp("w_out", (1024, 1024))
    c_oh = inp("c_onehot", (32, 384))
    c_ident = inp("c_ident", (128, 128))
    c_bones = inp("c_blockones", (128, 128))

    y_p = outp("y_prompt", (BLx, T, D))
    y_s = outp("y_sample", (BL * TS, D))
    k_p = outp("k_prompt", (BLx, T, 1024))
    v_p = outp("v_prompt", (BLx, T, 1024))
    conv_p = outp("conv_prompt", (BL, 3, 1024))
    lru_p = outp("lru_prompt", (BL, 1024))
    k_s = outp("k_sample", (BL * TS, 1024))
    v_s = outp("v_sample", (BL * TS, 1024))
    conv_s = outp("conv_sample", (BL, 3, 1024))
    lru_s = outp("lru_sample", (BL, 1024))

    WF = nc.dram_tensor("WF", [48, 128, 8, 128], BF16, kind="Internal").ap()
    WKV = nc.dram_tensor("WKV", [8, 128, 8, 256], BF16, kind="Internal").ap()
    WP = nc.dram_tensor("WP", [2, 8, 128, 8, 128], BF16, kind="Internal").ap()
    WO = nc.dram_tensor("WO", [2, 128, 8, 512], BF16, kind="Internal").ap()
    REP_t = nc.dram_tensor("REP", [8, 128, 384], F32, kind="Internal")
    REP = REP_t.ap()

    with ExitStack() as stack:
        kb = K(nc, stack, needed)
        PE, ACT, DVE, POOL, SP = kb.pe, kb.act, kb.dve, kb.pool, kb.sp
        op, dma = kb.op, kb.dma
        stack.enter_context(nc.allow_non_contiguous_dma(reason="small param / state layout DMAs"))

        def sb(name, shape, dt):
            return stack.enter_context(nc.sbuf_tensor(name, list(shape), dt))

        XT = sb("XT", [128, 8, T], BF16)
        XTb = Buf("XT")
        YB = sb("YB", [128, 8, T], BF16)
        YBb = Buf("YB")
        QT = sb("QT", [128, 2, T], BF16)
        QTb = Buf("QT")
        KT = sb("KT", [128, PAST + 128], BF16)
        KTb = Buf("KT")
        NVS = 1
        VS = [sb(f"VS{i}", [128, 33, 130], BF16) for i in range(NVS)]
        VSb = [Buf(f"VS{i}") for i in range(NVS)]
        KC = [sb(f"KC{i}", [128, 32, 128], BF16) for i in range(NVS)]
        KCb = [Buf(f"KC{i}") for i in range(NVS)]
        ZB = sb("ZB", [128, T], BF16)
        ZBb = Buf("ZB")
        YA = sb("YA", [128, 8, 512], BF16)
        YAb = Buf("YA")
        MM = sb("MM", [128, 8, 512], BF16)
        MMb = Buf("MM")
        XAws = [sb(f"XAw{i}", [128, 640], F32) for i in range(2)]
        XAwbs = [Buf(f"XAw{i}") for i in range(2)]
        HALO = sb("HALO", [128, 8, 4, 3], F32)
        HALOb = Buf("HALO")
        HC = sb("HC", [128, 8, 4], F32)
        HCb = Buf("HC")
        NG = 12
        G = [sb(f"G{i}", [128, 512], F32) for i in range(NG)]
        Gb = [Buf(f"G{i}") for i in range(NG)]
        gfree = deque(range(NG))
        WR = [sb(f"WR{i}", [128, 4096], BF16) for i in range(2)]
        WRb = [Buf(f"WR{i}") for i in range(2)]
        wring = Ring([0, 1])
        ET = [sb(f"ET{i}", [128, 512], BF16) for i in range(3)]
        ETb = [Buf(f"ET{i}") for i in range(3)]
        ering = Ring([0, 1, 2])
        XS = [sb(f"XS{i}", [128, 1024], F32) for i in range(2)]
        XSb = [Buf(f"XS{i}") for i in range(2)]
        xsring = Ring([0, 1])
        XNB = [sb(f"XNB{i}", [128, 1024], BF16) for i in range(2)]
        XNBb = [Buf(f"XNB{i}") for i in range(2)]
        SM = [sb(f"SM{i}", [128, 8], F32) for i in range(24)]
        SMb = [Buf(f"SM{i}") for i in range(24)]
        smring = Ring(list(range(24)))
        g_row = sb("g_row", [128, 1024], F32)
        gk_rep = sb("gk_rep", [128, 128], F32)
        gq2 = sb("gq2", [128, 1], F32)
        sg8 = sb("sg8", [128, 1], F32)
        lamt = sb("lamt", [128, 4, 64], F32)
        lamc = sb("lamc", [128, 8], F32)
        c15 = sb("c15", [128, 8], F32)
        rb = sb("rb", [32, 8], F32)
        ohr = sb("ohr", [32, 384], F32)
        ident = sb("ident", [128, 128], BF16)
        bones = sb("bones", [128, 128], BF16)
        BDb = sb("BDb", [128, 8, 128], BF16)
        BSb = sb("BSb", [128, 8, 128], BF16)
        cw = sb("cw", [128, 8, 4], F32)
        cbias = sb("cbias", [128, 8], F32)
        brg = sb("brg", [128, 8], F32)
        big = sb("big", [128, 8], F32)
        lam_l = sb("lam_l", [128, 8], F32)
        nsp = sb("nsp", [128, 8], F32)
        WRG = sb("WRG", [128, 8, 128], BF16)
        WIG = sb("WIG", [128, 8, 128], BF16)
        SCV = sb("SCV", [128, 4, 8, 3], F32)
        LR0 = sb("LR0", [128, 4, 8], F32)
        CONVST = sb("CONVST", [128, 4, 8, 3], F32)
        CST = Buf("consts")
        CB = {}

        def cb(name):
            if name not in CB:
                CB[name] = Buf("c" + name)
            return CB[name]
        CONVSTb = Buf("CONVST")

        PS = [stack.enter_context(nc.psum_tensor(f"PS{i}", [128, 512], F32)) for i in range(8)]
        PSb = [Buf(f"PS{i}") for i in range(8)]
        ringA = Ring([0, 1, 2, 3])
        ringB = Ring([4, 5, 6, 7])
        ringAll = Ring([0, 1, 2, 3, 4, 5, 6, 7])

        def galloc():
            return gfree.popleft()

        def grel(*ids):
            for i in ids:
                gfree.append(i)

        if True:
            def body():
                wf_b = [Buf(f"WFd{g}") for g in range(6)]
                srcbase = [0, 1024, 2048, 5120, 6144, 7168]
                for g in range(6):
                    for cc in range(8):
                        c0 = srcbase[g] + cc * 128
                        dma(POOL, WF[g * 8 + cc], w_in[:, c0:c0 + 128].rearrange("(kc p) col -> p kc col", p=128),
                            writes=[wf_b[g]])
                wkv_b = Buf("WKVd")
                for h in range(8):
                    dma(POOL, WKV[h, :, :, 0:128],
                        w_in[:, 3072 + h * 128:3072 + (h + 1) * 128].rearrange("(kc p) c -> p kc c", p=128), writes=[wkv_b])
                    dma(POOL, WKV[h, :, :, 128:256],
                        w_in[:, 4096 + h * 128:4096 + (h + 1) * 128].rearrange("(kc p) c -> p kc c", p=128), writes=[wkv_b])
                wp_b = Buf("WPd")
                for j in range(8):
                    dma(POOL, WP[0, j], w_pa[:, j * 128:(j + 1) * 128].rearrange("(c p) col -> p c col", p=128), writes=[wp_b])
                    dma(POOL, WP[1, j], w_pb[:, j * 128:(j + 1) * 128].rearrange("(c p) col -> p c col", p=128), writes=[wp_b])
                wo_b = Buf("WOd")
                for n in range(2):
                    dma(POOL, WO[n], w_out[:, n * 512:(n + 1) * 512].rearrange("(j p) col -> p j col", p=128), writes=[wo_b])
                dma(POOL, WRG[:], w_rg.rearrange("n c d -> c n d"), writes=[cb("WRG")])
                dma(POOL, WIG[:], w_ig.rearrange("n c d -> c n d"), writes=[cb("WIG")])
                dma(POOL, ident[:], c_ident, writes=[cb("ident")])
                dma(POOL, bones[:], c_bones, writes=[cb("bones")])
                dma(SP, g_row[:], norm_gain.to_broadcast([128, 1024]), writes=[cb("g_row")])
                dma(SP, gk_rep[:, 0:64], k_gain.to_broadcast([128, 64]), writes=[cb("gk_rep")])
                dma(SP, gk_rep[:, 64:128], k_gain.to_broadcast([128, 64]), writes=[cb("gk_rep")])
                dma(SP, gq2[0:64, :], q_gain.rearrange("o d -> d o"), writes=[cb("gq2")])
                dma(SP, gq2[64:128, :], q_gain.rearrange("o d -> d o"), writes=[cb("gq2")])
                dma(SP, sg8[:], subln_gain, writes=[cb("sg8")])
                for i, lv in enumerate((lam_q1, lam_k1, lam_q2, lam_k2)):
                    dma(SP, lamt[:, i, :], lv.to_broadcast([128, 64]), writes=[cb("lam")])
                dma(SP, c15[:], rel_bias[15:16, :].to_broadcast([128, 8]), writes=[cb("c15")])
                dma(SP, rb[:], rel_bias, writes=[cb("rb")])
                dma(SP, ohr[:], c_oh, writes=[cb("ohr")])
                for j in range(4):
                    dma(SP, cw[:, :, j], conv_w[j].rearrange("(c p) -> p c", p=128), writes=[cb("cw")])
                dma(SP, cbias[:], conv_b.rearrange("o (c p) -> p (o c)", p=128), writes=[cb("cbias")])
                dma(SP, brg[:], b_rg.rearrange("o (c p) -> p (o c)", p=128), writes=[cb("brg")])
                dma(SP, big[:], b_ig.rearrange("o (c p) -> p (o c)", p=128), writes=[cb("big")])
                dma(SP, lam_l[:], lru_lambda.rearrange("o (c p) -> p (o c)", p=128), writes=[cb("nsp")])
                for b in range(BL):
                    for j in range(3):
                        dma(SP, SCV[:, b, :, j], st_conv[b, j].rearrange("(c p) -> p c", p=128), writes=[cb("SCV")])
                dma(SP, LR0[:], st_lru.rearrange("b (c p) -> p b c", p=128), writes=[cb("LR0")])

                op(DVE, lambda: nc.vector.tensor_tensor(out=lamt[:, 0, :], in0=lamt[:, 0, :], in1=lamt[:, 1, :], op=ALU.mult),
                   reads=[cb("lam")], writes=[cb("lam")])
                op(DVE, lambda: nc.vector.tensor_tensor(out=lamt[:, 2, :], in0=lamt[:, 2, :], in1=lamt[:, 3, :], op=ALU.mult),
                   reads=[cb("lam")], writes=[cb("lam")])
                op(DVE, lambda: nc.vector.tensor_reduce(out=lamc[:, 2:3], in_=lamt[:, 0, :], axis=AX.X, op=ALU.add),
                   reads=[cb("lam")], writes=[cb("lamc")])
                op(DVE, lambda: nc.vector.tensor_reduce(out=lamc[:, 3:4], in_=lamt[:, 2, :], axis=AX.X, op=ALU.add),
                   reads=[cb("lam")], writes=[cb("lamc")])
                op(ACT, lambda: nc.scalar.activation(out=lamc[:, 4:6], in_=lamc[:, 2:4], func=AF.Exp),
                   reads=[cb("lamc")], writes=[cb("lamc")])
                op(DVE, lambda: nc.vector.tensor_tensor(out=lamc[:, 6:7], in0=lamc[:, 4:5], in1=lamc[:, 5:6], op=ALU.subtract),
                   reads=[cb("lamc")], writes=[cb("lamc")])
                op(DVE, lambda: nc.vector.tensor_scalar(out=lamc[:, 0:1], in0=lamc[:, 6:7], scalar1=LAM_INIT, scalar2=None,
                                                        op0=ALU.add), reads=[cb("lamc")], writes=[cb("lamc")])
                op(DVE, lambda: nc.vector.tensor_scalar(out=lamc[:, 1:2], in0=lamc[:, 0:1], scalar1=-1.0, scalar2=None,
                                                        op0=ALU.mult), reads=[cb("lamc")], writes=[cb("lamc")])
                op(DVE, lambda: nc.vector.tensor_scalar(out=sg8[:], in0=sg8[:], scalar1=1.0 - LAM_INIT, scalar2=None,
                                                        op0=ALU.mult), reads=[cb("sg8")], writes=[cb("sg8")])
                op(ACT, lambda: nc.scalar.activation(out=nsp[:], in_=lam_l[:], func=AF.Exp, scale=-1.0),
                   reads=[cb("nsp")], writes=[cb("nsp")])
                op(ACT, lambda: nc.scalar.activation(out=nsp[:], in_=nsp[:], func=AF.Ln, bias=1.0),
                   reads=[cb("nsp")], writes=[cb("nsp")])
                op(DVE, lambda: nc.vector.tensor_scalar(out=nsp[:], in0=nsp[:], scalar1=-8.0, scalar2=None, op0=ALU.mult),
                   reads=[cb("nsp")], writes=[cb("nsp")])
                repb = Buf("REPd")
                grb = galloc()
                rbh = G[grb][0:32, :].bitcast(F32)
                for hh in range(2):
                    op(DVE, lambda: nc.vector.tensor_copy(out=rbh[:, 0:512].rearrange("p (h r) -> p h r", h=4),
                                                          in_=rb[:, hh * 4:(hh + 1) * 4].unsqueeze(2).to_broadcast([32, 4, 128])),
                       reads=[cb("rb")], writes=[Gb[grb]])
                    for h4 in range(4):
                        h = hh * 4 + h4
                        bk = ringA.next()
                        gi = galloc()
                        op(PE, lambda: nc.tensor.matmul(PS[bk][:, 0:384], lhsT=rbh[:, h4 * 128:(h4 + 1) * 128], rhs=ohr[:],
                                                        start=True, stop=True),
                           reads=[cb("ohr"), Gb[grb]], writes=[PSb[bk]])
                        op(DVE, lambda: nc.vector.tensor_scalar(out=G[gi][:, 0:384], in0=PS[bk][:, 0:384],
                                                                scalar1=c15[:, h:h + 1], scalar2=8.0,
                                                                op0=ALU.subtract, op1=ALU.mult),
                           reads=[PSb[bk], cb("c15")], writes=[Gb[gi]])
                        dma(SP, REP[h], G[gi][:, 0:384], reads=[Gb[gi]], writes=[repb], track=Gb[gi])
                        grel(gi)
                grel(grb)
                for hh in range(2):
                    gd, gs_ = galloc(), galloc()
                    bdv = G[gd][:].rearrange("p (h q) -> p h q", h=4)
                    bsv = G[gs_][:].rearrange("p (h q) -> p h q", h=4)
                    skd = bass.AP(tensor=REP_t, offset=hh * 4 * 128 * 384 + 127, ap=[[383, 128], [128 * 384, 4], [1, 128]])
                    sks = bass.AP(tensor=REP_t, offset=hh * 4 * 128 * 384 + 255, ap=[[383, 128], [128 * 384, 4], [1, 128]])
                    dma(SP, bdv, skd, reads=[repb], writes=[Gb[gd]])
                    dma(SP, bsv, sks, reads=[repb], writes=[Gb[gs_]])
                    op(POOL, lambda: nc.gpsimd.memset(bdv[64:128, :, 0:64], NEG), writes=[Gb[gd]])
                    op(DVE, lambda: nc.vector.tensor_copy(out=BDb[:, hh * 4:(hh + 1) * 4, :], in_=bdv), reads=[Gb[gd]], writes=[cb("BDb")])
                    op(DVE, lambda: nc.vector.tensor_copy(out=BSb[:, hh * 4:(hh + 1) * 4, :], in_=bsv), reads=[Gb[gs_]], writes=[cb("BSb")])
                    grel(gd, gs_)
                for i in range(NVS):
                    op(POOL, lambda: nc.gpsimd.memset(VS[i][:, :, 128:130], 1.0), writes=[VSb[i]])
                op(POOL, lambda: nc.gpsimd.memset(QT[:], 0.0), writes=[QTb])

                for e_ in (PE, ACT, DVE, POOL, SP):
                    for c_ in CB.values():
                        if c_.w is not None:
                            e_.wait(c_.w)
                for c_ in CB.values():
                    c_.w = None
                    c_.r = {}
                def chk(name):
                    if stop == name:
                        raise _Stop()

                def wload(parts):
                    wi = wring.next()
                    for (off, n, src, sbuf) in parts:
                        dma(SP, WR[wi][:, off:off + n], src, reads=[sbuf], writes=[WRb[wi]])
                    return wi

                def small():
                    i = smring.next()
                    return SM[i], SMb[i]

                def rstd_from_ss(ss_ap, ssb, n, inv_n, np_=128):
                    t1, t1b = small()
                    op(DVE, lambda: nc.vector.tensor_scalar(out=t1[0:np_, 0:n], in0=ss_ap, scalar1=inv_n, scalar2=EPS,
                                                            op0=ALU.mult, op1=ALU.add), reads=[ssb], writes=[t1b])
                    op(ACT, lambda: nc.scalar.activation(out=t1[0:np_, 0:n], in_=t1[0:np_, 0:n], func=AF.Sqrt),
                       reads=[t1b], writes=[t1b])
                    t2, t2b = small()
                    op(DVE, lambda: nc.vector.reciprocal(out=t2[0:np_, 0:n], in_=t1[0:np_, 0:n]), reads=[t1b], writes=[t2b])
                    return t2, t2b

                def run_job(sample, b_idx):
                    if not sample:
                        ntok = T
                        tiles = [(i * 512, 512) for i in range(4)]
                        subs = [(i * 128, 128) for i in range(16)]
                        x_src = x_p[b_idx]
                        k_dst, v_dst, y_dst = k_p[b_idx], v_p[b_idx], y_p[b_idx]
                        nseg, L = 1, 512
                    else:
                        ntok = BL * TS
                        tiles = [(0, 256)]
                        subs = [(i * 64, 64) for i in range(4)]
                        x_src = x_s
                        k_dst, v_dst, y_dst = k_s, v_s, y_s
                        nseg, L = 4, 64

                    if len(heads) < 8:
                        op(POOL, lambda: nc.gpsimd.memset(YB[:], 0.0), writes=[YBb])
                    for s0 in range(0, ntok, 128):
                        xi = xsring.next()
                        dma(SP, XS[xi][:], x_src[s0:s0 + 128, :], writes=[XSb[xi]])
                        ss, ssb = small()
                        op(ACT, lambda: nc.scalar.activation(out=XNB[xi][:], in_=XS[xi][:], func=AF.Square,
                                                             accum_out=ss[:, 0:1]),
                           reads=[XSb[xi]], writes=[XNBb[xi], ssb])
                        r, rb_ = rstd_from_ss(ss[:, 0:1], ssb, 1, 1.0 / D)
                        op(DVE, lambda: nc.vector.scalar_tensor_tensor(out=XNB[xi][:], in0=XS[xi][:], scalar=r[:, 0:1],
                                                                       in1=g_row[:], op0=ALU.mult, op1=ALU.mult),
                           reads=[XSb[xi], rb_, CST], writes=[XNBb[xi]])
                        bk = ringA.next()
                        pbf = PS[bk][:].bitcast(BF16)

                        def tr():
                            ins = None
                            for kc in range(8):
                                ins = nc.tensor.transpose(pbf[:, kc * 128:(kc + 1) * 128],
                                                          XNB[xi][:, kc * 128:(kc + 1) * 128], ident[:])
                            return ins
                        op(PE, tr, reads=[XNBb[xi], CST], writes=[PSb[bk]])
                        op(ACT, lambda: nc.scalar.copy(out=XT[:, :, s0:s0 + 128],
                                                       in_=pbf.rearrange("p (k t) -> p k t", k=8)),
                           reads=[PSb[bk]], writes=[XTb])

                    chk("s0")

                    def proj_fm(wi, woff, tcol, tw, bk):
                        def f():
                            ins = None
                            for kc in range(8):
                                ins = nc.tensor.matmul(PS[bk][:, 0:tw], lhsT=WR[wi][:, woff + kc * 128: woff + (kc + 1) * 128],
                                                       rhs=XT[:, kc, tcol:tcol + tw], start=(kc == 0), stop=(kc == 7))
                            return ins
                        op(PE, f, reads=[WRb[wi], XTb], writes=[PSb[bk]])

                    for h in heads:
                        wi = wload([(0, 1024, WF[16 + h].rearrange("p kc c -> p (kc c)"), wf_b[2]),
                                    (1024, 1024, WF[24 + h].rearrange("p kc c -> p (kc c)"), wf_b[3]),
                                    (2048, 2048, WKV[h].rearrange("p kc c -> p (kc c)"), wkv_b)])
                        for (tc0, tw) in tiles:
                            bk = ringA.next()
                            proj_fm(wi, 0, tc0, tw, bk)
                            gq, gs = galloc(), galloc()
                            op(DVE, lambda: nc.vector.tensor_copy(out=G[gq][:, 0:tw], in_=PS[bk][:, 0:tw]),
                               reads=[PSb[bk]], writes=[Gb[gq]])
                            sqb = G[gs][:].bitcast(BF16)
                            op(ACT, lambda: nc.scalar.activation(out=sqb[:, 0:tw], in_=PS[bk][:, 0:tw], func=AF.Square),
                               reads=[PSb[bk]], writes=[Gb[gs]])
                            bk2 = ringB.next()
                            op(PE, lambda: nc.tensor.matmul(PS[bk2][:, 0:tw], lhsT=bones[:], rhs=sqb[:, 0:tw],
                                                            start=True, stop=True),
                               reads=[Gb[gs], CST], writes=[PSb[bk2]])
                            gr = galloc()
                            op(ACT, lambda: nc.scalar.activation(out=G[gr][:, 0:tw], in_=PS[bk2][:, 0:tw], func=AF.Sqrt,
                                                                 scale=1.0 / 64, bias=EPS),
                               reads=[PSb[bk2]], writes=[Gb[gr]])
                            op(DVE, lambda: nc.vector.reciprocal(out=G[gr][:, 0:tw], in_=G[gr][:, 0:tw]),
                               reads=[Gb[gr]], writes=[Gb[gr]])
                            for m_ in range(2):
                                ps_ = slice(64 * m_, 64 * m_ + 64)
                                op(DVE, lambda: nc.vector.scalar_tensor_tensor(out=QT[ps_, m_, tc0:tc0 + tw],
                                                                               in0=G[gq][ps_, 0:tw],
                                                                               scalar=gq2[ps_, 0:1], in1=G[gr][ps_, 0:tw],
                                                                               op0=ALU.mult, op1=ALU.mult),
                                   reads=[Gb[gq], Gb[gr], CST], writes=[QTb])
                            grel(gq, gs, gr)
                            bk = ringA.next()
                            proj_fm(wi, 1024, tc0, tw, bk)
                            op(ACT, lambda: nc.scalar.activation(out=ZB[:, tc0:tc0 + tw], in_=PS[bk][:, 0:tw], func=AF.Silu),
                               reads=[PSb[bk]], writes=[ZBb])
                        chk("s1")
                        if not sample:
                            s2_prompt_kv(h, wi, subs, k_dst, v_dst)
                            chk("kv")
                            s2_prompt_attn(h)
                            chk("attn")
                        else:
                            s2_sample(h, wi, k_dst, v_dst)

                    for ti, (tc0, tw) in enumerate(tiles):
                        chk("heads")
                        s3(sample, ti, tc0, tw, nseg, L, len(tiles))
                        chk("s3")
                        s4(tc0, tw)
                        chk("s4")
                        s5(tc0, tw, x_src, y_dst)
                        chk("s5")
                    cdst = conv_s if sample else conv_p[b_idx:b_idx + 1]
                    ldst = lru_s if sample else lru_p[b_idx:b_idx + 1]
                    for sg in range(nseg):
                        for j in range(3):
                            dma(SP, cdst[sg, j].rearrange("(c p) -> p c", p=128), CONVST[:, sg, :, j],
                                reads=[CONVSTb], track=CONVSTb, is_output=True)
                    for sg in range(nseg):
                        dma(SP, ldst[sg].rearrange("(c p) -> p c", p=128), HC[:, :, sg],
                            reads=[HCb], track=HCb, is_output=True)

                def kv_batch(h, wi, items, vsi, k_dst, v_dst, pbf, trb):
                    n = len(items)
                    bks = []
                    for (s0, sn, kt_idx, pcol) in items:
                        bk = ringA.next()
                        bks.append(bk)

                        def f():
                            ins = None
                            for kc in range(8):
                                ins = nc.tensor.matmul(PS[bk][0:sn, 0:256], lhsT=XT[:, kc, s0:s0 + sn],
                                                       rhs=WR[wi][:, 2048 + kc * 256: 2048 + (kc + 1) * 256],
                                                       start=(kc == 0), stop=(kc == 7))
                            return ins
                        op(PE, f, reads=[WRb[wi], XTb], writes=[PSb[bk]])
                    gsq = [galloc() for _ in range(n)]
                    gkv = [galloc() for _ in range(n)]
                    sss = [small() for _ in range(n)]
                    t1s = [small() for _ in range(n)]
                    t2s = [small() for _ in range(n)]
                    for i, (s0, sn, kt_idx, pcol) in enumerate(items):
                        op(ACT, lambda: nc.scalar.activation(out=G[gsq[i]][0:sn, 0:128], in_=PS[bks[i]][0:sn, 0:128],
                                                             func=AF.Square), reads=[PSb[bks[i]]], writes=[Gb[gsq[i]]])
                    for i, (s0, sn, kt_idx, pcol) in enumerate(items):
                        op(DVE, lambda: nc.vector.tensor_reduce(out=sss[i][0][0:sn, 0:2],
                                                                in_=G[gsq[i]][0:sn, 0:128].rearrange("p (g d) -> p g d", g=2),
                                                                axis=AX.X, op=ALU.add),
                           reads=[Gb[gsq[i]]], writes=[sss[i][1]])
                    for i, (s0, sn, kt_idx, pcol) in enumerate(items):
                        op(DVE, lambda: nc.vector.tensor_scalar(out=t1s[i][0][0:sn, 0:2], in0=sss[i][0][0:sn, 0:2],
                                                                scalar1=1.0 / 64, scalar2=EPS, op0=ALU.mult, op1=ALU.add),
                           reads=[sss[i][1]], writes=[t1s[i][1]])
                    for i, (s0, sn, kt_idx, pcol) in enumerate(items):
                        op(ACT, lambda: nc.scalar.activation(out=t1s[i][0][0:sn, 0:2], in_=t1s[i][0][0:sn, 0:2], func=AF.Sqrt),
                           reads=[t1s[i][1]], writes=[t1s[i][1]])
                    for i, (s0, sn, kt_idx, pcol) in enumerate(items):
                        op(ACT, lambda: nc.scalar.copy(out=G[gkv[i]][0:sn, 128:256], in_=PS[bks[i]][0:sn, 128:256]),
                           reads=[PSb[bks[i]]], writes=[Gb[gkv[i]]])
                    for i, (s0, sn, kt_idx, pcol) in enumerate(items):
                        op(DVE, lambda: nc.vector.reciprocal(out=t2s[i][0][0:sn, 0:2], in_=t1s[i][0][0:sn, 0:2]),
                           reads=[t1s[i][1]], writes=[t2s[i][1]])
                    for i, (s0, sn, kt_idx, pcol) in enumerate(items):
                        op(DVE, lambda: nc.vector.tensor_tensor(out=G[gkv[i]][0:sn, 0:128].rearrange("p (g d) -> p g d", g=2),
                                                                in0=PS[bks[i]][0:sn, 0:128].rearrange("p (g d) -> p g d", g=2),
                                                                in1=t2s[i][0][0:sn, 0:2].unsqueeze(2).to_broadcast([sn, 2, 64]),
                                                                op=ALU.mult),
                           reads=[PSb[bks[i]], t2s[i][1]], writes=[Gb[gkv[i]]])
                    for i, (s0, sn, kt_idx, pcol) in enumerate(items):
                        op(DVE, lambda: nc.vector.tensor_tensor(out=G[gkv[i]][0:sn, 0:128], in0=G[gkv[i]][0:sn, 0:128],
                                                                in1=gk_rep[0:sn, :], op=ALU.mult),
                           reads=[Gb[gkv[i]], CST], writes=[Gb[gkv[i]]])
                    for i, (s0, sn, kt_idx, pcol) in enumerate(items):
                        dma(SP, k_dst[s0:s0 + sn, h * 128:(h + 1) * 128], G[gkv[i]][0:sn, 0:128], reads=[Gb[gkv[i]]],
                            track=Gb[gkv[i]], is_output=True)
                        dma(SP, v_dst[s0:s0 + sn, h * 128:(h + 1) * 128], G[gkv[i]][0:sn, 128:256], reads=[Gb[gkv[i]]],
                            track=Gb[gkv[i]], is_output=True)
                    for i, (s0, sn, kt_idx, pcol) in enumerate(items):
                        knb = G[gsq[i]][:].bitcast(BF16)
                        op(POOL, lambda: nc.gpsimd.tensor_copy(out=knb[0:sn, 0:128], in_=G[gkv[i]][0:sn, 0:128]),
                           reads=[Gb[gkv[i]]], writes=[Gb[gsq[i]]])
                        op(POOL, lambda: nc.gpsimd.tensor_copy(out=VS[vsi][0:sn, kt_idx, 0:128], in_=G[gkv[i]][0:sn, 128:256]),
                           reads=[Gb[gkv[i]]], writes=[VSb[vsi]])
                    for i, (s0, sn, kt_idx, pcol) in enumerate(items):
                        knb = G[gsq[i]][:].bitcast(BF16)
                        op(PE, lambda: nc.tensor.transpose(pbf[:, pcol:pcol + sn], knb[0:sn, 0:128], ident[0:sn, 0:sn]),
                           reads=[Gb[gsq[i]], CST], writes=[trb])
                    grel(*gsq)
                    grel(*gkv)

                def s2_prompt_kv(h, wi, subs, k_dst, v_dst):
                    for g4 in range(0, 16, 4):
                        bkt = ringB.next()
                        pbf = PS[bkt][:].bitcast(BF16)
                        items = [(subs[g4 + j][0], subs[g4 + j][1], g4 + j, j * 128) for j in range(4)]
                        kv_batch(h, wi, items, 0, k_dst, v_dst, pbf, PSb[bkt])
                        op(DVE, lambda: nc.vector.tensor_copy(out=KT[:, g4 * 128:g4 * 128 + 512], in_=pbf[:, 0:512]),
                           reads=[PSb[bkt]], writes=[KTb])

                def attn_post_a(h, obank, nq, ycol):
                    ov = PS[obank][0:nq, 0:258].rearrange("p (m d) -> p m d", m=2)
                    rz, rzb = small()
                    op(DVE, lambda: nc.vector.reciprocal(out=rz[0:nq, 0:2], in_=ov[:, :, 128]),
                       reads=[PSb[obank]], writes=[rzb])
                    op(DVE, lambda: nc.vector.tensor_tensor(out=rz[0:nq, 2:3], in0=rz[0:nq, 1:2], in1=lamc[0:nq, 1:2],
                                                            op=ALU.mult), reads=[rzb, CST], writes=[rzb])
                    go = galloc()
                    op(DVE, lambda: nc.vector.tensor_scalar(out=G[go][0:nq, 0:128], in0=ov[:, 0, 0:128],
                                                            scalar1=rz[0:nq, 0:1], scalar2=None, op0=ALU.mult),
                       reads=[PSb[obank], rzb], writes=[Gb[go]])
                    op(DVE, lambda: nc.vector.scalar_tensor_tensor(out=G[go][0:nq, 0:128], in0=ov[:, 1, 0:128],
                                                                   scalar=rz[0:nq, 2:3], in1=G[go][0:nq, 0:128],
                                                                   op0=ALU.mult, op1=ALU.add),
                       reads=[PSb[obank], rzb, Gb[go]], writes=[Gb[go]])
                    ss, ssb = small()
                    op(ACT, lambda: nc.scalar.activation(out=G[go][0:nq, 128:256], in_=G[go][0:nq, 0:128], func=AF.Square,
                                                         accum_out=ss[0:nq, 0:1]),
                       reads=[Gb[go]], writes=[Gb[go], ssb])
                    rs, rsb = rstd_from_ss(ss[0:nq, 0:1], ssb, 1, 1.0 / 128, np_=nq)
                    onb = G[go][:].bitcast(BF16)
                    op(ACT, lambda: nc.scalar.activation(out=onb[0:nq, 512:640], in_=G[go][0:nq, 0:128], func=AF.Copy,
                                                         scale=rs[0:nq, 0:1]),
                       reads=[Gb[go], rsb], writes=[Gb[go]])
                    return (h, go, nq, ycol)

                def attn_post_b(st):
                    h, go, nq, ycol = st
                    onb = G[go][:].bitcast(BF16)
                    bk = ringA.next()
                    pbf = PS[bk][:].bitcast(BF16)
                    op(PE, lambda: nc.tensor.transpose(pbf[:, 0:nq], onb[0:nq, 512:640], ident[0:nq, 0:nq]),
                       reads=[Gb[go], CST], writes=[PSb[bk]])
                    op(DVE, lambda: nc.vector.scalar_tensor_tensor(out=YB[:, h, ycol:ycol + nq], in0=pbf[:, 0:nq],
                                                                   scalar=sg8[:, 0:1], in1=ZB[:, ycol:ycol + nq],
                                                                   op0=ALU.mult, op1=ALU.mult),
                       reads=[PSb[bk], ZBb, CST], writes=[YBb])
                    grel(go)

                def attn_post(h, obank, nq, ycol):
                    attn_post_b(attn_post_a(h, obank, nq, ycol))

                def s2_prompt_attn(h):
                    pending = []
                    for t in range(8):
                        ob = [ringB.next(), ringB.next()]
                        nkt = 2 * t + 2

                        def sc_exp(kt):
                            c0 = 0 if kt <= 2 * t else 128
                            bk = ringA.next()
                            sv = PS[bk][:].rearrange("p (m q) -> p m q", m=2)

                            def sc():
                                blocks = [ql for ql in range(2) if 0 <= (2 * t + ql) - kt <= 1]
                                ins = nc.tensor.matmul(sv[:, :, c0:256], lhsT=KT[:, kt * 128:(kt + 1) * 128],
                                                       rhs=QT[:, :, t * 256 + c0:t * 256 + 256],
                                                       start=True, stop=(len(blocks) == 0), skip_group_check=True)
                                for bi, ql in enumerate(blocks):
                                    bt = BDb if (2 * t + ql) == kt else BSb
                                    for m in range(2):
                                        ins = nc.tensor.matmul(sv[:, m, ql * 128:(ql + 1) * 128], lhsT=ident[:],
                                                               rhs=bt[:, h, :], start=False,
                                                               stop=(bi == len(blocks) - 1 and m == 1),
                                                               skip_group_check=True)
                                return ins
                            op(PE, sc, reads=[KTb, QTb, CST], writes=[PSb[bk]])
                            ei = ering.next()
                            ev = ET[ei][:].rearrange("p (m q) -> p m q", m=2)
                            op(ACT, lambda: nc.scalar.activation(out=ev[:, :, c0:256], in_=sv[:, :, c0:256], func=AF.Exp,
                                                                 scale=0.125, bias=c15[:, h:h + 1]),
                               reads=[PSb[bk], CST], writes=[ETb[ei]])
                            return ei, ev

                        def pvf(kt, ei, ev):
                            def pv():
                                ins = None
                                for ql in range(2):
                                    if kt > 2 * t + ql:
                                        continue
                                    last = (kt == 2 * t + ql)
                                    for m in range(2):
                                        ins = nc.tensor.matmul(PS[ob[ql]][:, m * 129:(m + 1) * 129],
                                                               lhsT=ev[:, m, ql * 128:(ql + 1) * 128],
                                                               rhs=VS[0][:, kt, 0:129], start=(kt == 0 and m == 0), stop=last,
                                                               skip_group_check=True)
                                return ins
                            op(PE, pv, reads=[ETb[ei], VSb[0]], writes=[PSb[ob[0]], PSb[ob[1]]])

                        cur = sc_exp(0)
                        for kt in range(nkt):
                            nxt = sc_exp(kt + 1) if kt + 1 < nkt else None
                            if kt == 1:
                                for st_ in pending:
                                    attn_post_b(st_)
                                pending = []
                            pvf(kt, *cur)
                            cur = nxt
                        for ql in range(2):
                            pending.append(attn_post_a(h, ob[ql], 128, t * 256 + ql * 128))
                    for st_ in pending:
                        attn_post_b(st_)

                def s2_sample(h, wi, k_dst, v_dst):
                    for b in range(BL):
                        u = h * BL + b
                        vsi = u % NVS
                        kci = u % NVS
                        dma(POOL, KC[kci][:], ck[b, :, h * 128:(h + 1) * 128].rearrange("(kt p) c -> p kt c", p=128),
                            writes=[KCb[kci]])
                        dma(POOL, VS[vsi][:, 0:32, 0:128],
                            cv[b, :, h * 128:(h + 1) * 128].rearrange("(kt p) c -> p kt c", p=128), writes=[VSb[vsi]])
                        for g8 in range(0, 32, 8):
                            bkt = ringB.next()
                            pbf = PS[bkt][:].bitcast(BF16)

                            def tr():
                                ins = None
                                for j in range(8):
                                    ins = nc.tensor.transpose(pbf[:, j * 128:(j + 1) * 128], KC[kci][:, g8 + j, :], ident[:])
                                return ins
                            op(PE, tr, reads=[KCb[kci], CST], writes=[PSb[bkt]])
                            op(DVE, lambda: nc.vector.tensor_copy(out=KT[:, g8 * 128:(g8 + 8) * 128], in_=pbf[:, 0:1024]),
                               reads=[PSb[bkt]], writes=[KTb])
                        bkt = ringB.next()
                        pbf = PS[bkt][:].bitcast(BF16)
                        kv_batch(h, wi, [(b * TS, TS, 32, 0)], vsi, k_dst, v_dst, pbf, PSb[bkt])
                        op(DVE, lambda: nc.vector.tensor_copy(out=KT[:, PAST:PAST + TS], in_=pbf[:, 0:TS]),
                           reads=[PSb[bkt]], writes=[KTb])
                        obank = ringB.next()
                        for g4 in range(0, 33, 4):
                            kts = list(range(g4, min(g4 + 4, 33)))
                            bk = ringA.next()
                            sv = PS[bk][:].rearrange("p (j m q) -> p j m q", j=4, m=2)

                            def sc():
                                ins = None
                                for j, kt in enumerate(kts):
                                    nk = 128 if kt < 32 else TS
                                    nb = kt >= 31
                                    ins = nc.tensor.matmul(sv[0:nk, j, :, :], lhsT=KT[:, kt * 128:kt * 128 + nk],
                                                           rhs=QT[:, :, b * TS:(b + 1) * TS],
                                                           start=True, stop=not nb, skip_group_check=True)
                                    for m in range(2):
                                        if kt == 31:
                                            ins = nc.tensor.matmul(sv[:, j, m, :], lhsT=ident[:], rhs=BSb[:, h, 0:TS],
                                                                   start=False, stop=(m == 1), skip_group_check=True)
                                        if kt == 32:
                                            ins = nc.tensor.matmul(sv[0:TS, j, m, :], lhsT=ident[0:TS, 0:TS],
                                                                   rhs=BDb[0:TS, h, 0:TS], start=False, stop=(m == 1),
                                                                   skip_group_check=True)
                                return ins
                            op(PE, sc, reads=[KTb, QTb, CST], writes=[PSb[bk]])
                            ei = ering.next()
                            ev = ET[ei][:].rearrange("p (j m q) -> p j m q", j=4, m=2)
                            nfull = len([kt for kt in kts if kt < 32])
                            if nfull:
                                op(ACT, lambda: nc.scalar.activation(out=ev[:, 0:nfull], in_=sv[:, 0:nfull], func=AF.Exp,
                                                                     scale=0.125, bias=c15[:, h:h + 1]),
                                   reads=[PSb[bk], CST], writes=[ETb[ei]])
                            if kts[-1] == 32:
                                j = len(kts) - 1
                                op(ACT, lambda: nc.scalar.activation(out=ev[0:TS, j], in_=sv[0:TS, j], func=AF.Exp,
                                                                     scale=0.125, bias=c15[0:TS, h:h + 1]),
                                   reads=[PSb[bk], CST], writes=[ETb[ei]])

                            def pv():
                                ins = None
                                for j, kt in enumerate(kts):
                                    nk = 128 if kt < 32 else TS
                                    for m in range(2):
                                        ins = nc.tensor.matmul(PS[obank][0:TS, m * 129:(m + 1) * 129],
                                                               lhsT=ev[0:nk, j, m, :], rhs=VS[vsi][0:nk, kt, 0:129],
                                                               start=(kt == 0 and m == 0), stop=(kt == 32), skip_group_check=True)
                                return ins
                            op(PE, pv, reads=[ETb[ei], VSb[vsi]], writes=[PSb[obank]])
                        attn_post(h, obank, TS, b * TS)

                def s3(sample, ti, tc0, tw, nseg, L, ntiles):
                    for c0_ in range(0, 8, 2):
                        cs = [c0_, c0_ + 1]
                        R = range(len(cs))
                        wis = [wload([(0, 1024, WF[c].rearrange("p kc c -> p (kc c)"), wf_b[0]),
                                      (1024, 1024, WF[8 + c].rearrange("p kc c -> p (kc c)"), wf_b[1])]) for c in cs]
                        xavs = [XAws[i][:, 0:nseg * (L + 3)].rearrange("p (s l) -> p s l", s=nseg) for i in R]
                        for i, c in enumerate(cs):
                            xav = xavs[i]
                            if sample:
                                op(DVE, lambda: nc.vector.tensor_copy(out=xav[:, :, 0:3], in_=SCV[:, :, c, :]),
                                   reads=[CST], writes=[XAwbs[i]])
                            elif ti == 0:
                                op(DVE, lambda: nc.vector.memset(xav[:, :, 0:3], 0.0), writes=[XAwbs[i]])
                            else:
                                op(DVE, lambda: nc.vector.tensor_copy(out=xav[:, :, 0:3], in_=HALO[:, c, 0:nseg, :]),
                                   reads=[HALOb], writes=[XAwbs[i]])
                        bkx = [ringAll.next() for _ in R]
                        for i, c in enumerate(cs):
                            proj_fm2(wis[i], 0, tc0, tw, bkx[i])
                        for i, c in enumerate(cs):
                            op(ACT, lambda: nc.scalar.copy(out=xavs[i][:, :, 3:3 + L],
                                                           in_=PS[bkx[i]][:, 0:tw].rearrange("p (s l) -> p s l", s=nseg)),
                               reads=[PSb[bkx[i]]], writes=[XAwbs[i]])
                        for i, c in enumerate(cs):
                            op(POOL, lambda: nc.gpsimd.tensor_copy(out=HALO[:, c, 0:nseg, :], in_=xavs[i][:, :, L:L + 3]),
                               reads=[XAwbs[i]], writes=[HALOb])
                            if sample or ti == ntiles - 1:
                                op(POOL, lambda: nc.gpsimd.tensor_copy(out=CONVST[:, 0:nseg, c, :], in_=xavs[i][:, :, L:L + 3]),
                                   reads=[XAwbs[i]], writes=[CONVSTb])
                        gx = [galloc() for _ in R]
                        xcvs = [G[gx[i]][:, 0:tw].rearrange("p (s l) -> p s l", s=nseg) for i in R]
                        for i, c in enumerate(cs):
                            op(DVE, lambda: nc.vector.tensor_scalar(out=xcvs[i], in0=xavs[i][:, :, 0:L], scalar1=cw[:, c, 0:1],
                                                                    scalar2=cbias[:, c:c + 1], op0=ALU.mult, op1=ALU.add),
                               reads=[XAwbs[i], CST], writes=[Gb[gx[i]]])
                        for j in range(1, 4):
                            for i, c in enumerate(cs):
                                op(DVE, lambda: nc.vector.scalar_tensor_tensor(out=xcvs[i], in0=xavs[i][:, :, j:j + L],
                                                                               scalar=cw[:, c, j:j + 1], in1=xcvs[i],
                                                                               op0=ALU.mult, op1=ALU.add),
                                   reads=[XAwbs[i], Gb[gx[i]], CST], writes=[Gb[gx[i]]])
                        gxb = [galloc() for _ in R]
                        xcbs = [G[gxb[i]][:].bitcast(BF16) for i in R]
                        for i, c in enumerate(cs):
                            op(ACT, lambda: nc.scalar.copy(out=xcbs[i][:, 0:tw], in_=G[gx[i]][:, 0:tw]),
                               reads=[Gb[gx[i]]], writes=[Gb[gxb[i]]])
                        bkr = [ringAll.next() for _ in R]
                        bki = [ringAll.next() for _ in R]
                        for i, c in enumerate(cs):
                            op(PE, lambda: nc.tensor.matmul(PS[bkr[i]][:, 0:tw], lhsT=WRG[:, c, :], rhs=xcbs[i][:, 0:tw],
                                                            start=True, stop=True), reads=[Gb[gxb[i]], CST], writes=[PSb[bkr[i]]])
                            op(PE, lambda: nc.tensor.matmul(PS[bki[i]][:, 0:tw], lhsT=WIG[:, c, :], rhs=xcbs[i][:, 0:tw],
                                                            start=True, stop=True), reads=[Gb[gxb[i]], CST], writes=[PSb[bki[i]]])
                        bkz = [ringAll.next() for _ in R]
                        for i, c in enumerate(cs):
                            proj_fm2(wis[i], 1024, tc0, tw, bkz[i])
                        gr_ = [galloc() for _ in R]
                        gi_ = [galloc() for _ in R]
                        ga_ = [galloc() for _ in R]
                        for i, c in enumerate(cs):
                            op(ACT, lambda: nc.scalar.activation(out=G[gr_[i]][:, 0:tw], in_=PS[bkr[i]][:, 0:tw], func=AF.Sigmoid,
                                                                 bias=brg[:, c:c + 1]), reads=[PSb[bkr[i]], CST], writes=[Gb[gr_[i]]])
                            op(ACT, lambda: nc.scalar.activation(out=G[gi_[i]][:, 0:tw], in_=PS[bki[i]][:, 0:tw], func=AF.Sigmoid,
                                                                 bias=big[:, c:c + 1]), reads=[PSb[bki[i]], CST], writes=[Gb[gi_[i]]])
                        for i, c in enumerate(cs):
                            op(ACT, lambda: nc.scalar.activation(out=G[ga_[i]][:, 0:tw], in_=G[gr_[i]][:, 0:tw], func=AF.Exp,
                                                                 scale=nsp[:, c:c + 1]), reads=[Gb[gr_[i]], CST], writes=[Gb[ga_[i]]])
                        for i, c in enumerate(cs):
                            op(DVE, lambda: nc.vector.tensor_tensor(out=G[gi_[i]][:, 0:tw], in0=G[gi_[i]][:, 0:tw],
                                                                    in1=G[gx[i]][:, 0:tw], op=ALU.mult),
                               reads=[Gb[gi_[i]], Gb[gx[i]]], writes=[Gb[gi_[i]]])
                        for i, c in enumerate(cs):
                            op(DVE, lambda: nc.vector.tensor_tensor(out=G[gr_[i]][:, 0:tw], in0=G[ga_[i]][:, 0:tw],
                                                                    in1=G[ga_[i]][:, 0:tw], op=ALU.mult),
                               reads=[Gb[ga_[i]]], writes=[Gb[gr_[i]]])
                        for i, c in enumerate(cs):
                            op(ACT, lambda: nc.scalar.activation(out=G[gr_[i]][:, 0:tw], in_=G[gr_[i]][:, 0:tw], func=AF.Sqrt,
                                                                 scale=-1.0, bias=1.0), reads=[Gb[gr_[i]]], writes=[Gb[gr_[i]]])
                        for i, c in enumerate(cs):
                            if (not sample) and ti == 0:
                                op(DVE, lambda: nc.vector.memset(G[gr_[i]][:, 0:1], 1.0), writes=[Gb[gr_[i]]])
                            op(DVE, lambda: nc.vector.tensor_tensor(out=G[gi_[i]][:, 0:tw], in0=G[gi_[i]][:, 0:tw],
                                                                    in1=G[gr_[i]][:, 0:tw], op=ALU.mult),
                               reads=[Gb[gi_[i]], Gb[gr_[i]]], writes=[Gb[gi_[i]]])
                        for i, c in enumerate(cs):
                            for sg in range(nseg):
                                if sample:
                                    init = LR0[:, sg, c:c + 1]
                                    rdi = [CST]
                                elif ti == 0:
                                    init = 0.0
                                    rdi = []
                                else:
                                    init = HC[:, c, sg:sg + 1]
                                    rdi = [HCb]
                                op(DVE, lambda: nc.vector.tensor_tensor_scan(out=G[gx[i]][:, sg * L:(sg + 1) * L],
                                                                             data0=G[ga_[i]][:, sg * L:(sg + 1) * L],
                                                                             data1=G[gi_[i]][:, sg * L:(sg + 1) * L],
                                                                             initial=init, op0=ALU.mult, op1=ALU.add),
                                   reads=[Gb[ga_[i]], Gb[gi_[i]]] + rdi, writes=[Gb[gx[i]]])
                        for i, c in enumerate(cs):
                            hv = G[gx[i]][:, 0:tw].rearrange("p (s l) -> p s l", s=nseg)
                            op(POOL, lambda: nc.gpsimd.tensor_copy(out=HC[:, c, 0:nseg], in_=hv[:, :, L - 1]),
                               reads=[Gb[gx[i]]], writes=[HCb])
                        for i, c in enumerate(cs):
                            op(ACT, lambda: nc.scalar.activation(out=G[ga_[i]][:, 0:tw], in_=PS[bkz[i]][:, 0:tw], func=AF.Silu),
                               reads=[PSb[bkz[i]]], writes=[Gb[ga_[i]]])
                        for i, c in enumerate(cs):
                            op(DVE, lambda: nc.vector.tensor_tensor(out=YA[:, c, 0:tw], in0=G[gx[i]][:, 0:tw],
                                                                    in1=G[ga_[i]][:, 0:tw], op=ALU.mult),
                               reads=[Gb[gx[i]], Gb[ga_[i]]], writes=[YAb])
                        grel(*gx)
                        grel(*gxb)
                        grel(*gr_)
                        grel(*gi_)
                        grel(*ga_)

                def proj_fm2(wi, woff, tcol, tw, bk):
                    def f():
                        ins = None
                        for kc in range(8):
                            ins = nc.tensor.matmul(PS[bk][:, 0:tw], lhsT=WR[wi][:, woff + kc * 128: woff + (kc + 1) * 128],
                                                   rhs=XT[:, kc, tcol:tcol + tw], start=(kc == 0), stop=(kc == 7))
                        return ins
                    op(PE, f, reads=[WRb[wi], XTb], writes=[PSb[bk]])

                def s4(tc0, tw):
                    for j in range(8):
                        wi = wload([(0, 1024, WP[0, j].rearrange("p c col -> p (c col)"), wp_b),
                                    (1024, 1024, WP[1, j].rearrange("p c col -> p (c col)"), wp_b),
                                    (2048, 1024, WF[32 + j].rearrange("p kc c -> p (kc c)"), wf_b[4]),
                                    (3072, 1024, WF[40 + j].rearrange("p kc c -> p (kc c)"), wf_b[5])])
                        bpa, bpb, bga, bgb = ringAll.next(), ringAll.next(), ringAll.next(), ringAll.next()

                        def mk(bk, woff, src, srcb):
                            def f():
                                ins = None
                                for cc in range(8):
                                    rhs = src[:, cc, 0:tw] if src is YA else src[:, cc, tc0:tc0 + tw]
                                    ins = nc.tensor.matmul(PS[bk][:, 0:tw],
                                                           lhsT=WR[wi][:, woff + cc * 128: woff + (cc + 1) * 128],
                                                           rhs=rhs, start=(cc == 0), stop=(cc == 7))
                                return ins
                            op(PE, f, reads=[WRb[wi], srcb], writes=[PSb[bk]])
                        mk(bga, 2048, XT, XTb)
                        mk(bgb, 3072, XT, XTb)
                        mk(bpa, 0, YA, YAb)
                        mk(bpb, 1024, YB, YBb)
                        g1, g2 = galloc(), galloc()
                        op(ACT, lambda: nc.scalar.activation(out=G[g1][:, 0:tw], in_=PS[bga][:, 0:tw], func=AF.Sigmoid),
                           reads=[PSb[bga]], writes=[Gb[g1]])
                        op(ACT, lambda: nc.scalar.activation(out=G[g2][:, 0:tw], in_=PS[bgb][:, 0:tw], func=AF.Sigmoid),
                           reads=[PSb[bgb]], writes=[Gb[g2]])
                        op(DVE, lambda: nc.vector.tensor_tensor(out=G[g1][:, 0:tw], in0=PS[bpa][:, 0:tw], in1=G[g1][:, 0:tw],
                                                                op=ALU.mult), reads=[PSb[bpa], Gb[g1]], writes=[Gb[g1]])
                        op(DVE, lambda: nc.vector.tensor_tensor(out=G[g2][:, 0:tw], in0=PS[bpb][:, 0:tw], in1=G[g2][:, 0:tw],
                                                                op=ALU.mult), reads=[PSb[bpb], Gb[g2]], writes=[Gb[g2]])
                        op(POOL, lambda: nc.gpsimd.tensor_tensor(out=MM[:, j, 0:tw], in0=G[g1][:, 0:tw], in1=G[g2][:, 0:tw],
                                                                 op=ALU.add), reads=[Gb[g1], Gb[g2]], writes=[MMb])
                        grel(g1, g2)

                def s5(tc0, tw, x_src, y_dst):
                    wis = [wload([(0, 4096, WO[n].rearrange("p j col -> p (j col)"), wo_b)]) for n in range(2)]
                    for s0 in range(0, tw, 128):
                        for n in range(2):
                            wi = wis[n]
                            bk = ringAll.next()

                            def f():
                                ins = None
                                for j in range(8):
                                    ins = nc.tensor.matmul(PS[bk][:, 0:512], lhsT=MM[:, j, s0:s0 + 128],
                                                           rhs=WR[wi][:, j * 512:(j + 1) * 512],
                                                           start=(j == 0), stop=(j == 7))
                                return ins
                            op(PE, f, reads=[WRb[wi], MMb], writes=[PSb[bk]])
                            gx_ = galloc()
                            r0 = tc0 + s0
                            dma(SP, G[gx_][:, :], x_src[r0:r0 + 128, n * 512:(n + 1) * 512], writes=[Gb[gx_]])
                            op(DVE, lambda: nc.vector.tensor_tensor(out=G[gx_][:, :], in0=PS[bk][:, 0:512], in1=G[gx_][:, :],
                                                                    op=ALU.add), reads=[PSb[bk], Gb[gx_]], writes=[Gb[gx_]])
                            dma(SP, y_dst[r0:r0 + 128, n * 512:(n + 1) * 512], G[gx_][:, :], reads=[Gb[gx_]],
                                track=Gb[gx_], is_output=True)
                            grel(gx_)

                try:
                    chk("pro")
                    for b in jobs:
                        run_job(False, b)
                        kb.new_epoch()
                    if do_sample:
                        run_job(True, None)
                except _Stop:
                    pass
                last = {}
                for ev in kb.out_events:
                    if ev.key not in last or last[ev.key].val < ev.val:
                        last[ev.key] = ev
                for ev in last.values():
                    SP.wait(ev)

            body()

    return nc, kb.used


_CONSTS = None


def _consts():
    global _CONSTS
    if _CONSTS is None:
        bo = np.zeros((128, 128), np.float32)
        bo[0:64, 0:64] = 1.0
        bo[64:128, 64:128] = 1.0
        _CONSTS = {"c_onehot": _bucket_table(), "c_ident": np.eye(128, dtype=np.float32), "c_blockones": bo}
    return _CONSTS


def kernel(**inputs):
    f = lambda a: np.ascontiguousarray(np.asarray(a, dtype=np.float32))
    x_prompt = f(inputs["x_prompt"])
    x_sample = f(inputs["x_sample"])
    cache_k = f(inputs["cache_k"])[0].reshape(32, PAST, 1024)
    cache_v = f(inputs["cache_v"])[0].reshape(32, PAST, 1024)
    state_conv = f(inputs["state_conv"])[0]
    state_lru = f(inputs["state_lru"])[0]
    shared = {
        "norm_gain": f(inputs["norm_gain"]), "w_in": f(inputs["w_in"])[0], "conv_w": f(inputs["conv_w"])[0],
        "conv_b": f(inputs["conv_b"]), "w_rg": f(inputs["w_rg"])[0], "b_rg": f(inputs["b_rg"]),
        "w_ig": f(inputs["w_ig"])[0], "b_ig": f(inputs["b_ig"]), "lru_lambda": f(inputs["lru_lambda"]),
        "q_norm_gain": f(inputs["q_norm_gain"]), "k_norm_gain": f(inputs["k_norm_gain"]),
        "rel_bias": f(inputs["rel_bias"]), "lam_q1": f(inputs["lam_q1"]), "lam_k1": f(inputs["lam_k1"]),
        "lam_q2": f(inputs["lam_q2"]), "lam_k2": f(inputs["lam_k2"]),
        "subln_gain": f(inputs["subln_gain"]).reshape(128, 1),
        "w_proj_a": f(inputs["w_proj_a"])[0], "w_proj_b": f(inputs["w_proj_b"])[0], "w_out": f(inputs["w_out"])[0],
    }
    shared.update(_consts())
    in_maps = []
    for c in range(NCORES):
        sl = slice(c * BL, (c + 1) * BL)
        m = dict(shared)
        m["x_prompt"] = x_prompt[sl]
        m["x_sample"] = x_sample[sl].reshape(BL * TS, D)
        m["cache_k"] = cache_k[sl]
        m["cache_v"] = cache_v[sl]
        m["state_conv"] = state_conv[sl]
        m["state_lru"] = state_lru[sl]
        in_maps.append(m)
    nc = build_program()
    res = run_bass_kernel_spmd(nc, in_maps, core_ids=list(range(NCORES)))
    R = res.results
    cat = lambda name: np.concatenate([np.asarray(r[name], dtype=np.float32) for r in R], axis=0)
    y_prompt = cat("y_prompt")
    y_sample = cat("y_sample").reshape(32, TS, D)
    k_prompt = cat("k_prompt").reshape(1, 32, T, 8, 128)
    v_prompt = cat("v_prompt").reshape(1, 32, T, 8, 128)
    conv_prompt = cat("conv_prompt").reshape(1, 32, 3, 1024)
    lru_prompt = cat("lru_prompt").reshape(1, 32, 1024)
    k_sample = cat("k_sample").reshape(1, 32, TS, 8, 128)
    v_sample = cat("v_sample").reshape(1, 32, TS, 8, 128)
    conv_sample = cat("conv_sample").reshape(1, 32, 3, 1024)
    lru_sample = cat("lru_sample").reshape(1, 32, 1024)
    return (y_prompt, y_sample, k_prompt, v_prompt, conv_prompt, lru_prompt, k_sample, v_sample, conv_sample, lru_sample)
```

```python
import numpy as np
from collections import deque
from contextlib import ExitStack
import concourse.bass as bass
import concourse.mybir as mybir
from concourse.bass_utils import run_bass_kernel_spmd

F32 = mybir.dt.float32
BF16 = mybir.dt.bfloat16
AF = mybir.ActivationFunctionType
ALU = mybir.AluOpType
AX = mybir.AxisListType

NCORES = 8
D = 1024
T = 2048
BL = 4
TS = 64
PAST = 4096
EPS = 1e-6
LAM_INIT = 0.2
NEG = -30000.0


class Ev:
    __slots__ = ("sem", "val", "key", "opid")

    def __init__(self, sem, val, key, opid=None):
        self.sem, self.val, self.key, self.opid = sem, val, key, opid


class Buf:
    def __init__(self, name):
        self.name = name
        self.w = None
        self.r = {}
        self.dsem = None
        self.psum = name.startswith("PS")


class Eng:
    def __init__(self, name, h, is_pe=False):
        self.name, self.h, self.is_pe = name, h, is_pe
        self.sem = None
        self.key = None
        self.cnt = 0
        self.seq = 0
        self.seen = {}
        self.used = None

    def wait(self, ev):
        if self.seen.get(ev.key, 0) >= ev.val:
            return
        self.h.wait_ge(ev.sem, ev.val)
        self.seen[ev.key] = ev.val
        if ev.opid is not None and self.used is not None:
            self.used.add(ev.opid)


class K:
    def __init__(self, nc, stack, needed=None):
        self.nc = nc
        self.stack = stack
        self.needed = needed
        self.used = set()
        self.pe = Eng("pe", nc.tensor, True)
        self.act = Eng("act", nc.scalar)
        self.dve = Eng("dve", nc.vector)
        self.pool = Eng("pool", nc.gpsimd)
        self.sp = Eng("sp", nc.sync)
        for e in (self.pe, self.act, self.dve, self.pool, self.sp):
            e.used = self.used
        self.nsem = 0
        self.out_events = []
        self.new_epoch()

    def sem(self, name):
        self.nsem += 1
        return self.stack.enter_context(self.nc.semaphore(f"{name}_{self.nsem}"))

    def new_epoch(self):
        for e in (self.pe, self.act, self.dve, self.pool):
            e.sem = self.sem("e" + e.name)
            e.key = id(e.sem)
            e.cnt = 0

    def _deps(self, eng, reads, writes, own_key):
        for b in reads:
            if b.w is not None:
                ev = b.w
                if ev.key == own_key and eng.is_pe:
                    continue
                if ev.key == own_key and own_key != eng.key:
                    continue
                eng.wait(ev)
            if b.psum:
                for ev in list(b.r.values()):
                    if ev.key != own_key:
                        eng.wait(ev)
        for b in writes:
            evs = list(b.r.values())
            if b.w is not None:
                evs.append(b.w)
            for ev in evs:
                if ev.key == own_key:
                    continue
                eng.wait(ev)

    def op(self, eng, fn, reads=(), writes=()):
        self._deps(eng, reads, writes, eng.key)
        ins = fn()
        eng.seq += 1
        opid = (eng.name, eng.seq)
        if self.needed is None or opid in self.needed:
            eng.cnt += 1
            ins.then_inc(eng.sem, 1)
        ev = Ev(eng.sem, eng.cnt, eng.key, opid)
        for b in writes:
            b.w = ev
            b.r = {}
        for b in reads:
            b.r[ev.key] = ev
        return ev

    def dma(self, q, out, in_, reads=(), writes=(), track=None, is_output=False):
        tb = track if track is not None else (writes[0] if writes else reads[0])
        if tb.dsem is None:
            tb.dsem = {}
        if q.name not in tb.dsem:
            s = self.sem("d" + tb.name + q.name)
            tb.dsem[q.name] = [s, id(s), 0]
        ds = tb.dsem[q.name]
        s, key = ds[0], ds[1]
        self._deps(q, reads, writes, key)
        ds[2] += 1
        q.h.dma_start(out=out, in_=in_).then_inc(s, 16)
        ev = Ev(s, 16 * ds[2], key)
        for b in writes:
            b.w = ev
            b.r = {}
        for b in reads:
            b.r[ev.key] = ev
        if is_output:
            self.out_events.append(ev)
        return ev


class Ring:
    def __init__(self, items):
        self.items = items
        self.i = 0

    def next(self):
        it = self.items[self.i % len(self.items)]
        self.i += 1
        return it


def _bucket_table():
    s = np.arange(384)
    rel = (127 - s).astype(np.int64)
    nb, max_exact = 16, 8
    n = np.abs(rel)
    nf = np.maximum(n, 1).astype(np.float32)
    large = max_exact + (np.log(nf / np.float32(max_exact)) / np.float32(np.log(128 / max_exact))
                         * np.float32(nb - max_exact)).astype(np.int32)
    large = np.minimum(large, nb - 1)
    b = np.where(rel > 0, nb, 0) + np.where(n < max_exact, n, large)
    oh = np.zeros((32, 384), np.float32)
    oh[b, s] = 1.0
    return oh


class _Stop(Exception):
    pass


def build_program(jobs=(0, 1, 2, 3), do_sample=True, heads=tuple(range(8)), stop=None, mini=False):
    _, used = _build(jobs, do_sample, heads, stop, None, mini)
    nc, _ = _build(jobs, do_sample, heads, stop, used, mini)
    return nc


def _build(jobs, do_sample, heads, stop, needed, mini=False):
    BLx = 1 if mini else BL
    PASTx = 128 if mini else PAST
    nc = bass.Bass("TRN2", target_bir_lowering=False)
    din = {}

    def inp(name, shape):
        din[name] = nc.dram_tensor(name, list(shape), F32, kind="ExternalInput").ap()
        return din[name]

    def outp(name, shape):
        return nc.dram_tensor(name, list(shape), F32, kind="ExternalOutput").ap()

    x_p = inp("x_prompt", (BLx, T, D))
    x_s = inp("x_sample", (BL * TS, D))
    ck = inp("cache_k", (BLx, PASTx, 1024))
    cv = inp("cache_v", (BLx, PASTx, 1024))
    st_conv = inp("state_conv", (BL, 3, 1024))
    st_lru = inp("state_lru", (BL, 1024))
    norm_gain = inp("norm_gain", (1, 1024))
    w_in = inp("w_in", (1024, 8192))
    conv_w = inp("conv_w", (4, 1024))
    conv_b = inp("conv_b", (1, 1024))
    w_rg = inp("w_rg", (8, 128, 128))
    b_rg = inp("b_rg", (1, 1024))
    w_ig = inp("w_ig", (8, 128, 128))
    b_ig = inp("b_ig", (1, 1024))
    lru_lambda = inp("lru_lambda", (1, 1024))
    q_gain = inp("q_norm_gain", (1, 64))
    k_gain = inp("k_norm_gain", (1, 64))
    rel_bias = inp("rel_bias", (32, 8))
    lam_q1 = inp("lam_q1", (1, 64))
    lam_k1 = inp("lam_k1", (1, 64))
    lam_q2 = inp("lam_q2", (1, 64))
    lam_k2 = inp("lam_k2", (1, 64))
    subln_gain = inp("subln_gain", (128, 1))
    w_pa = inp("w_proj_a", (1024, 1024))
    w_pb = inp("w_proj_b", (1024, 1024))
    w_out = inp("w_out", (1024, 1024))
    c_oh = inp("c_onehot", (32, 384))
    c_ident = inp("c_ident", (128, 128))
    c_bones = inp("c_blockones", (128, 128))

    y_p = outp("y_prompt", (BLx, T, D))
    y_s = outp("y_sample", (BL * TS, D))
    k_p = outp("k_prompt", (BLx, T, 1024))
    v_p = outp("v_prompt", (BLx, T, 1024))
    conv_p = outp("conv_prompt", (BL, 3, 1024))
    lru_p = outp("lru_prompt", (BL, 1024))
    k_s = outp("k_sample", (BL * TS, 1024))
    v_s = outp("v_sample", (BL * TS, 1024))
    conv_s = outp("conv_sample", (BL, 3, 1024))
    lru_s = outp("lru_sample", (BL, 1024))

    WF = nc.dram_tensor("WF", [48, 128, 8, 128], BF16, kind="Internal").ap()
    WKV = nc.dram_tensor("WKV", [8, 128, 8, 256], BF16, kind="Internal").ap()
    WP = nc.dram_tensor("WP", [2, 8, 128, 8, 128], BF16, kind="Internal").ap()
    WO = nc.dram_tensor("WO", [2, 128, 8, 512], BF16, kind="Internal").ap()
    REP_t = nc.dram_tensor("REP", [8, 128, 384], F32, kind="Internal")
    REP = REP_t.ap()

    with ExitStack() as stack:
        kb = K(nc, stack, needed)
        PE, ACT, DVE, POOL, SP = kb.pe, kb.act, kb.dve, kb.pool, kb.sp
        op, dma = kb.op, kb.dma
        stack.enter_context(nc.allow_non_contiguous_dma(reason="small param / state layout DMAs"))

        def sb(name, shape, dt):
            return stack.enter_context(nc.sbuf_tensor(name, list(shape), dt))

        XT = sb("XT", [128, 8, T], BF16)
        XTb = Buf("XT")
        YB = sb("YB", [128, 8, T], BF16)
        YBb = Buf("YB")
        QT = sb("QT", [128, 2, T], BF16)
        QTb = Buf("QT")
        KT = sb("KT", [128, PAST + 128], BF16)
        KTb = Buf("KT")
        NVS = 2
        VS = [sb(f"VS{i}", [128, 33, 130], BF16) for i in range(NVS)]
        VSb = [Buf(f"VS{i}") for i in range(NVS)]
        KC = [sb(f"KC{i}", [128, 32, 128], BF16) for i in range(NVS)]
        KCb = [Buf(f"KC{i}") for i in range(NVS)]
        ZB = sb("ZB", [128, T], BF16)
        ZBb = Buf("ZB")
        YA = sb("YA", [128, 8, 512], BF16)
        YAb = Buf("YA")
        MM = sb("MM", [128, 8, 512], BF16)
        MMb = Buf("MM")
        XAws = [sb(f"XAw{i}", [128, 640], F32) for i in range(2)]
        XAwbs = [Buf(f"XAw{i}") for i in range(2)]
        HALO = sb("HALO", [128, 8, 4, 3], F32)
        HALOb = Buf("HALO")
        HC = sb("HC", [128, 8, 4], F32)
        HCb = Buf("HC")
        NG = 10
        G = [sb(f"G{i}", [128, 512], F32) for i in range(NG)]
        Gb = [Buf(f"G{i}") for i in range(NG)]
        gfree = deque(range(NG))
        WR = [sb(f"WR{i}", [128, 4096], BF16) for i in range(2)]
        WRb = [Buf(f"WR{i}") for i in range(2)]
        wring = Ring([0, 1])
        ET = [sb(f"ET{i}", [128, 512], BF16) for i in range(3)]
        ETb = [Buf(f"ET{i}") for i in range(3)]
        ering = Ring([0, 1, 2])
        XS = [sb(f"XS{i}", [128, 1024], F32) for i in range(2)]
        XSb = [Buf(f"XS{i}") for i in range(2)]
        xsring = Ring([0, 1])
        XNB = [sb(f"XNB{i}", [128, 1024], BF16) for i in range(2)]
        XNBb = [Buf(f"XNB{i}") for i in range(2)]
        SM = [sb(f"SM{i}", [128, 8], F32) for i in range(24)]
        SMb = [Buf(f"SM{i}") for i in range(24)]
        smring = Ring(list(range(24)))
        g_row = sb("g_row", [128, 1024], F32)
        gk_rep = sb("gk_rep", [128, 128], F32)
        gq2 = sb("gq2", [128, 1], F32)
        sg8 = sb("sg8", [128, 1], F32)
        lamt = sb("lamt", [128, 4, 64], F32)
        lamc = sb("lamc", [128, 8], F32)
        c15 = sb("c15", [128, 8], F32)
        rb = sb("rb", [32, 8], F32)
        ohr = sb("ohr", [32, 384], F32)
        ident = sb("ident", [128, 128], BF16)
        bones = sb("bones", [128, 128], BF16)
        BDb = sb("BDb", [128, 8, 128], BF16)
        BSb = sb("BSb", [128, 8, 128], BF16)
        cw = sb("cw", [128, 8, 4], F32)
        cbias = sb("cbias", [128, 8], F32)
        brg = sb("brg", [128, 8], F32)
        big = sb("big", [128, 8], F32)
        lam_l = sb("lam_l", [128, 8], F32)
        nsp = sb("nsp", [128, 8], F32)
        WRG = sb("WRG", [128, 8, 128], BF16)
        WIG = sb("WIG", [128, 8, 128], BF16)
        SCV = sb("SCV", [128, 4, 8, 3], F32)
        LR0 = sb("LR0", [128, 4, 8], F32)
        CONVST = sb("CONVST", [128, 4, 8, 3], F32)
        CST = Buf("consts")
        CB = {}

        def cb(name):
            if name not in CB:
                CB[name] = Buf("c" + name)
            return CB[name]
        CONVSTb = Buf("CONVST")

        PS = [stack.enter_context(nc.psum_tensor(f"PS{i}", [128, 512], F32)) for i in range(8)]
        PSb = [Buf(f"PS{i}") for i in range(8)]
        ringA = Ring([0, 1, 2, 3])
        ringB = Ring([4, 5, 6, 7])
        ringAll = Ring([0, 1, 2, 3, 4, 5, 6, 7])

        def galloc():
            return gfree.popleft()

        def grel(*ids):
            for i in ids:
                gfree.append(i)

        if True:
            def body():
                wf_b = [Buf(f"WFd{g}") for g in range(6)]
                srcbase = [0, 1024, 2048, 5120, 6144, 7168]
                for g in range(6):
                    for cc in range(8):
                        c0 = srcbase[g] + cc * 128
                        dma(POOL, WF[g * 8 + cc], w_in[:, c0:c0 + 128].rearrange("(kc p) col -> p kc col", p=128),
                            writes=[wf_b[g]])
                wkv_b = Buf("WKVd")
                for h in range(8):
                    dma(POOL, WKV[h, :, :, 0:128],
                        w_in[:, 3072 + h * 128:3072 + (h + 1) * 128].rearrange("(kc p) c -> p kc c", p=128), writes=[wkv_b])
                    dma(POOL, WKV[h, :, :, 128:256],
                        w_in[:, 4096 + h * 128:4096 + (h + 1) * 128].rearrange("(kc p) c -> p kc c", p=128), writes=[wkv_b])
                wp_b = Buf("WPd")
                for j in range(8):
                    dma(POOL, WP[0, j], w_pa[:, j * 128:(j + 1) * 128].rearrange("(c p) col -> p c col", p=128), writes=[wp_b])
                    dma(POOL, WP[1, j], w_pb[:, j * 128:(j + 1) * 128].rearrange("(c p) col -> p c col", p=128), writes=[wp_b])
                wo_b = Buf("WOd")
                for n in range(2):
                    dma(POOL, WO[n], w_out[:, n * 512:(n + 1) * 512].rearrange("(j p) col -> p j col", p=128), writes=[wo_b])
                dma(POOL, WRG[:], w_rg.rearrange("n c d -> c n d"), writes=[cb("WRG")])
                dma(POOL, WIG[:], w_ig.rearrange("n c d -> c n d"), writes=[cb("WIG")])
                dma(POOL, ident[:], c_ident, writes=[cb("ident")])
                dma(POOL, bones[:], c_bones, writes=[cb("bones")])
                dma(SP, g_row[:], norm_gain.to_broadcast([128, 1024]), writes=[cb("g_row")])
                dma(SP, gk_rep[:, 0:64], k_gain.to_broadcast([128, 64]), writes=[cb("gk_rep")])
                dma(SP, gk_rep[:, 64:128], k_gain.to_broadcast([128, 64]), writes=[cb("gk_rep")])
                dma(SP, gq2[0:64, :], q_gain.rearrange("o d -> d o"), writes=[cb("gq2")])
                dma(SP, gq2[64:128, :], q_gain.rearrange("o d -> d o"), writes=[cb("gq2")])
                dma(SP, sg8[:], subln_gain, writes=[cb("sg8")])
                for i, lv in enumerate((lam_q1, lam_k1, lam_q2, lam_k2)):
                    dma(SP, lamt[:, i, :], lv.to_broadcast([128, 64]), writes=[cb("lam")])
                dma(SP, c15[:], rel_bias[15:16, :].to_broadcast([128, 8]), writes=[cb("c15")])
                dma(SP, rb[:], rel_bias, writes=[cb("rb")])
                dma(SP, ohr[:], c_oh, writes=[cb("ohr")])
                for j in range(4):
                    dma(SP, cw[:, :, j], conv_w[j].rearrange("(c p) -> p c", p=128), writes=[cb("cw")])
                dma(SP, cbias[:], conv_b.rearrange("o (c p) -> p (o c)", p=128), writes=[cb("cbias")])
                dma(SP, brg[:], b_rg.rearrange("o (c p) -> p (o c)", p=128), writes=[cb("brg")])
                dma(SP, big[:], b_ig.rearrange("o (c p) -> p (o c)", p=128), writes=[cb("big")])
                dma(SP, lam_l[:], lru_lambda.rearrange("o (c p) -> p (o c)", p=128), writes=[cb("nsp")])
                for b in range(BL):
                    for j in range(3):
                        dma(SP, SCV[:, b, :, j], st_conv[b, j].rearrange("(c p) -> p c", p=128), writes=[cb("SCV")])
                dma(SP, LR0[:], st_lru.rearrange("b (c p) -> p b c", p=128), writes=[cb("LR0")])

                op(DVE, lambda: nc.vector.tensor_tensor(out=lamt[:, 0, :], in0=lamt[:, 0, :], in1=lamt[:, 1, :], op=ALU.mult),
                   reads=[cb("lam")], writes=[cb("lam")])
                op(DVE, lambda: nc.vector.tensor_tensor(out=lamt[:, 2, :], in0=lamt[:, 2, :], in1=lamt[:, 3, :], op=ALU.mult),
                   reads=[cb("lam")], writes=[cb("lam")])
                op(DVE, lambda: nc.vector.tensor_reduce(out=lamc[:, 2:3], in_=lamt[:, 0, :], axis=AX.X, op=ALU.add),
                   reads=[cb("lam")], writes=[cb("lamc")])
                op(DVE, lambda: nc.vector.tensor_reduce(out=lamc[:, 3:4], in_=lamt[:, 2, :], axis=AX.X, op=ALU.add),
                   reads=[cb("lam")], writes=[cb("lamc")])
                op(ACT, lambda: nc.scalar.activation(out=lamc[:, 4:6], in_=lamc[:, 2:4], func=AF.Exp),
                   reads=[cb("lamc")], writes=[cb("lamc")])
                op(DVE, lambda: nc.vector.tensor_tensor(out=lamc[:, 6:7], in0=lamc[:, 4:5], in1=lamc[:, 5:6], op=ALU.subtract),
                   reads=[cb("lamc")], writes=[cb("lamc")])
                op(DVE, lambda: nc.vector.tensor_scalar(out=lamc[:, 0:1], in0=lamc[:, 6:7], scalar1=LAM_INIT, scalar2=None,
                                                        op0=ALU.add), reads=[cb("lamc")], writes=[cb("lamc")])
                op(DVE, lambda: nc.vector.tensor_scalar(out=lamc[:, 1:2], in0=lamc[:, 0:1], scalar1=-1.0, scalar2=None,
                                                        op0=ALU.mult), reads=[cb("lamc")], writes=[cb("lamc")])
                op(DVE, lambda: nc.vector.tensor_scalar(out=sg8[:], in0=sg8[:], scalar1=1.0 - LAM_INIT, scalar2=None,
                                                        op0=ALU.mult), reads=[cb("sg8")], writes=[cb("sg8")])
                op(ACT, lambda: nc.scalar.activation(out=nsp[:], in_=lam_l[:], func=AF.Exp, scale=-1.0),
                   reads=[cb("nsp")], writes=[cb("nsp")])
                op(ACT, lambda: nc.scalar.activation(out=nsp[:], in_=nsp[:], func=AF.Ln, bias=1.0),
                   reads=[cb("nsp")], writes=[cb("nsp")])
                op(DVE, lambda: nc.vector.tensor_scalar(out=nsp[:], in0=nsp[:], scalar1=-8.0, scalar2=None, op0=ALU.mult),
                   reads=[cb("nsp")], writes=[cb("nsp")])
                repb = Buf("REPd")
                grb = galloc()
                rbh = G[grb][0:32, :].bitcast(F32)
                for hh in range(2):
                    op(DVE, lambda: nc.vector.tensor_copy(out=rbh[:, 0:512].rearrange("p (h r) -> p h r", h=4),
                                                          in_=rb[:, hh * 4:(hh + 1) * 4].unsqueeze(2).to_broadcast([32, 4, 128])),
                       reads=[cb("rb")], writes=[Gb[grb]])
                    for h4 in range(4):
                        h = hh * 4 + h4
                        bk = ringA.next()
                        gi = galloc()
                        op(PE, lambda: nc.tensor.matmul(PS[bk][:, 0:384], lhsT=rbh[:, h4 * 128:(h4 + 1) * 128], rhs=ohr[:],
                                                        start=True, stop=True),
                           reads=[cb("ohr"), Gb[grb]], writes=[PSb[bk]])
                        op(DVE, lambda: nc.vector.tensor_scalar(out=G[gi][:, 0:384], in0=PS[bk][:, 0:384],
                                                                scalar1=c15[:, h:h + 1], scalar2=8.0,
                                                                op0=ALU.subtract, op1=ALU.mult),
                           reads=[PSb[bk], cb("c15")], writes=[Gb[gi]])
                        dma(SP, REP[h], G[gi][:, 0:384], reads=[Gb[gi]], writes=[repb], track=Gb[gi])
                        grel(gi)
                grel(grb)
                for hh in range(2):
                    gd, gs_ = galloc(), galloc()
                    bdv = G[gd][:].rearrange("p (h q) -> p h q", h=4)
                    bsv = G[gs_][:].rearrange("p (h q) -> p h q", h=4)
                    skd = bass.AP(tensor=REP_t, offset=hh * 4 * 128 * 384 + 127, ap=[[383, 128], [128 * 384, 4], [1, 128]])
                    sks = bass.AP(tensor=REP_t, offset=hh * 4 * 128 * 384 + 255, ap=[[383, 128], [128 * 384, 4], [1, 128]])
                    dma(SP, bdv, skd, reads=[repb], writes=[Gb[gd]])
                    dma(SP, bsv, sks, reads=[repb], writes=[Gb[gs_]])
                    op(POOL, lambda: nc.gpsimd.memset(bdv[64:128, :, 0:64], NEG), writes=[Gb[gd]])
                    op(DVE, lambda: nc.vector.tensor_copy(out=BDb[:, hh * 4:(hh + 1) * 4, :], in_=bdv), reads=[Gb[gd]], writes=[cb("BDb")])
                    op(DVE, lambda: nc.vector.tensor_copy(out=BSb[:, hh * 4:(hh + 1) * 4, :], in_=bsv), reads=[Gb[gs_]], writes=[cb("BSb")])
                    grel(gd, gs_)
                for i in range(NVS):
                    op(POOL, lambda: nc.gpsimd.memset(VS[i][:, :, 128:130], 1.0), writes=[VSb[i]])
                op(POOL, lambda: nc.gpsimd.memset(QT[:], 0.0), writes=[QTb])

                for e_ in (PE, ACT, DVE, POOL, SP):
                    for c_ in CB.values():
                        if c_.w is not None:
                            e_.wait(c_.w)
                for c_ in CB.values():
                    c_.w = None
                    c_.r = {}
                def chk(name):
                    if stop == name:
                        raise _Stop()

                def wload(parts):
                    wi = wring.next()
                    for (off, n, src, sbuf) in parts:
                        dma(SP, WR[wi][:, off:off + n], src, reads=[sbuf], writes=[WRb[wi]])
                    return wi

                def small():
                    i = smring.next()
                    return SM[i], SMb[i]

                def rstd_from_ss(ss_ap, ssb, n, inv_n, np_=128):
                    t1, t1b = small()
                    op(DVE, lambda: nc.vector.tensor_scalar(out=t1[0:np_, 0:n], in0=ss_ap, scalar1=inv_n, scalar2=EPS,
                                                            op0=ALU.mult, op1=ALU.add), reads=[ssb], writes=[t1b])
                    op(ACT, lambda: nc.scalar.activation(out=t1[0:np_, 0:n], in_=t1[0:np_, 0:n], func=AF.Sqrt),
                       reads=[t1b], writes=[t1b])
                    t2, t2b = small()
                    op(DVE, lambda: nc.vector.reciprocal(out=t2[0:np_, 0:n], in_=t1[0:np_, 0:n]), reads=[t1b], writes=[t2b])
                    return t2, t2b

                def run_job(sample, b_idx):
                    if not sample:
                        ntok = T
                        tiles = [(i * 512, 512) for i in range(4)]
                        subs = [(i * 128, 128) for i in range(16)]
                        x_src = x_p[b_idx]
                        k_dst, v_dst, y_dst = k_p[b_idx], v_p[b_idx], y_p[b_idx]
                        nseg, L = 1, 512
                    else:
                        ntok = BL * TS
                        tiles = [(0, 256)]
                        subs = [(i * 64, 64) for i in range(4)]
                        x_src = x_s
                        k_dst, v_dst, y_dst = k_s, v_s, y_s
                        nseg, L = 4, 64

                    if len(heads) < 8:
                        op(POOL, lambda: nc.gpsimd.memset(YB[:], 0.0), writes=[YBb])
                    if sample:
                        cache_prefetch(0)
                    for s0 in range(0, ntok, 128):
                        xi = xsring.next()
                        dma(SP, XS[xi][:], x_src[s0:s0 + 128, :], writes=[XSb[xi]])
                        ss, ssb = small()
                        op(ACT, lambda: nc.scalar.activation(out=XNB[xi][:], in_=XS[xi][:], func=AF.Square,
                                                             accum_out=ss[:, 0:1]),
                           reads=[XSb[xi]], writes=[XNBb[xi], ssb])
                        r, rb_ = rstd_from_ss(ss[:, 0:1], ssb, 1, 1.0 / D)
                        op(DVE, lambda: nc.vector.scalar_tensor_tensor(out=XNB[xi][:], in0=XS[xi][:], scalar=r[:, 0:1],
                                                                       in1=g_row[:], op0=ALU.mult, op1=ALU.mult),
                           reads=[XSb[xi], rb_, CST], writes=[XNBb[xi]])
                        bk = ringA.next()
                        pbf = PS[bk][:].bitcast(BF16)

                        def tr():
                            ins = None
                            for kc in range(8):
                                ins = nc.tensor.transpose(pbf[:, kc * 128:(kc + 1) * 128],
                                                          XNB[xi][:, kc * 128:(kc + 1) * 128], ident[:])
                            return ins
                        op(PE, tr, reads=[XNBb[xi], CST], writes=[PSb[bk]])
                        op(ACT, lambda: nc.scalar.copy(out=XT[:, :, s0:s0 + 128],
                                                       in_=pbf.rearrange("p (k t) -> p k t", k=8)),
                           reads=[PSb[bk]], writes=[XTb])

                    chk("s0")

                    def proj_fm(wi, woff, tcol, tw, bk):
                        def f():
                            ins = None
                            for kc in range(8):
                                ins = nc.tensor.matmul(PS[bk][:, 0:tw], lhsT=WR[wi][:, woff + kc * 128: woff + (kc + 1) * 128],
                                                       rhs=XT[:, kc, tcol:tcol + tw], start=(kc == 0), stop=(kc == 7))
                            return ins
                        op(PE, f, reads=[WRb[wi], XTb], writes=[PSb[bk]])

                    for h in heads:
                        wi = wload([(0, 1024, WF[16 + h].rearrange("p kc c -> p (kc c)"), wf_b[2]),
                                    (1024, 1024, WF[24 + h].rearrange("p kc c -> p (kc c)"), wf_b[3]),
                                    (2048, 2048, WKV[h].rearrange("p kc c -> p (kc c)"), wkv_b)])
                        for (tc0, tw) in tiles:
                            bk = ringA.next()
                            proj_fm(wi, 0, tc0, tw, bk)
                            gq, gs = galloc(), galloc()
                            op(DVE, lambda: nc.vector.tensor_copy(out=G[gq][:, 0:tw], in_=PS[bk][:, 0:tw]),
                               reads=[PSb[bk]], writes=[Gb[gq]])
                            sqb = G[gs][:].bitcast(BF16)
                            op(ACT, lambda: nc.scalar.activation(out=sqb[:, 0:tw], in_=PS[bk][:, 0:tw], func=AF.Square),
                               reads=[PSb[bk]], writes=[Gb[gs]])
                            bk2 = ringB.next()
                            op(PE, lambda: nc.tensor.matmul(PS[bk2][:, 0:tw], lhsT=bones[:], rhs=sqb[:, 0:tw],
                                                            start=True, stop=True),
                               reads=[Gb[gs], CST], writes=[PSb[bk2]])
                            gr = galloc()
                            op(ACT, lambda: nc.scalar.activation(out=G[gr][:, 0:tw], in_=PS[bk2][:, 0:tw], func=AF.Sqrt,
                                                                 scale=1.0 / 64, bias=EPS),
                               reads=[PSb[bk2]], writes=[Gb[gr]])
                            op(DVE, lambda: nc.vector.reciprocal(out=G[gr][:, 0:tw], in_=G[gr][:, 0:tw]),
                               reads=[Gb[gr]], writes=[Gb[gr]])
                            for m_ in range(2):
                                ps_ = slice(64 * m_, 64 * m_ + 64)
                                op(DVE, lambda: nc.vector.scalar_tensor_tensor(out=QT[ps_, m_, tc0:tc0 + tw],
                                                                               in0=G[gq][ps_, 0:tw],
                                                                               scalar=gq2[ps_, 0:1], in1=G[gr][ps_, 0:tw],
                                                                               op0=ALU.mult, op1=ALU.mult),
                                   reads=[Gb[gq], Gb[gr], CST], writes=[QTb])
                            grel(gq, gs, gr)
                            bk = ringA.next()
                            proj_fm(wi, 1024, tc0, tw, bk)
                            op(ACT, lambda: nc.scalar.activation(out=ZB[:, tc0:tc0 + tw], in_=PS[bk][:, 0:tw], func=AF.Silu),
                               reads=[PSb[bk]], writes=[ZBb])
                        chk("s1")
                        if not sample:
                            s2_prompt_kv(h, wi, subs, k_dst, v_dst)
                            chk("kv")
                            s2_prompt_attn(h)
                            chk("attn")
                        else:
                            s2_sample(h, wi, k_dst, v_dst)

                    for ti, (tc0, tw) in enumerate(tiles):
                        chk("heads")
                        s3(sample, ti, tc0, tw, nseg, L, len(tiles))
                        chk("s3")
                        s4(tc0, tw)
                        chk("s4")
                        s5(tc0, tw, x_src, y_dst)
                        chk("s5")
                    cdst = conv_s if sample else conv_p[b_idx:b_idx + 1]
                    ldst = lru_s if sample else lru_p[b_idx:b_idx + 1]
                    for sg in range(nseg):
                        for j in range(3):
                            dma(SP, cdst[sg, j].rearrange("(c p) -> p c", p=128), CONVST[:, sg, :, j],
                                reads=[CONVSTb], track=CONVSTb, is_output=True)
                    for sg in range(nseg):
                        dma(SP, ldst[sg].rearrange("(c p) -> p c", p=128), HC[:, :, sg],
                            reads=[HCb], track=HCb, is_output=True)

                def kv_batch(h, wi, items, vsi, k_dst, v_dst, pbf, trb):
                    n = len(items)
                    bks = []
                    for (s0, sn, kt_idx, pcol) in items:
                        bk = ringA.next()
                        bks.append(bk)

                        def f():
                            ins = None
                            for kc in range(8):
                                ins = nc.tensor.matmul(PS[bk][0:sn, 0:256], lhsT=XT[:, kc, s0:s0 + sn],
                                                       rhs=WR[wi][:, 2048 + kc * 256: 2048 + (kc + 1) * 256],
                                                       start=(kc == 0), stop=(kc == 7))
                            return ins
                        op(PE, f, reads=[WRb[wi], XTb], writes=[PSb[bk]])
                    gsq = [galloc() for _ in range(n)]
                    gkv = [galloc() for _ in range(n)]
                    sss = [small() for _ in range(n)]
                    t1s = [small() for _ in range(n)]
                    t2s = [small() for _ in range(n)]
                    for i, (s0, sn, kt_idx, pcol) in enumerate(items):
                        op(ACT, lambda: nc.scalar.activation(out=G[gsq[i]][0:sn, 0:128], in_=PS[bks[i]][0:sn, 0:128],
                                                             func=AF.Square), reads=[PSb[bks[i]]], writes=[Gb[gsq[i]]])
                    for i, (s0, sn, kt_idx, pcol) in enumerate(items):
                        op(DVE, lambda: nc.vector.tensor_reduce(out=sss[i][0][0:sn, 0:2],
                                                                in_=G[gsq[i]][0:sn, 0:128].rearrange("p (g d) -> p g d", g=2),
                                                                axis=AX.X, op=ALU.add),
                           reads=[Gb[gsq[i]]], writes=[sss[i][1]])
                    for i, (s0, sn, kt_idx, pcol) in enumerate(items):
                        op(DVE, lambda: nc.vector.tensor_scalar(out=t1s[i][0][0:sn, 0:2], in0=sss[i][0][0:sn, 0:2],
                                                                scalar1=1.0 / 64, scalar2=EPS, op0=ALU.mult, op1=ALU.add),
                           reads=[sss[i][1]], writes=[t1s[i][1]])
                    for i, (s0, sn, kt_idx, pcol) in enumerate(items):
                        op(ACT, lambda: nc.scalar.activation(out=t1s[i][0][0:sn, 0:2], in_=t1s[i][0][0:sn, 0:2], func=AF.Sqrt),
                           reads=[t1s[i][1]], writes=[t1s[i][1]])
                    for i, (s0, sn, kt_idx, pcol) in enumerate(items):
                        op(ACT, lambda: nc.scalar.copy(out=G[gkv[i]][0:sn, 128:256], in_=PS[bks[i]][0:sn, 128:256]),
                           reads=[PSb[bks[i]]], writes=[Gb[gkv[i]]])
                    for i, (s0, sn, kt_idx, pcol) in enumerate(items):
                        op(DVE, lambda: nc.vector.reciprocal(out=t2s[i][0][0:sn, 0:2], in_=t1s[i][0][0:sn, 0:2]),
                           reads=[t1s[i][1]], writes=[t2s[i][1]])
                    for i, (s0, sn, kt_idx, pcol) in enumerate(items):
                        op(DVE, lambda: nc.vector.tensor_tensor(out=G[gkv[i]][0:sn, 0:128].rearrange("p (g d) -> p g d", g=2),
                                                                in0=PS[bks[i]][0:sn, 0:128].rearrange("p (g d) -> p g d", g=2),
                                                                in1=t2s[i][0][0:sn, 0:2].unsqueeze(2).to_broadcast([sn, 2, 64]),
                                                                op=ALU.mult),
                           reads=[PSb[bks[i]], t2s[i][1]], writes=[Gb[gkv[i]]])
                    for i, (s0, sn, kt_idx, pcol) in enumerate(items):
                        op(DVE, lambda: nc.vector.tensor_tensor(out=G[gkv[i]][0:sn, 0:128], in0=G[gkv[i]][0:sn, 0:128],
                                                                in1=gk_rep[0:sn, :], op=ALU.mult),
                           reads=[Gb[gkv[i]], CST], writes=[Gb[gkv[i]]])
                    for i, (s0, sn, kt_idx, pcol) in enumerate(items):
                        dma(SP, k_dst[s0:s0 + sn, h * 128:(h + 1) * 128], G[gkv[i]][0:sn, 0:128], reads=[Gb[gkv[i]]],
                            track=Gb[gkv[i]], is_output=True)
                        dma(SP, v_dst[s0:s0 + sn, h * 128:(h + 1) * 128], G[gkv[i]][0:sn, 128:256], reads=[Gb[gkv[i]]],
                            track=Gb[gkv[i]], is_output=True)
                    for i, (s0, sn, kt_idx, pcol) in enumerate(items):
                        knb = G[gsq[i]][:].bitcast(BF16)
                        op(POOL, lambda: nc.gpsimd.tensor_copy(out=knb[0:sn, 0:128], in_=G[gkv[i]][0:sn, 0:128]),
                           reads=[Gb[gkv[i]]], writes=[Gb[gsq[i]]])
                        op(POOL, lambda: nc.gpsimd.tensor_copy(out=VS[vsi][0:sn, kt_idx, 0:128], in_=G[gkv[i]][0:sn, 128:256]),
                           reads=[Gb[gkv[i]]], writes=[VSb[vsi]])
                    for i, (s0, sn, kt_idx, pcol) in enumerate(items):
                        knb = G[gsq[i]][:].bitcast(BF16)
                        op(PE, lambda: nc.tensor.transpose(pbf[:, pcol:pcol + sn], knb[0:sn, 0:128], ident[0:sn, 0:sn]),
                           reads=[Gb[gsq[i]], CST], writes=[trb])
                    grel(*gsq)
                    grel(*gkv)

                def s2_prompt_kv(h, wi, subs, k_dst, v_dst):
                    for g4 in range(0, 16, 4):
                        bkt = ringB.next()
                        pbf = PS[bkt][:].bitcast(BF16)
                        items = [(subs[g4 + j][0], subs[g4 + j][1], g4 + j, j * 128) for j in range(4)]
                        kv_batch(h, wi, items, 0, k_dst, v_dst, pbf, PSb[bkt])
                        op(DVE, lambda: nc.vector.tensor_copy(out=KT[:, g4 * 128:g4 * 128 + 512], in_=pbf[:, 0:512]),
                           reads=[PSb[bkt]], writes=[KTb])

                def attn_post_a(h, obank, nq, ycol):
                    ov = PS[obank][0:nq, 0:258].rearrange("p (m d) -> p m d", m=2)
                    rz, rzb = small()
                    op(DVE, lambda: nc.vector.reciprocal(out=rz[0:nq, 0:2], in_=ov[:, :, 128]),
                       reads=[PSb[obank]], writes=[rzb])
                    op(DVE, lambda: nc.vector.tensor_tensor(out=rz[0:nq, 2:3], in0=rz[0:nq, 1:2], in1=lamc[0:nq, 1:2],
                                                            op=ALU.mult), reads=[rzb, CST], writes=[rzb])
                    go = galloc()
                    op(DVE, lambda: nc.vector.tensor_scalar(out=G[go][0:nq, 0:128], in0=ov[:, 0, 0:128],
                                                            scalar1=rz[0:nq, 0:1], scalar2=None, op0=ALU.mult),
                       reads=[PSb[obank], rzb], writes=[Gb[go]])
                    op(DVE, lambda: nc.vector.scalar_tensor_tensor(out=G[go][0:nq, 0:128], in0=ov[:, 1, 0:128],
                                                                   scalar=rz[0:nq, 2:3], in1=G[go][0:nq, 0:128],
                                                                   op0=ALU.mult, op1=ALU.add),
                       reads=[PSb[obank], rzb, Gb[go]], writes=[Gb[go]])
                    ss, ssb = small()
                    op(ACT, lambda: nc.scalar.activation(out=G[go][0:nq, 128:256], in_=G[go][0:nq, 0:128], func=AF.Square,
                                                         accum_out=ss[0:nq, 0:1]),
                       reads=[Gb[go]], writes=[Gb[go], ssb])
                    rs, rsb = rstd_from_ss(ss[0:nq, 0:1], ssb, 1, 1.0 / 128, np_=nq)
                    onb = G[go][:].bitcast(BF16)
                    op(ACT, lambda: nc.scalar.activation(out=onb[0:nq, 512:640], in_=G[go][0:nq, 0:128], func=AF.Copy,
                                                         scale=rs[0:nq, 0:1]),
                       reads=[Gb[go], rsb], writes=[Gb[go]])
                    return (h, go, nq, ycol)

                def attn_post_b(st):
                    h, go, nq, ycol = st
                    onb = G[go][:].bitcast(BF16)
                    bk = ringA.next()
                    pbf = PS[bk][:].bitcast(BF16)
                    op(PE, lambda: nc.tensor.transpose(pbf[:, 0:nq], onb[0:nq, 512:640], ident[0:nq, 0:nq]),
                       reads=[Gb[go], CST], writes=[PSb[bk]])
                    op(DVE, lambda: nc.vector.scalar_tensor_tensor(out=YB[:, h, ycol:ycol + nq], in0=pbf[:, 0:nq],
                                                                   scalar=sg8[:, 0:1], in1=ZB[:, ycol:ycol + nq],
                                                                   op0=ALU.mult, op1=ALU.mult),
                       reads=[PSb[bk], ZBb, CST], writes=[YBb])
                    grel(go)

                def attn_post(h, obank, nq, ycol):
                    attn_post_b(attn_post_a(h, obank, nq, ycol))

                def s2_prompt_attn(h):
                    pending = []
                    for t in range(8):
                        ob = [ringB.next(), ringB.next()]
                        nkt = 2 * t + 2

                        def sc_exp(kt):
                            c0 = 0 if kt <= 2 * t else 128
                            bk = ringA.next()
                            sv = PS[bk][:].rearrange("p (m q) -> p m q", m=2)

                            def sc():
                                blocks = [ql for ql in range(2) if 0 <= (2 * t + ql) - kt <= 1]
                                if c0 == 0:
                                    ins = nc.tensor.matmul(sv[:, :, 0:256], lhsT=KT[:, kt * 128:(kt + 1) * 128],
                                                           rhs=QT[:, :, t * 256:t * 256 + 256],
                                                           start=True, stop=(len(blocks) == 0), skip_group_check=True)
                                else:
                                    for m in range(2):
                                        ins = nc.tensor.matmul(sv[:, m, c0:256], lhsT=KT[:, kt * 128:(kt + 1) * 128],
                                                               rhs=QT[:, m, t * 256 + c0:t * 256 + 256],
                                                               start=(m == 0), stop=False, skip_group_check=True)
                                for bi, ql in enumerate(blocks):
                                    bt = BDb if (2 * t + ql) == kt else BSb
                                    for m in range(2):
                                        ins = nc.tensor.matmul(sv[:, m, ql * 128:(ql + 1) * 128], lhsT=ident[:],
                                                               rhs=bt[:, h, :], start=False,
                                                               stop=(bi == len(blocks) - 1 and m == 1),
                                                               skip_group_check=True)
                                return ins
                            op(PE, sc, reads=[KTb, QTb, CST], writes=[PSb[bk]])
                            ei = ering.next()
                            ev = ET[ei][:].rearrange("p (m q) -> p m q", m=2)
                            op(ACT, lambda: nc.scalar.activation(out=ev[:, :, c0:256], in_=sv[:, :, c0:256], func=AF.Exp,
                                                                 scale=0.125, bias=c15[:, h:h + 1]),
                               reads=[PSb[bk], CST], writes=[ETb[ei]])
                            return ei, ev

                        def pvf(kt, ei, ev):
                            def pv():
                                ins = None
                                for ql in range(2):
                                    if kt > 2 * t + ql:
                                        continue
                                    last = (kt == 2 * t + ql)
                                    for m in range(2):
                                        ins = nc.tensor.matmul(PS[ob[ql]][:, m * 129:(m + 1) * 129],
                                                               lhsT=ev[:, m, ql * 128:(ql + 1) * 128],
                                                               rhs=VS[0][:, kt, 0:129], start=(kt == 0 and m == 0), stop=last,
                                                               skip_group_check=True)
                                return ins
                            op(PE, pv, reads=[ETb[ei], VSb[0]], writes=[PSb[ob[0]], PSb[ob[1]]])

                        LA = 2
                        q_ = deque()
                        for kt in range(min(LA, nkt)):
                            q_.append(sc_exp(kt))
                        for kt in range(nkt):
                            if kt + LA < nkt:
                                q_.append(sc_exp(kt + LA))
                            if kt == 1:
                                for st_ in pending:
                                    attn_post_b(st_)
                                pending = []
                            pvf(kt, *q_.popleft())
                        for ql in range(2):
                            pending.append(attn_post_a(h, ob[ql], 128, t * 256 + ql * 128))
                    for st_ in pending:
                        attn_post_b(st_)

                cache_loaded = set()

                def cache_prefetch(un):
                    if un in cache_loaded or un >= len(heads) * BL:
                        return
                    cache_loaded.add(un)
                    h2, b2 = heads[un // BL], un % BL
                    v2 = (h2 * BL + b2) % NVS
                    dma(POOL, KC[v2][:], ck[b2, :, h2 * 128:(h2 + 1) * 128].rearrange("(kt p) c -> p kt c", p=128),
                        writes=[KCb[v2]])
                    dma(POOL, VS[v2][:, 0:32, 0:128],
                        cv[b2, :, h2 * 128:(h2 + 1) * 128].rearrange("(kt p) c -> p kt c", p=128), writes=[VSb[v2]])

                def s2_sample(h, wi, k_dst, v_dst):
                    for b in range(BL):
                        u = h * BL + b
                        vsi = u % NVS
                        kci = u % NVS
                        hi = list(heads).index(h)
                        uu = hi * BL + b
                        cache_prefetch(uu)
                        cache_prefetch(uu + 1)
                        for g8 in range(0, 32, 8):
                            bkt = ringB.next()
                            pbf = PS[bkt][:].bitcast(BF16)

                            def tr():
                                ins = None
                                for j in range(8):
                                    ins = nc.tensor.transpose(pbf[:, j * 128:(j + 1) * 128], KC[kci][:, g8 + j, :], ident[:])
                                return ins
                            op(PE, tr, reads=[KCb[kci], CST], writes=[PSb[bkt]])
                            op(DVE, lambda: nc.vector.tensor_copy(out=KT[:, g8 * 128:(g8 + 8) * 128], in_=pbf[:, 0:1024]),
                               reads=[PSb[bkt]], writes=[KTb])
                        bkt = ringB.next()
                        pbf = PS[bkt][:].bitcast(BF16)
                        kv_batch(h, wi, [(b * TS, TS, 32, 0)], vsi, k_dst, v_dst, pbf, PSb[bkt])
                        op(DVE, lambda: nc.vector.tensor_copy(out=KT[:, PAST:PAST + TS], in_=pbf[:, 0:TS]),
                           reads=[PSb[bkt]], writes=[KTb])
                        obank = ringB.next()
                        for g4 in range(0, 33, 4):
                            kts = list(range(g4, min(g4 + 4, 33)))
                            bk = ringA.next()
                            sv = PS[bk][:].rearrange("p (j m q) -> p j m q", j=4, m=2)

                            def sc():
                                ins = None
                                for j, kt in enumerate(kts):
                                    nk = 128 if kt < 32 else TS
                                    nb = kt >= 31
                                    ins = nc.tensor.matmul(sv[0:nk, j, :, :], lhsT=KT[:, kt * 128:kt * 128 + nk],
                                                           rhs=QT[:, :, b * TS:(b + 1) * TS],
                                                           start=True, stop=not nb, skip_group_check=True)
                                    for m in range(2):
                                        if kt == 31:
                                            ins = nc.tensor.matmul(sv[:, j, m, :], lhsT=ident[:], rhs=BSb[:, h, 0:TS],
                                                                   start=False, stop=(m == 1), skip_group_check=True)
                                        if kt == 32:
                                            ins = nc.tensor.matmul(sv[0:TS, j, m, :], lhsT=ident[0:TS, 0:TS],
                                                                   rhs=BDb[0:TS, h, 0:TS], start=False, stop=(m == 1),
                                                                   skip_group_check=True)
                                return ins
                            op(PE, sc, reads=[KTb, QTb, CST], writes=[PSb[bk]])
                            ei = ering.next()
                            ev = ET[ei][:].rearrange("p (j m q) -> p j m q", j=4, m=2)
                            nfull = len([kt for kt in kts if kt < 32])
                            if nfull:
                                op(ACT, lambda: nc.scalar.activation(out=ev[:, 0:nfull], in_=sv[:, 0:nfull], func=AF.Exp,
                                                                     scale=0.125, bias=c15[:, h:h + 1]),
                                   reads=[PSb[bk], CST], writes=[ETb[ei]])
                            if kts[-1] == 32:
                                j = len(kts) - 1
                                op(ACT, lambda: nc.scalar.activation(out=ev[0:TS, j], in_=sv[0:TS, j], func=AF.Exp,
                                                                     scale=0.125, bias=c15[0:TS, h:h + 1]),
                                   reads=[PSb[bk], CST], writes=[ETb[ei]])

                            def pv():
                                ins = None
                                for j, kt in enumerate(kts):
                                    nk = 128 if kt < 32 else TS
                                    for m in range(2):
                                        ins = nc.tensor.matmul(PS[obank][0:TS, m * 129:(m + 1) * 129],
                                                               lhsT=ev[0:nk, j, m, :], rhs=VS[vsi][0:nk, kt, 0:129],
                                                               start=(kt == 0 and m == 0), stop=(kt == 32), skip_group_check=True)
                                return ins
                            op(PE, pv, reads=[ETb[ei], VSb[vsi]], writes=[PSb[obank]])
                        attn_post(h, obank, TS, b * TS)

                def s3(sample, ti, tc0, tw, nseg, L, ntiles):
                    for c0_ in range(0, 8, 2):
                        cs = [c0_, c0_ + 1]
                        R = range(len(cs))
                        wis = [wload([(0, 1024, WF[c].rearrange("p kc c -> p (kc c)"), wf_b[0]),
                                      (1024, 1024, WF[8 + c].rearrange("p kc c -> p (kc c)"), wf_b[1])]) for c in cs]
                        xavs = [XAws[i][:, 0:nseg * (L + 3)].rearrange("p (s l) -> p s l", s=nseg) for i in R]
                        for i, c in enumerate(cs):
                            xav = xavs[i]
                            if sample:
                                op(DVE, lambda: nc.vector.tensor_copy(out=xav[:, :, 0:3], in_=SCV[:, :, c, :]),
                                   reads=[CST], writes=[XAwbs[i]])
                            elif ti == 0:
                                op(DVE, lambda: nc.vector.memset(xav[:, :, 0:3], 0.0), writes=[XAwbs[i]])
                            else:
                                op(DVE, lambda: nc.vector.tensor_copy(out=xav[:, :, 0:3], in_=HALO[:, c, 0:nseg, :]),
                                   reads=[HALOb], writes=[XAwbs[i]])
                        bkx = [ringAll.next() for _ in R]
                        for i, c in enumerate(cs):
                            proj_fm2(wis[i], 0, tc0, tw, bkx[i])
                        for i, c in enumerate(cs):
                            op(ACT, lambda: nc.scalar.copy(out=xavs[i][:, :, 3:3 + L],
                                                           in_=PS[bkx[i]][:, 0:tw].rearrange("p (s l) -> p s l", s=nseg)),
                               reads=[PSb[bkx[i]]], writes=[XAwbs[i]])
                        for i, c in enumerate(cs):
                            op(POOL, lambda: nc.gpsimd.tensor_copy(out=HALO[:, c, 0:nseg, :], in_=xavs[i][:, :, L:L + 3]),
                               reads=[XAwbs[i]], writes=[HALOb])
                            if sample or ti == ntiles - 1:
                                op(POOL, lambda: nc.gpsimd.tensor_copy(out=CONVST[:, 0:nseg, c, :], in_=xavs[i][:, :, L:L + 3]),
                                   reads=[XAwbs[i]], writes=[CONVSTb])
                        gx = [galloc() for _ in R]
                        xcvs = [G[gx[i]][:, 0:tw].rearrange("p (s l) -> p s l", s=nseg) for i in R]
                        for i, c in enumerate(cs):
                            op(DVE, lambda: nc.vector.tensor_scalar(out=xcvs[i], in0=xavs[i][:, :, 0:L], scalar1=cw[:, c, 0:1],
                                                                    scalar2=cbias[:, c:c + 1], op0=ALU.mult, op1=ALU.add),
                               reads=[XAwbs[i], CST], writes=[Gb[gx[i]]])
                        for j in range(1, 4):
                            for i, c in enumerate(cs):
                                op(DVE, lambda: nc.vector.scalar_tensor_tensor(out=xcvs[i], in0=xavs[i][:, :, j:j + L],
                                                                               scalar=cw[:, c, j:j + 1], in1=xcvs[i],
                                                                               op0=ALU.mult, op1=ALU.add),
                                   reads=[XAwbs[i], Gb[gx[i]], CST], writes=[Gb[gx[i]]])
                        gxb = [galloc() for _ in R]
                        xcbs = [G[gxb[i]][:].bitcast(BF16) for i in R]
                        for i, c in enumerate(cs):
                            op(ACT, lambda: nc.scalar.copy(out=xcbs[i][:, 0:tw], in_=G[gx[i]][:, 0:tw]),
                               reads=[Gb[gx[i]]], writes=[Gb[gxb[i]]])
                        bkr = [ringAll.next() for _ in R]
                        bki = [ringAll.next() for _ in R]
                        for i, c in enumerate(cs):
                            op(PE, lambda: nc.tensor.matmul(PS[bkr[i]][:, 0:tw], lhsT=WRG[:, c, :], rhs=xcbs[i][:, 0:tw],
                                                            start=True, stop=True), reads=[Gb[gxb[i]], CST], writes=[PSb[bkr[i]]])
                            op(PE, lambda: nc.tensor.matmul(PS[bki[i]][:, 0:tw], lhsT=WIG[:, c, :], rhs=xcbs[i][:, 0:tw],
                                                            start=True, stop=True), reads=[Gb[gxb[i]], CST], writes=[PSb[bki[i]]])
                        bkz = [ringAll.next() for _ in R]
                        for i, c in enumerate(cs):
                            proj_fm2(wis[i], 1024, tc0, tw, bkz[i])
                        gr_ = [galloc() for _ in R]
                        gi_ = [galloc() for _ in R]
                        ga_ = [galloc() for _ in R]
                        for i, c in enumerate(cs):
                            op(ACT, lambda: nc.scalar.activation(out=G[gr_[i]][:, 0:tw], in_=PS[bkr[i]][:, 0:tw], func=AF.Sigmoid,
                                                                 bias=brg[:, c:c + 1]), reads=[PSb[bkr[i]], CST], writes=[Gb[gr_[i]]])
                            op(ACT, lambda: nc.scalar.activation(out=G[gi_[i]][:, 0:tw], in_=PS[bki[i]][:, 0:tw], func=AF.Sigmoid,
                                                                 bias=big[:, c:c + 1]), reads=[PSb[bki[i]], CST], writes=[Gb[gi_[i]]])
                        for i, c in enumerate(cs):
                            op(ACT, lambda: nc.scalar.activation(out=G[ga_[i]][:, 0:tw], in_=G[gr_[i]][:, 0:tw], func=AF.Exp,
                                                                 scale=nsp[:, c:c + 1]), reads=[Gb[gr_[i]], CST], writes=[Gb[ga_[i]]])
                        for i, c in enumerate(cs):
                            op(DVE, lambda: nc.vector.tensor_tensor(out=G[gi_[i]][:, 0:tw], in0=G[gi_[i]][:, 0:tw],
                                                                    in1=G[gx[i]][:, 0:tw], op=ALU.mult),
                               reads=[Gb[gi_[i]], Gb[gx[i]]], writes=[Gb[gi_[i]]])
                        for i, c in enumerate(cs):
                            op(DVE, lambda: nc.vector.tensor_tensor(out=G[gr_[i]][:, 0:tw], in0=G[ga_[i]][:, 0:tw],
                                                                    in1=G[ga_[i]][:, 0:tw], op=ALU.mult),
                               reads=[Gb[ga_[i]]], writes=[Gb[gr_[i]]])
                        for i, c in enumerate(cs):
                            op(ACT, lambda: nc.scalar.activation(out=G[gr_[i]][:, 0:tw], in_=G[gr_[i]][:, 0:tw], func=AF.Sqrt,
                                                                 scale=-1.0, bias=1.0), reads=[Gb[gr_[i]]], writes=[Gb[gr_[i]]])
                        for i, c in enumerate(cs):
                            if (not sample) and ti == 0:
                                op(DVE, lambda: nc.vector.memset(G[gr_[i]][:, 0:1], 1.0), writes=[Gb[gr_[i]]])
                            op(DVE, lambda: nc.vector.tensor_tensor(out=G[gi_[i]][:, 0:tw], in0=G[gi_[i]][:, 0:tw],
                                                                    in1=G[gr_[i]][:, 0:tw], op=ALU.mult),
                               reads=[Gb[gi_[i]], Gb[gr_[i]]], writes=[Gb[gi_[i]]])
                        for i, c in enumerate(cs):
                            for sg in range(nseg):
                                if sample:
                                    init = LR0[:, sg, c:c + 1]
                                    rdi = [CST]
                                elif ti == 0:
                                    init = 0.0
                                    rdi = []
                                else:
                                    init = HC[:, c, sg:sg + 1]
                                    rdi = [HCb]
                                op(DVE, lambda: nc.vector.tensor_tensor_scan(out=G[gx[i]][:, sg * L:(sg + 1) * L],
                                                                             data0=G[ga_[i]][:, sg * L:(sg + 1) * L],
                                                                             data1=G[gi_[i]][:, sg * L:(sg + 1) * L],
                                                                             initial=init, op0=ALU.mult, op1=ALU.add),
                                   reads=[Gb[ga_[i]], Gb[gi_[i]]] + rdi, writes=[Gb[gx[i]]])
                        for i, c in enumerate(cs):
                            hv = G[gx[i]][:, 0:tw].rearrange("p (s l) -> p s l", s=nseg)
                            op(POOL, lambda: nc.gpsimd.tensor_copy(out=HC[:, c, 0:nseg], in_=hv[:, :, L - 1]),
                               reads=[Gb[gx[i]]], writes=[HCb])
                        for i, c in enumerate(cs):
                            op(ACT, lambda: nc.scalar.activation(out=G[ga_[i]][:, 0:tw], in_=PS[bkz[i]][:, 0:tw], func=AF.Silu),
                               reads=[PSb[bkz[i]]], writes=[Gb[ga_[i]]])
                        for i, c in enumerate(cs):
                            op(DVE, lambda: nc.vector.tensor_tensor(out=YA[:, c, 0:tw], in0=G[gx[i]][:, 0:tw],
                                                                    in1=G[ga_[i]][:, 0:tw], op=ALU.mult),
                               reads=[Gb[gx[i]], Gb[ga_[i]]], writes=[YAb])
                        grel(*gx)
                        grel(*gxb)
                        grel(*gr_)
                        grel(*gi_)
                        grel(*ga_)

                def proj_fm2(wi, woff, tcol, tw, bk):
                    def f():
                        ins = None
                        for kc in range(8):
                            ins = nc.tensor.matmul(PS[bk][:, 0:tw], lhsT=WR[wi][:, woff + kc * 128: woff + (kc + 1) * 128],
                                                   rhs=XT[:, kc, tcol:tcol + tw], start=(kc == 0), stop=(kc == 7))
                        return ins
                    op(PE, f, reads=[WRb[wi], XTb], writes=[PSb[bk]])

                def s4(tc0, tw):
                    for j in range(8):
                        wi = wload([(0, 1024, WP[0, j].rearrange("p c col -> p (c col)"), wp_b),
                                    (1024, 1024, WP[1, j].rearrange("p c col -> p (c col)"), wp_b),
                                    (2048, 1024, WF[32 + j].rearrange("p kc c -> p (kc c)"), wf_b[4]),
                                    (3072, 1024, WF[40 + j].rearrange("p kc c -> p (kc c)"), wf_b[5])])
                        bpa, bpb, bga, bgb = ringAll.next(), ringAll.next(), ringAll.next(), ringAll.next()

                        def mk(bk, woff, src, srcb):
                            def f():
                                ins = None
                                for cc in range(8):
                                    rhs = src[:, cc, 0:tw] if src is YA else src[:, cc, tc0:tc0 + tw]
                                    ins = nc.tensor.matmul(PS[bk][:, 0:tw],
                                                           lhsT=WR[wi][:, woff + cc * 128: woff + (cc + 1) * 128],
                                                           rhs=rhs, start=(cc == 0), stop=(cc == 7))
                                return ins
                            op(PE, f, reads=[WRb[wi], srcb], writes=[PSb[bk]])
                        mk(bga, 2048, XT, XTb)
                        mk(bgb, 3072, XT, XTb)
                        mk(bpa, 0, YA, YAb)
                        mk(bpb, 1024, YB, YBb)
                        g1, g2 = galloc(), galloc()
                        op(ACT, lambda: nc.scalar.activation(out=G[g1][:, 0:tw], in_=PS[bga][:, 0:tw], func=AF.Sigmoid),
                           reads=[PSb[bga]], writes=[Gb[g1]])
                        op(ACT, lambda: nc.scalar.activation(out=G[g2][:, 0:tw], in_=PS[bgb][:, 0:tw], func=AF.Sigmoid),
                           reads=[PSb[bgb]], writes=[Gb[g2]])
                        op(DVE, lambda: nc.vector.tensor_tensor(out=G[g1][:, 0:tw], in0=PS[bpa][:, 0:tw], in1=G[g1][:, 0:tw],
                                                                op=ALU.mult), reads=[PSb[bpa], Gb[g1]], writes=[Gb[g1]])
                        op(DVE, lambda: nc.vector.tensor_tensor(out=G[g2][:, 0:tw], in0=PS[bpb][:, 0:tw], in1=G[g2][:, 0:tw],
                                                                op=ALU.mult), reads=[PSb[bpb], Gb[g2]], writes=[Gb[g2]])
                        op(POOL, lambda: nc.gpsimd.tensor_tensor(out=MM[:, j, 0:tw], in0=G[g1][:, 0:tw], in1=G[g2][:, 0:tw],
                                                                 op=ALU.add), reads=[Gb[g1], Gb[g2]], writes=[MMb])
                        grel(g1, g2)

                def s5(tc0, tw, x_src, y_dst):
                    wis = [wload([(0, 4096, WO[n].rearrange("p j col -> p (j col)"), wo_b)]) for n in range(2)]
                    for s0 in range(0, tw, 128):
                        for n in range(2):
                            wi = wis[n]
                            bk = ringAll.next()

                            def f():
                                ins = None
                                for j in range(8):
                                    ins = nc.tensor.matmul(PS[bk][:, 0:512], lhsT=MM[:, j, s0:s0 + 128],
                                                           rhs=WR[wi][:, j * 512:(j + 1) * 512],
                                                           start=(j == 0), stop=(j == 7))
                                return ins
                            op(PE, f, reads=[WRb[wi], MMb], writes=[PSb[bk]])
                            gx_ = galloc()
                            r0 = tc0 + s0
                            dma(SP, G[gx_][:, :], x_src[r0:r0 + 128, n * 512:(n + 1) * 512], writes=[Gb[gx_]])
                            op(DVE, lambda: nc.vector.tensor_tensor(out=G[gx_][:, :], in0=PS[bk][:, 0:512], in1=G[gx_][:, :],
                                                                    op=ALU.add), reads=[PSb[bk], Gb[gx_]], writes=[Gb[gx_]])
                            dma(SP, y_dst[r0:r0 + 128, n * 512:(n + 1) * 512], G[gx_][:, :], reads=[Gb[gx_]],
                                track=Gb[gx_], is_output=True)
                            grel(gx_)

                try:
                    chk("pro")
                    for b in jobs:
                        run_job(False, b)
                        kb.new_epoch()
                    if do_sample:
                        run_job(True, None)
                except _Stop:
                    pass
                last = {}
                for ev in kb.out_events:
                    if ev.key not in last or last[ev.key].val < ev.val:
                        last[ev.key] = ev
                for ev in last.values():
                    SP.wait(ev)

            body()

    return nc, kb.used


_CONSTS = None


def _consts():
    global _CONSTS
    if _CONSTS is None:
        bo = np.zeros((128, 128), np.float32)
        bo[0:64, 0:64] = 1.0
        bo[64:128, 64:128] = 1.0
        _CONSTS = {"c_onehot": _bucket_table(), "c_ident": np.eye(128, dtype=np.float32), "c_blockones": bo}
    return _CONSTS


def kernel(**inputs):
    f = lambda a: np.ascontiguousarray(np.asarray(a, dtype=np.float32))
    x_prompt = f(inputs["x_prompt"])
    x_sample = f(inputs["x_sample"])
    cache_k = f(inputs["cache_k"])[0].reshape(32, PAST, 1024)
    cache_v = f(inputs["cache_v"])[0].reshape(32, PAST, 1024)
    state_conv = f(inputs["state_conv"])[0]
    state_lru = f(inputs["state_lru"])[0]
    shared = {
        "norm_gain": f(inputs["norm_gain"]), "w_in": f(inputs["w_in"])[0], "conv_w": f(inputs["conv_w"])[0],
        "conv_b": f(inputs["conv_b"]), "w_rg": f(inputs["w_rg"])[0], "b_rg": f(inputs["b_rg"]),
        "w_ig": f(inputs["w_ig"])[0], "b_ig": f(inputs["b_ig"]), "lru_lambda": f(inputs["lru_lambda"]),
        "q_norm_gain": f(inputs["q_norm_gain"]), "k_norm_gain": f(inputs["k_norm_gain"]),
        "rel_bias": f(inputs["rel_bias"]), "lam_q1": f(inputs["lam_q1"]), "lam_k1": f(inputs["lam_k1"]),
        "lam_q2": f(inputs["lam_q2"]), "lam_k2": f(inputs["lam_k2"]),
        "subln_gain": f(inputs["subln_gain"]).reshape(128, 1),
        "w_proj_a": f(inputs["w_proj_a"])[0], "w_proj_b": f(inputs["w_proj_b"])[0], "w_out": f(inputs["w_out"])[0],
    }
    shared.update(_consts())
    in_maps = []
    for c in range(NCORES):
        sl = slice(c * BL, (c + 1) * BL)
        m = dict(shared)
        m["x_prompt"] = x_prompt[sl]
        m["x_sample"] = x_sample[sl].reshape(BL * TS, D)
        m["cache_k"] = cache_k[sl]
        m["cache_v"] = cache_v[sl]
        m["state_conv"] = state_conv[sl]
        m["state_lru"] = state_lru[sl]
        in_maps.append(m)
    nc = build_program()
    res = run_bass_kernel_spmd(nc, in_maps, core_ids=list(range(NCORES)))
    R = res.results
    cat = lambda name: np.concatenate([np.asarray(r[name], dtype=np.float32) for r in R], axis=0)
    y_prompt = cat("y_prompt")
    y_sample = cat("y_sample").reshape(32, TS, D)
    k_prompt = cat("k_prompt").reshape(1, 32, T, 8, 128)
    v_prompt = cat("v_prompt").reshape(1, 32, T, 8, 128)
    conv_prompt = cat("conv_prompt").reshape(1, 32, 3, 1024)
    lru_prompt = cat("lru_prompt").reshape(1, 32, 1024)
    k_sample = cat("k_sample").reshape(1, 32, TS, 8, 128)
    v_sample = cat("v_sample").reshape(1, 32, TS, 8, 128)
    conv_sample = cat("conv_sample").reshape(1, 32, 3, 1024)
    lru_sample = cat("lru_sample").reshape(1, 32, 1024)
    return (y_prompt, y_sample, k_prompt, v_prompt, conv_prompt, lru_prompt, k_sample, v_sample, conv_sample, lru_sample)
```

```python
import numpy as np
from collections import deque
from contextlib import ExitStack
import concourse.bass as bass
import concourse.mybir as mybir
from concourse.bass_utils import run_bass_kernel_spmd

F32 = mybir.dt.float32
BF16 = mybir.dt.bfloat16
AF = mybir.ActivationFunctionType
ALU = mybir.AluOpType
AX = mybir.AxisListType

NCORES = 8
D = 1024
T = 2048
BL = 4
TS = 64
PAST = 4096
EPS = 1e-6
LAM_INIT = 0.2
NEG = -30000.0


class Ev:
    __slots__ = ("sem", "val", "key", "opid")

    def __init__(self, sem, val, key, opid=None):
        self.sem, self.val, self.key, self.opid = sem, val, key, opid


class Buf:
    def __init__(self, name):
        self.name = name
        self.w = None
        self.r = {}
        self.dsem = None
        self.psum = name.startswith("PS")


class Eng:
    def __init__(self, name, h, is_pe=False):
        self.name, self.h, self.is_pe = name, h, is_pe
        self.sem = None
        self.key = None
        self.cnt = 0
        self.seq = 0
        self.seen = {}
        self.used = None

    def wait(self, ev):
        if self.seen.get(ev.key, 0) >= ev.val:
            return
        self.h.wait_ge(ev.sem, ev.val)
        self.seen[ev.key] = ev.val
        if ev.opid is not None and self.used is not None:
            self.used.add(ev.opid)


class K:
    def __init__(self, nc, stack, needed=None):
        self.nc = nc
        self.stack = stack
        self.needed = needed
        self.used = set()
        self.pe = Eng("pe", nc.tensor, True)
        self.act = Eng("act", nc.scalar)
        self.dve = Eng("dve", nc.vector)
        self.pool = Eng("pool", nc.gpsimd)
        self.sp = Eng("sp", nc.sync)
        for e in (self.pe, self.act, self.dve, self.pool, self.sp):
            e.used = self.used
        self.nsem = 0
        self.out_events = []
        self.new_epoch()

    def sem(self, name):
        self.nsem += 1
        return self.stack.enter_context(self.nc.semaphore(f"{name}_{self.nsem}"))

    def new_epoch(self):
        for e in (self.pe, self.act, self.dve, self.pool):
            e.sem = self.sem("e" + e.name)
            e.key = id(e.sem)
            e.cnt = 0

    def _deps(self, eng, reads, writes, own_key):
        for b in reads:
            if b.w is not None:
                ev = b.w
                if ev.key == own_key and eng.is_pe:
                    continue
                if ev.key == own_key and own_key != eng.key:
                    continue
                eng.wait(ev)
            if b.psum:
                for ev in list(b.r.values()):
                    if ev.key != own_key:
                        eng.wait(ev)
        for b in writes:
            evs = list(b.r.values())
            if b.w is not None:
                evs.append(b.w)
            for ev in evs:
                if ev.key == own_key:
                    continue
                eng.wait(ev)

    def op(self, eng, fn, reads=(), writes=()):
        self._deps(eng, reads, writes, eng.key)
        ins = fn()
        eng.seq += 1
        opid = (eng.name, eng.seq)
        if self.needed is None or opid in self.needed:
            eng.cnt += 1
            ins.then_inc(eng.sem, 1)
        ev = Ev(eng.sem, eng.cnt, eng.key, opid)
        for b in writes:
            b.w = ev
            b.r = {}
        for b in reads:
            b.r[ev.key] = ev
        return ev

    def dma(self, q, out, in_, reads=(), writes=(), track=None, is_output=False):
        tb = track if track is not None else (writes[0] if writes else reads[0])
        if tb.dsem is None:
            tb.dsem = {}
        if q.name not in tb.dsem:
            s = self.sem("d" + tb.name + q.name)
            tb.dsem[q.name] = [s, id(s), 0]
        ds = tb.dsem[q.name]
        s, key = ds[0], ds[1]
        self._deps(q, reads, writes, key)
        ds[2] += 1
        q.h.dma_start(out=out, in_=in_).then_inc(s, 16)
        ev = Ev(s, 16 * ds[2], key)
        for b in writes:
            b.w = ev
            b.r = {}
        for b in reads:
            b.r[ev.key] = ev
        if is_output:
            self.out_events.append(ev)
        return ev


class Ring:
    def __init__(self, items):
        self.items = items
        self.i = 0

    def next(self):
        it = self.items[self.i % len(self.items)]
        self.i += 1
        return it


def _bucket_table():
    s = np.arange(384)
    rel = (127 - s).astype(np.int64)
    nb, max_exact = 16, 8
    n = np.abs(rel)
    nf = np.maximum(n, 1).astype(np.float32)
    large = max_exact + (np.log(nf / np.float32(max_exact)) / np.float32(np.log(128 / max_exact))
                         * np.float32(nb - max_exact)).astype(np.int32)
    large = np.minimum(large, nb - 1)
    b = np.where(rel > 0, nb, 0) + np.where(n < max_exact, n, large)
    oh = np.zeros((32, 384), np.float32)
    oh[b, s] = 1.0
    return oh


class _Stop(Exception):
    pass


def build_program(jobs=(0, 1, 2, 3), do_sample=True, heads=tuple(range(8)), stop=None, mini=False):
    _, used = _build(jobs, do_sample, heads, stop, None, mini)
    nc, _ = _build(jobs, do_sample, heads, stop, used, mini)
    return nc


def _build(jobs, do_sample, heads, stop, needed, mini=False):
    BLx = 1 if mini else BL
    PASTx = 128 if mini else PAST
    nc = bass.Bass("TRN2", target_bir_lowering=False)
    din = {}

    def inp(name, shape):
        din[name] = nc.dram_tensor(name, list(shape), F32, kind="ExternalInput").ap()
        return din[name]

    def outp(name, shape):
        return nc.dram_tensor(name, list(shape), F32, kind="ExternalOutput").ap()

    x_p = inp("x_prompt", (BLx, T, D))
    x_s = inp("x_sample", (BL * TS, D))
    ck = inp("cache_k", (BLx, PASTx, 1024))
    cv = inp("cache_v", (BLx, PASTx, 1024))
    st_conv = inp("state_conv", (BL, 3, 1024))
    st_lru = inp("state_lru", (BL, 1024))
    norm_gain = inp("norm_gain", (1, 1024))
    w_in = inp("w_in", (1024, 8192))
    conv_w = inp("conv_w", (4, 1024))
    conv_b = inp("conv_b", (1, 1024))
    w_rg = inp("w_rg", (8, 128, 128))
    b_rg = inp("b_rg", (1, 1024))
    w_ig = inp("w_ig", (8, 128, 128))
    b_ig = inp("b_ig", (1, 1024))
    lru_lambda = inp("lru_lambda", (1, 1024))
    q_gain = inp("q_norm_gain", (1, 64))
    k_gain = inp("k_norm_gain", (1, 64))
    rel_bias = inp("rel_bias", (32, 8))
    lam_q1 = inp("lam_q1", (1, 64))
    lam_k1 = inp("lam_k1", (1, 64))
    lam_q2 = inp("lam_q2", (1, 64))
    lam_k2 = inp("lam_k2", (1, 64))
    subln_gain = inp("subln_gain", (128, 1))
    w_pa = inp("w_proj_a", (1024, 1024))
    w_pb = inp("w_proj_b", (1024, 1024))
    w_out = inp("w_out", (1024, 1024))
    c_oh = inp("c_onehot", (32, 384))
    c_ident = inp("c_ident", (128, 128))
    c_bones = inp("c_blockones", (128, 128))

    y_p = outp("y_prompt", (BLx, T, D))
    y_s = outp("y_sample", (BL * TS, D))
    k_p = outp("k_prompt", (BLx, T, 1024))
    v_p = outp("v_prompt", (BLx, T, 1024))
    conv_p = outp("conv_prompt", (BL, 3, 1024))
    lru_p = outp("lru_prompt", (BL, 1024))
    k_s = outp("k_sample", (BL * TS, 1024))
    v_s = outp("v_sample", (BL * TS, 1024))
    conv_s = outp("conv_sample", (BL, 3, 1024))
    lru_s = outp("lru_sample", (BL, 1024))

    WF = nc.dram_tensor("WF", [48, 128, 8, 128], BF16, kind="Internal").ap()
    WKV = nc.dram_tensor("WKV", [8, 128, 8, 256], BF16, kind="Internal").ap()
    WP = nc.dram_tensor("WP", [2, 8, 128, 8, 128], BF16, kind="Internal").ap()
    WO = nc.dram_tensor("WO", [2, 128, 8, 512], BF16, kind="Internal").ap()
    REP_t = nc.dram_tensor("REP", [8, 128, 384], F32, kind="Internal")
    REP = REP_t.ap()

    with ExitStack() as stack:
        kb = K(nc, stack, needed)
        PE, ACT, DVE, POOL, SP = kb.pe, kb.act, kb.dve, kb.pool, kb.sp
        op, dma = kb.op, kb.dma
        stack.enter_context(nc.allow_non_contiguous_dma(reason="small param / state layout DMAs"))

        def sb(name, shape, dt):
            return stack.enter_context(nc.sbuf_tensor(name, list(shape), dt))

        XT = sb("XT", [128, 8, T], BF16)
        XTb = Buf("XT")
        YB = sb("YB", [128, 8, T], BF16)
        YBb = Buf("YB")
        QT = sb("QT", [128, 2, T], BF16)
        QTb = Buf("QT")
        KT = sb("KT", [128, PAST + 128], BF16)
        KTb = Buf("KT")
        NVS = 2
        VS = [sb(f"VS{i}", [128, 33, 130], BF16) for i in range(NVS)]
        VSb = [Buf(f"VS{i}") for i in range(NVS)]
        KC = [sb(f"KC{i}", [128, 32, 128], BF16) for i in range(NVS)]
        KCb = [Buf(f"KC{i}") for i in range(NVS)]
        ZB = sb("ZB", [128, T], BF16)
        ZBb = Buf("ZB")
        YA = sb("YA", [128, 8, 512], BF16)
        YAb = Buf("YA")
        MM = sb("MM", [128, 8, 512], BF16)
        MMb = Buf("MM")
        XAws = [sb(f"XAw{i}", [128, 640], F32) for i in range(2)]
        XAwbs = [Buf(f"XAw{i}") for i in range(2)]
        HALO = sb("HALO", [128, 8, 4, 3], F32)
        HALOb = Buf("HALO")
        HC = sb("HC", [128, 8, 4], F32)
        HCb = Buf("HC")
        NG = 10
        G = [sb(f"G{i}", [128, 512], F32) for i in range(NG)]
        Gb = [Buf(f"G{i}") for i in range(NG)]
        gfree = deque(range(NG))
        WR = [sb(f"WR{i}", [128, 4096], BF16) for i in range(2)]
        WRb = [Buf(f"WR{i}") for i in range(2)]
        wring = Ring([0, 1])
        ET = [sb(f"ET{i}", [128, 512], BF16) for i in range(3)]
        ETb = [Buf(f"ET{i}") for i in range(3)]
        ering = Ring([0, 1, 2])
        XS = [sb(f"XS{i}", [128, 1024], F32) for i in range(2)]
        XSb = [Buf(f"XS{i}") for i in range(2)]
        xsring = Ring([0, 1])
        XNB = [sb(f"XNB{i}", [128, 1024], BF16) for i in range(2)]
        XNBb = [Buf(f"XNB{i}") for i in range(2)]
        SM = [sb(f"SM{i}", [128, 8], F32) for i in range(24)]
        SMb = [Buf(f"SM{i}") for i in range(24)]
        smring = Ring(list(range(24)))
        g_row = sb("g_row", [128, 1024], F32)
        gk_rep = sb("gk_rep", [128, 128], F32)
        gq2 = sb("gq2", [128, 1], F32)
        sg8 = sb("sg8", [128, 1], F32)
        lamt = sb("lamt", [128, 4, 64], F32)
        lamc = sb("lamc", [128, 8], F32)
        c15 = sb("c15", [128, 8], F32)
        rb = sb("rb", [32, 8], F32)
        ohr = sb("ohr", [32, 384], F32)
        ident = sb("ident", [128, 128], BF16)
        bones = sb("bones", [128, 128], BF16)
        BDb = sb("BDb", [128, 8, 128], BF16)
        BSb = sb("BSb", [128, 8, 128], BF16)
        cw = sb("cw", [128, 8, 4], F32)
        cbias = sb("cbias", [128, 8], F32)
        brg = sb("brg", [128, 8], F32)
        big = sb("big", [128, 8], F32)
        lam_l = sb("lam_l", [128, 8], F32)
        nsp = sb("nsp", [128, 8], F32)
        WRG = sb("WRG", [128, 8, 128], BF16)
        WIG = sb("WIG", [128, 8, 128], BF16)
        SCV = sb("SCV", [128, 4, 8, 3], F32)
        LR0 = sb("LR0", [128, 4, 8], F32)
        CONVST = sb("CONVST", [128, 4, 8, 3], F32)
        CST = Buf("consts")
        CB = {}

        def cb(name):
            if name not in CB:
                CB[name] = Buf("c" + name)
            return CB[name]
        CONVSTb = Buf("CONVST")

        PS = [stack.enter_context(nc.psum_tensor(f"PS{i}", [128, 512], F32)) for i in range(8)]
        PSb = [Buf(f"PS{i}") for i in range(8)]
        ringA = Ring([0, 1, 2, 3])
        ringB = Ring([4, 5, 6, 7])
        ringAll = Ring([0, 1, 2, 3, 4, 5, 6, 7])

        def galloc():
            return gfree.popleft()

        def grel(*ids):
            for i in ids:
                gfree.append(i)

        if True:
            def body():
                wf_b = [Buf(f"WFd{g}") for g in range(6)]
                srcbase = [0, 1024, 2048, 5120, 6144, 7168]
                def conv_wf(g):
                    for cc in range(8):
                        c0 = srcbase[g] + cc * 128
                        dma(POOL, WF[g * 8 + cc], w_in[:, c0:c0 + 128].rearrange("(kc p) col -> p kc col", p=128),
                            writes=[wf_b[g]])
                conv_wf(2)
                conv_wf(3)
                wkv_b = Buf("WKVd")
                for h in range(8):
                    dma(POOL, WKV[h, :, :, 0:128],
                        w_in[:, 3072 + h * 128:3072 + (h + 1) * 128].rearrange("(kc p) c -> p kc c", p=128), writes=[wkv_b])
                    dma(POOL, WKV[h, :, :, 128:256],
                        w_in[:, 4096 + h * 128:4096 + (h + 1) * 128].rearrange("(kc p) c -> p kc c", p=128), writes=[wkv_b])
                for g in (0, 1, 4, 5):
                    conv_wf(g)
                wp_b = Buf("WPd")
                for j in range(8):
                    dma(POOL, WP[0, j], w_pa[:, j * 128:(j + 1) * 128].rearrange("(c p) col -> p c col", p=128), writes=[wp_b])
                    dma(POOL, WP[1, j], w_pb[:, j * 128:(j + 1) * 128].rearrange("(c p) col -> p c col", p=128), writes=[wp_b])
                wo_b = Buf("WOd")
                for n in range(2):
                    dma(POOL, WO[n], w_out[:, n * 512:(n + 1) * 512].rearrange("(j p) col -> p j col", p=128), writes=[wo_b])
                dma(POOL, WRG[:], w_rg.rearrange("n c d -> c n d"), writes=[cb("WRG")])
                dma(POOL, WIG[:], w_ig.rearrange("n c d -> c n d"), writes=[cb("WIG")])
                dma(POOL, ident[:], c_ident, writes=[cb("ident")])
                dma(POOL, bones[:], c_bones, writes=[cb("bones")])
                dma(SP, g_row[:], norm_gain.to_broadcast([128, 1024]), writes=[cb("g_row")])
                dma(SP, gk_rep[:, 0:64], k_gain.to_broadcast([128, 64]), writes=[cb("gk_rep")])
                dma(SP, gk_rep[:, 64:128], k_gain.to_broadcast([128, 64]), writes=[cb("gk_rep")])
                dma(SP, gq2[0:64, :], q_gain.rearrange("o d -> d o"), writes=[cb("gq2")])
                dma(SP, gq2[64:128, :], q_gain.rearrange("o d -> d o"), writes=[cb("gq2")])
                dma(SP, sg8[:], subln_gain, writes=[cb("sg8")])
                for i, lv in enumerate((lam_q1, lam_k1, lam_q2, lam_k2)):
                    dma(SP, lamt[:, i, :], lv.to_broadcast([128, 64]), writes=[cb("lam")])
                dma(SP, c15[:], rel_bias[15:16, :].to_broadcast([128, 8]), writes=[cb("c15")])
                dma(SP, rb[:], rel_bias, writes=[cb("rb")])
                dma(SP, ohr[:], c_oh, writes=[cb("ohr")])
                for j in range(4):
                    dma(SP, cw[:, :, j], conv_w[j].rearrange("(c p) -> p c", p=128), writes=[cb("cw")])
                dma(SP, cbias[:], conv_b.rearrange("o (c p) -> p (o c)", p=128), writes=[cb("cbias")])
                dma(SP, brg[:], b_rg.rearrange("o (c p) -> p (o c)", p=128), writes=[cb("brg")])
                dma(SP, big[:], b_ig.rearrange("o (c p) -> p (o c)", p=128), writes=[cb("big")])
                dma(SP, lam_l[:], lru_lambda.rearrange("o (c p) -> p (o c)", p=128), writes=[cb("nsp")])
                for b in range(BL):
                    for j in range(3):
                        dma(SP, SCV[:, b, :, j], st_conv[b, j].rearrange("(c p) -> p c", p=128), writes=[cb("SCV")])
                dma(SP, LR0[:], st_lru.rearrange("b (c p) -> p b c", p=128), writes=[cb("LR0")])

                op(DVE, lambda: nc.vector.tensor_tensor(out=lamt[:, 0, :], in0=lamt[:, 0, :], in1=lamt[:, 1, :], op=ALU.mult),
                   reads=[cb("lam")], writes=[cb("lam")])
                op(DVE, lambda: nc.vector.tensor_tensor(out=lamt[:, 2, :], in0=lamt[:, 2, :], in1=lamt[:, 3, :], op=ALU.mult),
                   reads=[cb("lam")], writes=[cb("lam")])
                op(DVE, lambda: nc.vector.tensor_reduce(out=lamc[:, 2:3], in_=lamt[:, 0, :], axis=AX.X, op=ALU.add),
                   reads=[cb("lam")], writes=[cb("lamc")])
                op(DVE, lambda: nc.vector.tensor_reduce(out=lamc[:, 3:4], in_=lamt[:, 2, :], axis=AX.X, op=ALU.add),
                   reads=[cb("lam")], writes=[cb("lamc")])
                op(ACT, lambda: nc.scalar.activation(out=lamc[:, 4:6], in_=lamc[:, 2:4], func=AF.Exp),
                   reads=[cb("lamc")], writes=[cb("lamc")])
                op(DVE, lambda: nc.vector.tensor_tensor(out=lamc[:, 6:7], in0=lamc[:, 4:5], in1=lamc[:, 5:6], op=ALU.subtract),
                   reads=[cb("lamc")], writes=[cb("lamc")])
                op(DVE, lambda: nc.vector.tensor_scalar(out=lamc[:, 0:1], in0=lamc[:, 6:7], scalar1=LAM_INIT, scalar2=None,
                                                        op0=ALU.add), reads=[cb("lamc")], writes=[cb("lamc")])
                op(DVE, lambda: nc.vector.tensor_scalar(out=lamc[:, 1:2], in0=lamc[:, 0:1], scalar1=-1.0, scalar2=None,
                                                        op0=ALU.mult), reads=[cb("lamc")], writes=[cb("lamc")])
                op(DVE, lambda: nc.vector.tensor_scalar(out=sg8[:], in0=sg8[:], scalar1=1.0 - LAM_INIT, scalar2=None,
                                                        op0=ALU.mult), reads=[cb("sg8")], writes=[cb("sg8")])
                op(ACT, lambda: nc.scalar.activation(out=nsp[:], in_=lam_l[:], func=AF.Exp, scale=-1.0),
                   reads=[cb("nsp")], writes=[cb("nsp")])
                op(ACT, lambda: nc.scalar.activation(out=nsp[:], in_=nsp[:], func=AF.Ln, bias=1.0),
                   reads=[cb("nsp")], writes=[cb("nsp")])
                op(DVE, lambda: nc.vector.tensor_scalar(out=nsp[:], in0=nsp[:], scalar1=-8.0, scalar2=None, op0=ALU.mult),
                   reads=[cb("nsp")], writes=[cb("nsp")])
                repb = Buf("REPd")
                grb = galloc()
                rbh = G[grb][0:32, :].bitcast(F32)
                for hh in range(2):
                    op(DVE, lambda: nc.vector.tensor_copy(out=rbh[:, 0:512].rearrange("p (h r) -> p h r", h=4),
                                                          in_=rb[:, hh * 4:(hh + 1) * 4].unsqueeze(2).to_broadcast([32, 4, 128])),
                       reads=[cb("rb")], writes=[Gb[grb]])
                    for h4 in range(4):
                        h = hh * 4 + h4
                        bk = ringA.next()
                        gi = galloc()
                        op(PE, lambda: nc.tensor.matmul(PS[bk][:, 0:384], lhsT=rbh[:, h4 * 128:(h4 + 1) * 128], rhs=ohr[:],
                                                        start=True, stop=True),
                           reads=[cb("ohr"), Gb[grb]], writes=[PSb[bk]])
                        op(DVE, lambda: nc.vector.tensor_scalar(out=G[gi][:, 0:384], in0=PS[bk][:, 0:384],
                                                                scalar1=c15[:, h:h + 1], scalar2=8.0,
                                                                op0=ALU.subtract, op1=ALU.mult),
                           reads=[PSb[bk], cb("c15")], writes=[Gb[gi]])
                        dma(SP, REP[h], G[gi][:, 0:384], reads=[Gb[gi]], writes=[repb], track=Gb[gi])
                        grel(gi)
                grel(grb)
                for hh in range(2):
                    gd, gs_ = galloc(), galloc()
                    bdv = G[gd][:].rearrange("p (h q) -> p h q", h=4)
                    bsv = G[gs_][:].rearrange("p (h q) -> p h q", h=4)
                    skd = bass.AP(tensor=REP_t, offset=hh * 4 * 128 * 384 + 127, ap=[[383, 128], [128 * 384, 4], [1, 128]])
                    sks = bass.AP(tensor=REP_t, offset=hh * 4 * 128 * 384 + 255, ap=[[383, 128], [128 * 384, 4], [1, 128]])
                    dma(SP, bdv, skd, reads=[repb], writes=[Gb[gd]])
                    dma(SP, bsv, sks, reads=[repb], writes=[Gb[gs_]])
                    op(POOL, lambda: nc.gpsimd.memset(bdv[64:128, :, 0:64], NEG), writes=[Gb[gd]])
                    op(DVE, lambda: nc.vector.tensor_copy(out=BDb[:, hh * 4:(hh + 1) * 4, :], in_=bdv), reads=[Gb[gd]], writes=[cb("BDb")])
                    op(DVE, lambda: nc.vector.tensor_copy(out=BSb[:, hh * 4:(hh + 1) * 4, :], in_=bsv), reads=[Gb[gs_]], writes=[cb("BSb")])
                    grel(gd, gs_)
                for i in range(NVS):
                    op(POOL, lambda: nc.gpsimd.memset(VS[i][:, :, 128:130], 1.0), writes=[VSb[i]])
                op(POOL, lambda: nc.gpsimd.memset(QT[:], 0.0), writes=[QTb])

                for e_ in (PE, ACT, DVE, POOL, SP):
                    for c_ in CB.values():
                        if c_.w is not None:
                            e_.wait(c_.w)
                for c_ in CB.values():
                    c_.w = None
                    c_.r = {}
                def chk(name):
                    if stop == name:
                        raise _Stop()

                def wload(parts):
                    wi = wring.next()
                    for (off, n, src, sbuf) in parts:
                        dma(SP, WR[wi][:, off:off + n], src, reads=[sbuf], writes=[WRb[wi]])
                    return wi

                def small():
                    i = smring.next()
                    return SM[i], SMb[i]

                def rstd_from_ss(ss_ap, ssb, n, inv_n, np_=128):
                    t1, t1b = small()
                    op(DVE, lambda: nc.vector.tensor_scalar(out=t1[0:np_, 0:n], in0=ss_ap, scalar1=inv_n, scalar2=EPS,
                                                            op0=ALU.mult, op1=ALU.add), reads=[ssb], writes=[t1b])
                    op(ACT, lambda: nc.scalar.activation(out=t1[0:np_, 0:n], in_=t1[0:np_, 0:n], func=AF.Sqrt),
                       reads=[t1b], writes=[t1b])
                    t2, t2b = small()
                    op(DVE, lambda: nc.vector.reciprocal(out=t2[0:np_, 0:n], in_=t1[0:np_, 0:n]), reads=[t1b], writes=[t2b])
                    return t2, t2b

                def run_job(sample, b_idx):
                    if not sample:
                        ntok = T
                        tiles = [(i * 512, 512) for i in range(4)]
                        subs = [(i * 128, 128) for i in range(16)]
                        x_src = x_p[b_idx]
                        k_dst, v_dst, y_dst = k_p[b_idx], v_p[b_idx], y_p[b_idx]
                        nseg, L = 1, 512
                    else:
                        ntok = BL * TS
                        tiles = [(0, 256)]
                        subs = [(i * 64, 64) for i in range(4)]
                        x_src = x_s
                        k_dst, v_dst, y_dst = k_s, v_s, y_s
                        nseg, L = 4, 64

                    if len(heads) < 8:
                        op(POOL, lambda: nc.gpsimd.memset(YB[:], 0.0), writes=[YBb])
                    if sample:
                        cache_prefetch(0)
                    for s0 in range(0, ntok, 128):
                        xi = xsring.next()
                        dma(SP, XS[xi][:], x_src[s0:s0 + 128, :], writes=[XSb[xi]])
                        ss, ssb = small()
                        op(ACT, lambda: nc.scalar.activation(out=XNB[xi][:], in_=XS[xi][:], func=AF.Square,
                                                             accum_out=ss[:, 0:1]),
                           reads=[XSb[xi]], writes=[XNBb[xi], ssb])
                        r, rb_ = rstd_from_ss(ss[:, 0:1], ssb, 1, 1.0 / D)
                        op(DVE, lambda: nc.vector.scalar_tensor_tensor(out=XNB[xi][:], in0=XS[xi][:], scalar=r[:, 0:1],
                                                                       in1=g_row[:], op0=ALU.mult, op1=ALU.mult),
                           reads=[XSb[xi], rb_, CST], writes=[XNBb[xi]])
                        bk = ringA.next()
                        pbf = PS[bk][:].bitcast(BF16)

                        def tr():
                            ins = None
                            for kc in range(8):
                                ins = nc.tensor.transpose(pbf[:, kc * 128:(kc + 1) * 128],
                                                          XNB[xi][:, kc * 128:(kc + 1) * 128], ident[:])
                            return ins
                        op(PE, tr, reads=[XNBb[xi], CST], writes=[PSb[bk]])
                        op(ACT, lambda: nc.scalar.copy(out=XT[:, :, s0:s0 + 128],
                                                       in_=pbf.rearrange("p (k t) -> p k t", k=8)),
                           reads=[PSb[bk]], writes=[XTb])

                    chk("s0")

                    def proj_fm(wi, woff, tcol, tw, bk):
                        def f():
                            ins = None
                            for kc in range(8):
                                ins = nc.tensor.matmul(PS[bk][:, 0:tw], lhsT=WR[wi][:, woff + kc * 128: woff + (kc + 1) * 128],
                                                       rhs=XT[:, kc, tcol:tcol + tw], start=(kc == 0), stop=(kc == 7))
                            return ins
                        op(PE, f, reads=[WRb[wi], XTb], writes=[PSb[bk]])

                    for h in heads:
                        wi = wload([(0, 1024, WF[16 + h].rearrange("p kc c -> p (kc c)"), wf_b[2]),
                                    (1024, 1024, WF[24 + h].rearrange("p kc c -> p (kc c)"), wf_b[3]),
                                    (2048, 2048, WKV[h].rearrange("p kc c -> p (kc c)"), wkv_b)])
                        for tb0 in range(0, len(tiles), 2):
                            tbs = tiles[tb0:tb0 + 2]
                            RB = range(len(tbs))
                            bkq = [ringA.next() for _ in RB]
                            for i, (tc0, tw) in enumerate(tbs):
                                proj_fm(wi, 0, tc0, tw, bkq[i])
                            bkz = [ringA.next() for _ in RB]
                            for i, (tc0, tw) in enumerate(tbs):
                                proj_fm(wi, 1024, tc0, tw, bkz[i])
                            gq = [galloc() for _ in RB]
                            gs = [galloc() for _ in RB]
                            gr = [galloc() for _ in RB]
                            sqbs = [G[gs[i]][:].bitcast(BF16) for i in RB]
                            for i, (tc0, tw) in enumerate(tbs):
                                op(ACT, lambda: nc.scalar.activation(out=sqbs[i][:, 0:tw], in_=PS[bkq[i]][:, 0:tw], func=AF.Square),
                                   reads=[PSb[bkq[i]]], writes=[Gb[gs[i]]])
                            for i, (tc0, tw) in enumerate(tbs):
                                op(DVE, lambda: nc.vector.tensor_copy(out=G[gq[i]][:, 0:tw], in_=PS[bkq[i]][:, 0:tw]),
                                   reads=[PSb[bkq[i]]], writes=[Gb[gq[i]]])
                            bk2 = [ringB.next() for _ in RB]
                            for i, (tc0, tw) in enumerate(tbs):
                                op(PE, lambda: nc.tensor.matmul(PS[bk2[i]][:, 0:tw], lhsT=bones[:], rhs=sqbs[i][:, 0:tw],
                                                                start=True, stop=True),
                                   reads=[Gb[gs[i]], CST], writes=[PSb[bk2[i]]])
                            for i, (tc0, tw) in enumerate(tbs):
                                op(ACT, lambda: nc.scalar.activation(out=G[gr[i]][:, 0:tw], in_=PS[bk2[i]][:, 0:tw], func=AF.Sqrt,
                                                                     scale=1.0 / 64, bias=EPS),
                                   reads=[PSb[bk2[i]]], writes=[Gb[gr[i]]])
                            for i, (tc0, tw) in enumerate(tbs):
                                op(ACT, lambda: nc.scalar.activation(out=ZB[:, tc0:tc0 + tw], in_=PS[bkz[i]][:, 0:tw], func=AF.Silu),
                                   reads=[PSb[bkz[i]]], writes=[ZBb])
                            for i, (tc0, tw) in enumerate(tbs):
                                op(DVE, lambda: nc.vector.reciprocal(out=G[gr[i]][:, 0:tw], in_=G[gr[i]][:, 0:tw]),
                                   reads=[Gb[gr[i]]], writes=[Gb[gr[i]]])
                            for i, (tc0, tw) in enumerate(tbs):
                                for m_ in range(2):
                                    ps_ = slice(64 * m_, 64 * m_ + 64)
                                    op(DVE, lambda: nc.vector.scalar_tensor_tensor(out=QT[ps_, m_, tc0:tc0 + tw],
                                                                                   in0=G[gq[i]][ps_, 0:tw],
                                                                                   scalar=gq2[ps_, 0:1], in1=G[gr[i]][ps_, 0:tw],
                                                                                   op0=ALU.mult, op1=ALU.mult),
                                       reads=[Gb[gq[i]], Gb[gr[i]], CST], writes=[QTb])
                            grel(*gq)
                            grel(*gs)
                            grel(*gr)
                        chk("s1")
                        if not sample:
                            s2_prompt_kv(h, wi, subs, k_dst, v_dst)
                            chk("kv")
                            s2_prompt_attn(h)
                            chk("attn")
                        else:
                            s2_sample(h, wi, k_dst, v_dst)

                    for ti, (tc0, tw) in enumerate(tiles):
                        chk("heads")
                        s3(sample, ti, tc0, tw, nseg, L, len(tiles))
                        chk("s3")
                        s4(tc0, tw)
                        chk("s4")
                        s5(tc0, tw, x_src, y_dst)
                        chk("s5")
                    cdst = conv_s if sample else conv_p[b_idx:b_idx + 1]
                    ldst = lru_s if sample else lru_p[b_idx:b_idx + 1]
                    for sg in range(nseg):
                        for j in range(3):
                            dma(SP, cdst[sg, j].rearrange("(c p) -> p c", p=128), CONVST[:, sg, :, j],
                                reads=[CONVSTb], track=CONVSTb, is_output=True)
                    for sg in range(nseg):
                        dma(SP, ldst[sg].rearrange("(c p) -> p c", p=128), HC[:, :, sg],
                            reads=[HCb], track=HCb, is_output=True)

                def kv_batch(h, wi, items, vsi, k_dst, v_dst, pbf, trb):
                    n = len(items)
                    bks = []
                    for (s0, sn, kt_idx, pcol) in items:
                        bk = ringA.next()
                        bks.append(bk)

                        def f():
                            ins = None
                            for kc in range(8):
                                ins = nc.tensor.matmul(PS[bk][0:sn, 0:256], lhsT=XT[:, kc, s0:s0 + sn],
                                                       rhs=WR[wi][:, 2048 + kc * 256: 2048 + (kc + 1) * 256],
                                                       start=(kc == 0), stop=(kc == 7))
                            return ins
                        op(PE, f, reads=[WRb[wi], XTb], writes=[PSb[bk]])
                    gsq = [galloc() for _ in range(n)]
                    gkv = [galloc() for _ in range(n)]
                    sss = [small() for _ in range(n)]
                    t1s = [small() for _ in range(n)]
                    t2s = [small() for _ in range(n)]
                    for i, (s0, sn, kt_idx, pcol) in enumerate(items):
                        op(ACT, lambda: nc.scalar.activation(out=G[gsq[i]][0:sn, 0:128], in_=PS[bks[i]][0:sn, 0:128],
                                                             func=AF.Square), reads=[PSb[bks[i]]], writes=[Gb[gsq[i]]])
                    for i, (s0, sn, kt_idx, pcol) in enumerate(items):
                        op(DVE, lambda: nc.vector.tensor_reduce(out=sss[i][0][0:sn, 0:2],
                                                                in_=G[gsq[i]][0:sn, 0:128].rearrange("p (g d) -> p g d", g=2),
                                                                axis=AX.X, op=ALU.add),
                           reads=[Gb[gsq[i]]], writes=[sss[i][1]])
                    for i, (s0, sn, kt_idx, pcol) in enumerate(items):
                        op(DVE, lambda: nc.vector.tensor_scalar(out=t1s[i][0][0:sn, 0:2], in0=sss[i][0][0:sn, 0:2],
                                                                scalar1=1.0 / 64, scalar2=EPS, op0=ALU.mult, op1=ALU.add),
                           reads=[sss[i][1]], writes=[t1s[i][1]])
                    for i, (s0, sn, kt_idx, pcol) in enumerate(items):
                        op(ACT, lambda: nc.scalar.activation(out=t1s[i][0][0:sn, 0:2], in_=t1s[i][0][0:sn, 0:2], func=AF.Sqrt),
                           reads=[t1s[i][1]], writes=[t1s[i][1]])
                    for i, (s0, sn, kt_idx, pcol) in enumerate(items):
                        op(ACT, lambda: nc.scalar.copy(out=G[gkv[i]][0:sn, 128:256], in_=PS[bks[i]][0:sn, 128:256]),
                           reads=[PSb[bks[i]]], writes=[Gb[gkv[i]]])
                    for i, (s0, sn, kt_idx, pcol) in enumerate(items):
                        op(DVE, lambda: nc.vector.reciprocal(out=t2s[i][0][0:sn, 0:2], in_=t1s[i][0][0:sn, 0:2]),
                           reads=[t1s[i][1]], writes=[t2s[i][1]])
                    for i, (s0, sn, kt_idx, pcol) in enumerate(items):
                        op(DVE, lambda: nc.vector.tensor_tensor(out=G[gkv[i]][0:sn, 0:128].rearrange("p (g d) -> p g d", g=2),
                                                                in0=PS[bks[i]][0:sn, 0:128].rearrange("p (g d) -> p g d", g=2),
                                                                in1=t2s[i][0][0:sn, 0:2].unsqueeze(2).to_broadcast([sn, 2, 64]),
                                                                op=ALU.mult),
                           reads=[PSb[bks[i]], t2s[i][1]], writes=[Gb[gkv[i]]])
                    for i, (s0, sn, kt_idx, pcol) in enumerate(items):
                        op(DVE, lambda: nc.vector.tensor_tensor(out=G[gkv[i]][0:sn, 0:128], in0=G[gkv[i]][0:sn, 0:128],
                                                                in1=gk_rep[0:sn, :], op=ALU.mult),
                           reads=[Gb[gkv[i]], CST], writes=[Gb[gkv[i]]])
                    for i, (s0, sn, kt_idx, pcol) in enumerate(items):
                        dma(SP, k_dst[s0:s0 + sn, h * 128:(h + 1) * 128], G[gkv[i]][0:sn, 0:128], reads=[Gb[gkv[i]]],
                            track=Gb[gkv[i]], is_output=True)
                        dma(SP, v_dst[s0:s0 + sn, h * 128:(h + 1) * 128], G[gkv[i]][0:sn, 128:256], reads=[Gb[gkv[i]]],
                            track=Gb[gkv[i]], is_output=True)
                    for i, (s0, sn, kt_idx, pcol) in enumerate(items):
                        knb = G[gsq[i]][:].bitcast(BF16)
                        op(POOL, lambda: nc.gpsimd.tensor_copy(out=knb[0:sn, 0:128], in_=G[gkv[i]][0:sn, 0:128]),
                           reads=[Gb[gkv[i]]], writes=[Gb[gsq[i]]])
                        op(POOL, lambda: nc.gpsimd.tensor_copy(out=VS[vsi][0:sn, kt_idx, 0:128], in_=G[gkv[i]][0:sn, 128:256]),
                           reads=[Gb[gkv[i]]], writes=[VSb[vsi]])
                    for i, (s0, sn, kt_idx, pcol) in enumerate(items):
                        knb = G[gsq[i]][:].bitcast(BF16)
                        op(PE, lambda: nc.tensor.transpose(pbf[:, pcol:pcol + sn], knb[0:sn, 0:128], ident[0:sn, 0:sn]),
                           reads=[Gb[gsq[i]], CST], writes=[trb])
                    grel(*gsq)
                    grel(*gkv)

                def s2_prompt_kv(h, wi, subs, k_dst, v_dst):
                    for g4 in range(0, 16, 4):
                        bkt = ringB.next()
                        pbf = PS[bkt][:].bitcast(BF16)
                        items = [(subs[g4 + j][0], subs[g4 + j][1], g4 + j, j * 128) for j in range(4)]
                        kv_batch(h, wi, items, 0, k_dst, v_dst, pbf, PSb[bkt])
                        op(DVE, lambda: nc.vector.tensor_copy(out=KT[:, g4 * 128:g4 * 128 + 512], in_=pbf[:, 0:512]),
                           reads=[PSb[bkt]], writes=[KTb])

                def attn_post_a(h, obank, nq, ycol):
                    ov = PS[obank][0:nq, 0:258].rearrange("p (m d) -> p m d", m=2)
                    rz, rzb = small()
                    op(DVE, lambda: nc.vector.reciprocal(out=rz[0:nq, 0:2], in_=ov[:, :, 128]),
                       reads=[PSb[obank]], writes=[rzb])
                    op(DVE, lambda: nc.vector.tensor_tensor(out=rz[0:nq, 2:3], in0=rz[0:nq, 1:2], in1=lamc[0:nq, 1:2],
                                                            op=ALU.mult), reads=[rzb, CST], writes=[rzb])
                    go = galloc()
                    op(DVE, lambda: nc.vector.tensor_scalar(out=G[go][0:nq, 0:128], in0=ov[:, 0, 0:128],
                                                            scalar1=rz[0:nq, 0:1], scalar2=None, op0=ALU.mult),
                       reads=[PSb[obank], rzb], writes=[Gb[go]])
                    op(DVE, lambda: nc.vector.scalar_tensor_tensor(out=G[go][0:nq, 0:128], in0=ov[:, 1, 0:128],
                                                                   scalar=rz[0:nq, 2:3], in1=G[go][0:nq, 0:128],
                                                                   op0=ALU.mult, op1=ALU.add),
                       reads=[PSb[obank], rzb, Gb[go]], writes=[Gb[go]])
                    ss, ssb = small()
                    op(ACT, lambda: nc.scalar.activation(out=G[go][0:nq, 128:256], in_=G[go][0:nq, 0:128], func=AF.Square,
                                                         accum_out=ss[0:nq, 0:1]),
                       reads=[Gb[go]], writes=[Gb[go], ssb])
                    rs, rsb = rstd_from_ss(ss[0:nq, 0:1], ssb, 1, 1.0 / 128, np_=nq)
                    onb = G[go][:].bitcast(BF16)
                    op(ACT, lambda: nc.scalar.activation(out=onb[0:nq, 512:640], in_=G[go][0:nq, 0:128], func=AF.Copy,
                                                         scale=rs[0:nq, 0:1]),
                       reads=[Gb[go], rsb], writes=[Gb[go]])
                    return (h, go, nq, ycol)

                def attn_post_b(st):
                    h, go, nq, ycol = st
                    onb = G[go][:].bitcast(BF16)
                    bk = ringA.next()
                    pbf = PS[bk][:].bitcast(BF16)
                    op(PE, lambda: nc.tensor.transpose(pbf[:, 0:nq], onb[0:nq, 512:640], ident[0:nq, 0:nq]),
                       reads=[Gb[go], CST], writes=[PSb[bk]])
                    op(DVE, lambda: nc.vector.scalar_tensor_tensor(out=YB[:, h, ycol:ycol + nq], in0=pbf[:, 0:nq],
                                                                   scalar=sg8[:, 0:1], in1=ZB[:, ycol:ycol + nq],
                                                                   op0=ALU.mult, op1=ALU.mult),
                       reads=[PSb[bk], ZBb, CST], writes=[YBb])
                    grel(go)

                def attn_post(h, obank, nq, ycol):
                    attn_post_b(attn_post_a(h, obank, nq, ycol))

                def s2_prompt_attn(h):
                    pending = []
                    for t in range(8):
                        ob = [ringB.next(), ringB.next()]
                        nkt = 2 * t + 2

                        def sc_exp(kt):
                            c0 = 0 if kt <= 2 * t else 128
                            bk = ringA.next()
                            sv = PS[bk][:].rearrange("p (m q) -> p m q", m=2)

                            def sc():
                                blocks = [ql for ql in range(2) if 0 <= (2 * t + ql) - kt <= 1]
                                if c0 == 0:
                                    ins = nc.tensor.matmul(sv[:, :, 0:256], lhsT=KT[:, kt * 128:(kt + 1) * 128],
                                                           rhs=QT[:, :, t * 256:t * 256 + 256],
                                                           start=True, stop=(len(blocks) == 0), skip_group_check=True)
                                else:
                                    for m in range(2):
                                        ins = nc.tensor.matmul(sv[:, m, c0:256], lhsT=KT[:, kt * 128:(kt + 1) * 128],
                                                               rhs=QT[:, m, t * 256 + c0:t * 256 + 256],
                                                               start=(m == 0), stop=False, skip_group_check=True)
                                for bi, ql in enumerate(blocks):
                                    bt = BDb if (2 * t + ql) == kt else BSb
                                    for m in range(2):
                                        ins = nc.tensor.matmul(sv[:, m, ql * 128:(ql + 1) * 128], lhsT=ident[:],
                                                               rhs=bt[:, h, :], start=False,
                                                               stop=(bi == len(blocks) - 1 and m == 1),
                                                               skip_group_check=True)
                                return ins
                            op(PE, sc, reads=[KTb, QTb, CST], writes=[PSb[bk]])
                            ei = ering.next()
                            ev = ET[ei][:].rearrange("p (m q) -> p m q", m=2)
                            op(ACT, lambda: nc.scalar.activation(out=ev[:, :, c0:256], in_=sv[:, :, c0:256], func=AF.Exp,
                                                                 scale=0.125, bias=c15[:, h:h + 1]),
                               reads=[PSb[bk], CST], writes=[ETb[ei]])
                            return ei, ev

                        def pvf(kt, ei, ev):
                            def pv():
                                ins = None
                                for ql in range(2):
                                    if kt > 2 * t + ql:
                                        continue
                                    last = (kt == 2 * t + ql)
                                    for m in range(2):
                                        ins = nc.tensor.matmul(PS[ob[ql]][:, m * 129:(m + 1) * 129],
                                                               lhsT=ev[:, m, ql * 128:(ql + 1) * 128],
                                                               rhs=VS[0][:, kt, 0:129], start=(kt == 0 and m == 0), stop=last,
                                                               skip_group_check=True)
                                return ins
                            op(PE, pv, reads=[ETb[ei], VSb[0]], writes=[PSb[ob[0]], PSb[ob[1]]])

                        LA = 2
                        q_ = deque()
                        for kt in range(min(LA, nkt)):
                            q_.append(sc_exp(kt))
                        for kt in range(nkt):
                            if kt + LA < nkt:
                                q_.append(sc_exp(kt + LA))
                            if kt == 1:
                                for st_ in pending:
                                    attn_post_b(st_)
                                pending = []
                            pvf(kt, *q_.popleft())
                        for ql in range(2):
                            pending.append(attn_post_a(h, ob[ql], 128, t * 256 + ql * 128))
                    for st_ in pending:
                        attn_post_b(st_)

                cache_loaded = set()

                def cache_prefetch(un):
                    if un in cache_loaded or un >= len(heads) * BL:
                        return
                    cache_loaded.add(un)
                    h2, b2 = heads[un // BL], un % BL
                    v2 = (h2 * BL + b2) % NVS
                    dma(POOL, KC[v2][:], ck[b2, :, h2 * 128:(h2 + 1) * 128].rearrange("(kt p) c -> p kt c", p=128),
                        writes=[KCb[v2]])
                    dma(POOL, VS[v2][:, 0:32, 0:128],
                        cv[b2, :, h2 * 128:(h2 + 1) * 128].rearrange("(kt p) c -> p kt c", p=128), writes=[VSb[v2]])

                def s2_sample(h, wi, k_dst, v_dst):
                    for b in range(BL):
                        u = h * BL + b
                        vsi = u % NVS
                        kci = u % NVS
                        hi = list(heads).index(h)
                        uu = hi * BL + b
                        cache_prefetch(uu)
                        cache_prefetch(uu + 1)
                        for g8 in range(0, 32, 8):
                            bkt = ringB.next()
                            pbf = PS[bkt][:].bitcast(BF16)

                            def tr():
                                ins = None
                                for j in range(8):
                                    ins = nc.tensor.transpose(pbf[:, j * 128:(j + 1) * 128], KC[kci][:, g8 + j, :], ident[:])
                                return ins
                            op(PE, tr, reads=[KCb[kci], CST], writes=[PSb[bkt]])
                            op(DVE, lambda: nc.vector.tensor_copy(out=KT[:, g8 * 128:(g8 + 8) * 128], in_=pbf[:, 0:1024]),
                               reads=[PSb[bkt]], writes=[KTb])
                        bkt = ringB.next()
                        pbf = PS[bkt][:].bitcast(BF16)
                        kv_batch(h, wi, [(b * TS, TS, 32, 0)], vsi, k_dst, v_dst, pbf, PSb[bkt])
                        op(DVE, lambda: nc.vector.tensor_copy(out=KT[:, PAST:PAST + TS], in_=pbf[:, 0:TS]),
                           reads=[PSb[bkt]], writes=[KTb])
                        obank = ringB.next()
                        for g4 in range(0, 33, 4):
                            kts = list(range(g4, min(g4 + 4, 33)))
                            bk = ringA.next()
                            sv = PS[bk][:].rearrange("p (j m q) -> p j m q", j=4, m=2)

                            def sc():
                                ins = None
                                for j, kt in enumerate(kts):
                                    nk = 128 if kt < 32 else TS
                                    nb = kt >= 31
                                    ins = nc.tensor.matmul(sv[0:nk, j, :, :], lhsT=KT[:, kt * 128:kt * 128 + nk],
                                                           rhs=QT[:, :, b * TS:(b + 1) * TS],
                                                           start=True, stop=not nb, skip_group_check=True)
                                    for m in range(2):
                                        if kt == 31:
                                            ins = nc.tensor.matmul(sv[:, j, m, :], lhsT=ident[:], rhs=BSb[:, h, 0:TS],
                                                                   start=False, stop=(m == 1), skip_group_check=True)
                                        if kt == 32:
                                            ins = nc.tensor.matmul(sv[0:TS, j, m, :], lhsT=ident[0:TS, 0:TS],
                                                                   rhs=BDb[0:TS, h, 0:TS], start=False, stop=(m == 1),
                                                                   skip_group_check=True)
                                return ins
                            op(PE, sc, reads=[KTb, QTb, CST], writes=[PSb[bk]])
                            ei = ering.next()
                            ev = ET[ei][:].rearrange("p (j m q) -> p j m q", j=4, m=2)
                            nfull = len([kt for kt in kts if kt < 32])
                            if nfull:
                                op(ACT, lambda: nc.scalar.activation(out=ev[:, 0:nfull], in_=sv[:, 0:nfull], func=AF.Exp,
                                                                     scale=0.125, bias=c15[:, h:h + 1]),
                                   reads=[PSb[bk], CST], writes=[ETb[ei]])
                            if kts[-1] == 32:
                                j = len(kts) - 1
                                op(ACT, lambda: nc.scalar.activation(out=ev[0:TS, j], in_=sv[0:TS, j], func=AF.Exp,
                                                                     scale=0.125, bias=c15[0:TS, h:h + 1]),
                                   reads=[PSb[bk], CST], writes=[ETb[ei]])

                            def pv():
                                ins = None
                                for j, kt in enumerate(kts):
                                    nk = 128 if kt < 32 else TS
                                    for m in range(2):
                                        ins = nc.tensor.matmul(PS[obank][0:TS, m * 129:(m + 1) * 129],
                                                               lhsT=ev[0:nk, j, m, :], rhs=VS[vsi][0:nk, kt, 0:129],
                                                               start=(kt == 0 and m == 0), stop=(kt == 32), skip_group_check=True)
                                return ins
                            op(PE, pv, reads=[ETb[ei], VSb[vsi]], writes=[PSb[obank]])
                        attn_post(h, obank, TS, b * TS)

                def s3(sample, ti, tc0, tw, nseg, L, ntiles):
                    for c0_ in range(0, 8, 2):
                        cs = [c0_, c0_ + 1]
                        R = range(len(cs))
                        wis = [wload([(0, 1024, WF[c].rearrange("p kc c -> p (kc c)"), wf_b[0]),
                                      (1024, 1024, WF[8 + c].rearrange("p kc c -> p (kc c)"), wf_b[1])]) for c in cs]
                        xavs = [XAws[i][:, 0:nseg * (L + 3)].rearrange("p (s l) -> p s l", s=nseg) for i in R]
                        for i, c in enumerate(cs):
                            xav = xavs[i]
                            if sample:
                                op(DVE, lambda: nc.vector.tensor_copy(out=xav[:, :, 0:3], in_=SCV[:, :, c, :]),
                                   reads=[CST], writes=[XAwbs[i]])
                            elif ti == 0:
                                op(DVE, lambda: nc.vector.memset(xav[:, :, 0:3], 0.0), writes=[XAwbs[i]])
                            else:
                                op(DVE, lambda: nc.vector.tensor_copy(out=xav[:, :, 0:3], in_=HALO[:, c, 0:nseg, :]),
                                   reads=[HALOb], writes=[XAwbs[i]])
                        bkx = [ringAll.next() for _ in R]
                        for i, c in enumerate(cs):
                            proj_fm2(wis[i], 0, tc0, tw, bkx[i])
                        for i, c in enumerate(cs):
                            op(ACT, lambda: nc.scalar.copy(out=xavs[i][:, :, 3:3 + L],
                                                           in_=PS[bkx[i]][:, 0:tw].rearrange("p (s l) -> p s l", s=nseg)),
                               reads=[PSb[bkx[i]]], writes=[XAwbs[i]])
                        for i, c in enumerate(cs):
                            op(POOL, lambda: nc.gpsimd.tensor_copy(out=HALO[:, c, 0:nseg, :], in_=xavs[i][:, :, L:L + 3]),
                               reads=[XAwbs[i]], writes=[HALOb])
                            if sample or ti == ntiles - 1:
                                op(POOL, lambda: nc.gpsimd.tensor_copy(out=CONVST[:, 0:nseg, c, :], in_=xavs[i][:, :, L:L + 3]),
                                   reads=[XAwbs[i]], writes=[CONVSTb])
                        gx = [galloc() for _ in R]
                        xcvs = [G[gx[i]][:, 0:tw].rearrange("p (s l) -> p s l", s=nseg) for i in R]
                        for i, c in enumerate(cs):
                            op(DVE, lambda: nc.vector.tensor_scalar(out=xcvs[i], in0=xavs[i][:, :, 0:L], scalar1=cw[:, c, 0:1],
                                                                    scalar2=cbias[:, c:c + 1], op0=ALU.mult, op1=ALU.add),
                               reads=[XAwbs[i], CST], writes=[Gb[gx[i]]])
                        for j in range(1, 4):
                            for i, c in enumerate(cs):
                                op(DVE, lambda: nc.vector.scalar_tensor_tensor(out=xcvs[i], in0=xavs[i][:, :, j:j + L],
                                                                               scalar=cw[:, c, j:j + 1], in1=xcvs[i],
                                                                               op0=ALU.mult, op1=ALU.add),
                                   reads=[XAwbs[i], Gb[gx[i]], CST], writes=[Gb[gx[i]]])
                        gxb = [galloc() for _ in R]
                        xcbs = [G[gxb[i]][:].bitcast(BF16) for i in R]
                        for i, c in enumerate(cs):
                            op(ACT, lambda: nc.scalar.copy(out=xcbs[i][:, 0:tw], in_=G[gx[i]][:, 0:tw]),
                               reads=[Gb[gx[i]]], writes=[Gb[gxb[i]]])
                        bkr = [ringAll.next() for _ in R]
                        bki = [ringAll.next() for _ in R]
                        for i, c in enumerate(cs):
                            op(PE, lambda: nc.tensor.matmul(PS[bkr[i]][:, 0:tw], lhsT=WRG[:, c, :], rhs=xcbs[i][:, 0:tw],
                                                            start=True, stop=True), reads=[Gb[gxb[i]], CST], writes=[PSb[bkr[i]]])
                            op(PE, lambda: nc.tensor.matmul(PS[bki[i]][:, 0:tw], lhsT=WIG[:, c, :], rhs=xcbs[i][:, 0:tw],
                                                            start=True, stop=True), reads=[Gb[gxb[i]], CST], writes=[PSb[bki[i]]])
                        bkz = [ringAll.next() for _ in R]
                        for i, c in enumerate(cs):
                            proj_fm2(wis[i], 1024, tc0, tw, bkz[i])
                        gr_ = [galloc() for _ in R]
                        gi_ = [galloc() for _ in R]
                        ga_ = [galloc() for _ in R]
                        for i, c in enumerate(cs):
                            op(ACT, lambda: nc.scalar.activation(out=G[gr_[i]][:, 0:tw], in_=PS[bkr[i]][:, 0:tw], func=AF.Sigmoid,
                                                                 bias=brg[:, c:c + 1]), reads=[PSb[bkr[i]], CST], writes=[Gb[gr_[i]]])
                            op(ACT, lambda: nc.scalar.activation(out=G[gi_[i]][:, 0:tw], in_=PS[bki[i]][:, 0:tw], func=AF.Sigmoid,
                                                                 bias=big[:, c:c + 1]), reads=[PSb[bki[i]], CST], writes=[Gb[gi_[i]]])
                        for i, c in enumerate(cs):
                            op(ACT, lambda: nc.scalar.activation(out=G[ga_[i]][:, 0:tw], in_=G[gr_[i]][:, 0:tw], func=AF.Exp,
                                                                 scale=nsp[:, c:c + 1]), reads=[Gb[gr_[i]], CST], writes=[Gb[ga_[i]]])
                        for i, c in enumerate(cs):
                            op(DVE, lambda: nc.vector.tensor_tensor(out=G[gi_[i]][:, 0:tw], in0=G[gi_[i]][:, 0:tw],
                                                                    in1=G[gx[i]][:, 0:tw], op=ALU.mult),
                               reads=[Gb[gi_[i]], Gb[gx[i]]], writes=[Gb[gi_[i]]])
                        for i, c in enumerate(cs):
                            op(DVE, lambda: nc.vector.tensor_tensor(out=G[gr_[i]][:, 0:tw], in0=G[ga_[i]][:, 0:tw],
                                                                    in1=G[ga_[i]][:, 0:tw], op=ALU.mult),
                               reads=[Gb[ga_[i]]], writes=[Gb[gr_[i]]])
                        for i, c in enumerate(cs):
                            op(ACT, lambda: nc.scalar.activation(out=G[gr_[i]][:, 0:tw], in_=G[gr_[i]][:, 0:tw], func=AF.Sqrt,
                                                                 scale=-1.0, bias=1.0), reads=[Gb[gr_[i]]], writes=[Gb[gr_[i]]])
                        for i, c in enumerate(cs):
                            if (not sample) and ti == 0:
                                op(DVE, lambda: nc.vector.memset(G[gr_[i]][:, 0:1], 1.0), writes=[Gb[gr_[i]]])
                            op(DVE, lambda: nc.vector.tensor_tensor(out=G[gi_[i]][:, 0:tw], in0=G[gi_[i]][:, 0:tw],
                                                                    in1=G[gr_[i]][:, 0:tw], op=ALU.mult),
                               reads=[Gb[gi_[i]], Gb[gr_[i]]], writes=[Gb[gi_[i]]])
                        for i, c in enumerate(cs):
                            for sg in range(nseg):
                                if sample:
                                    init = LR0[:, sg, c:c + 1]
                                    rdi = [CST]
                                elif ti == 0:
                                    init = 0.0
                                    rdi = []
                                else:
                                    init = HC[:, c, sg:sg + 1]
                                    rdi = [HCb]
                                op(DVE, lambda: nc.vector.tensor_tensor_scan(out=G[gx[i]][:, sg * L:(sg + 1) * L],
                                                                             data0=G[ga_[i]][:, sg * L:(sg + 1) * L],
                                                                             data1=G[gi_[i]][:, sg * L:(sg + 1) * L],
                                                                             initial=init, op0=ALU.mult, op1=ALU.add),
                                   reads=[Gb[ga_[i]], Gb[gi_[i]]] + rdi, writes=[Gb[gx[i]]])
                        for i, c in enumerate(cs):
                            hv = G[gx[i]][:, 0:tw].rearrange("p (s l) -> p s l", s=nseg)
                            op(POOL, lambda: nc.gpsimd.tensor_copy(out=HC[:, c, 0:nseg], in_=hv[:, :, L - 1]),
                               reads=[Gb[gx[i]]], writes=[HCb])
                        for i, c in enumerate(cs):
                            op(ACT, lambda: nc.scalar.activation(out=G[ga_[i]][:, 0:tw], in_=PS[bkz[i]][:, 0:tw], func=AF.Silu),
                               reads=[PSb[bkz[i]]], writes=[Gb[ga_[i]]])
                        for i, c in enumerate(cs):
                            op(DVE, lambda: nc.vector.tensor_tensor(out=YA[:, c, 0:tw], in0=G[gx[i]][:, 0:tw],
                                                                    in1=G[ga_[i]][:, 0:tw], op=ALU.mult),
                               reads=[Gb[gx[i]], Gb[ga_[i]]], writes=[YAb])
                        grel(*gx)
                        grel(*gxb)
                        grel(*gr_)
                        grel(*gi_)
                        grel(*ga_)

                def proj_fm2(wi, woff, tcol, tw, bk):
                    def f():
                        ins = None
                        for kc in range(8):
                            ins = nc.tensor.matmul(PS[bk][:, 0:tw], lhsT=WR[wi][:, woff + kc * 128: woff + (kc + 1) * 128],
                                                   rhs=XT[:, kc, tcol:tcol + tw], start=(kc == 0), stop=(kc == 7))
                        return ins
                    op(PE, f, reads=[WRb[wi], XTb], writes=[PSb[bk]])

                def s4(tc0, tw):
                    for j in range(8):
                        wi = wload([(0, 1024, WP[0, j].rearrange("p c col -> p (c col)"), wp_b),
                                    (1024, 1024, WP[1, j].rearrange("p c col -> p (c col)"), wp_b),
                                    (2048, 1024, WF[32 + j].rearrange("p kc c -> p (kc c)"), wf_b[4]),
                                    (3072, 1024, WF[40 + j].rearrange("p kc c -> p (kc c)"), wf_b[5])])
                        bpa, bpb, bga, bgb = ringAll.next(), ringAll.next(), ringAll.next(), ringAll.next()

                        def mk(bk, woff, src, srcb):
                            def f():
                                ins = None
                                for cc in range(8):
                                    rhs = src[:, cc, 0:tw] if src is YA else src[:, cc, tc0:tc0 + tw]
                                    ins = nc.tensor.matmul(PS[bk][:, 0:tw],
                                                           lhsT=WR[wi][:, woff + cc * 128: woff + (cc + 1) * 128],
                                                           rhs=rhs, start=(cc == 0), stop=(cc == 7))
                                return ins
                            op(PE, f, reads=[WRb[wi], srcb], writes=[PSb[bk]])
                        mk(bga, 2048, XT, XTb)
                        mk(bgb, 3072, XT, XTb)
                        mk(bpa, 0, YA, YAb)
                        mk(bpb, 1024, YB, YBb)
                        g1, g2 = galloc(), galloc()
                        op(ACT, lambda: nc.scalar.activation(out=G[g1][:, 0:tw], in_=PS[bga][:, 0:tw], func=AF.Sigmoid),
                           reads=[PSb[bga]], writes=[Gb[g1]])
                        op(ACT, lambda: nc.scalar.activation(out=G[g2][:, 0:tw], in_=PS[bgb][:, 0:tw], func=AF.Sigmoid),
                           reads=[PSb[bgb]], writes=[Gb[g2]])
                        op(DVE, lambda: nc.vector.tensor_tensor(out=G[g1][:, 0:tw], in0=PS[bpa][:, 0:tw], in1=G[g1][:, 0:tw],
                                                                op=ALU.mult), reads=[PSb[bpa], Gb[g1]], writes=[Gb[g1]])
                        op(DVE, lambda: nc.vector.tensor_tensor(out=G[g2][:, 0:tw], in0=PS[bpb][:, 0:tw], in1=G[g2][:, 0:tw],
                                                                op=ALU.mult), reads=[PSb[bpb], Gb[g2]], writes=[Gb[g2]])
                        op(POOL, lambda: nc.gpsimd.tensor_tensor(out=MM[:, j, 0:tw], in0=G[g1][:, 0:tw], in1=G[g2][:, 0:tw],
                                                                 op=ALU.add), reads=[Gb[g1], Gb[g2]], writes=[MMb])
                        grel(g1, g2)

                def s5(tc0, tw, x_src, y_dst):
                    wis = [wload([(0, 4096, WO[n].rearrange("p j col -> p (j col)"), wo_b)]) for n in range(2)]
                    for s0 in range(0, tw, 128):
                        for n in range(2):
                            wi = wis[n]
                            bk = ringAll.next()

                            def f():
                                ins = None
                                for j in range(8):
                                    ins = nc.tensor.matmul(PS[bk][:, 0:512], lhsT=MM[:, j, s0:s0 + 128],
                                                           rhs=WR[wi][:, j * 512:(j + 1) * 512],
                                                           start=(j == 0), stop=(j == 7))
                                return ins
                            op(PE, f, reads=[WRb[wi], MMb], writes=[PSb[bk]])
                            gx_ = galloc()
                            r0 = tc0 + s0
                            dma(SP, G[gx_][:, :], x_src[r0:r0 + 128, n * 512:(n + 1) * 512], writes=[Gb[gx_]])
                            op(DVE, lambda: nc.vector.tensor_tensor(out=G[gx_][:, :], in0=PS[bk][:, 0:512], in1=G[gx_][:, :],
                                                                    op=ALU.add), reads=[PSb[bk], Gb[gx_]], writes=[Gb[gx_]])
                            dma(SP, y_dst[r0:r0 + 128, n * 512:(n + 1) * 512], G[gx_][:, :], reads=[Gb[gx_]],
                                track=Gb[gx_], is_output=True)
                            grel(gx_)

                try:
                    chk("pro")
                    for b in jobs:
                        run_job(False, b)
                        kb.new_epoch()
                    if do_sample:
                        run_job(True, None)
                except _Stop:
                    pass
                last = {}
                for ev in kb.out_events:
                    if ev.key not in last or last[ev.key].val < ev.val:
                        last[ev.key] = ev
                for ev in last.values():
                    SP.wait(ev)

            body()

    return nc, kb.used


_CONSTS = None


def _consts():
    global _CONSTS
    if _CONSTS is None:
        bo = np.zeros((128, 128), np.float32)
        bo[0:64, 0:64] = 1.0
        bo[64:128, 64:128] = 1.0
        _CONSTS = {"c_onehot": _bucket_table(), "c_ident": np.eye(128, dtype=np.float32), "c_blockones": bo}
    return _CONSTS


def kernel(**inputs):
    f = lambda a: np.ascontiguousarray(np.asarray(a, dtype=np.float32))
    x_prompt = f(inputs["x_prompt"])
    x_sample = f(inputs["x_sample"])
    cache_k = f(inputs["cache_k"])[0].reshape(32, PAST, 1024)
    cache_v = f(inputs["cache_v"])[0].reshape(32, PAST, 1024)
    state_conv = f(inputs["state_conv"])[0]
    state_lru = f(inputs["state_lru"])[0]
    shared = {
        "norm_gain": f(inputs["norm_gain"]), "w_in": f(inputs["w_in"])[0], "conv_w": f(inputs["conv_w"])[0],
        "conv_b": f(inputs["conv_b"]), "w_rg": f(inputs["w_rg"])[0], "b_rg": f(inputs["b_rg"]),
        "w_ig": f(inputs["w_ig"])[0], "b_ig": f(inputs["b_ig"]), "lru_lambda": f(inputs["lru_lambda"]),
        "q_norm_gain": f(inputs["q_norm_gain"]), "k_norm_gain": f(inputs["k_norm_gain"]),
        "rel_bias": f(inputs["rel_bias"]), "lam_q1": f(inputs["lam_q1"]), "lam_k1": f(inputs["lam_k1"]),
        "lam_q2": f(inputs["lam_q2"]), "lam_k2": f(inputs["lam_k2"]),
        "subln_gain": f(inputs["subln_gain"]).reshape(128, 1),
        "w_proj_a": f(inputs["w_proj_a"])[0], "w_proj_b": f(inputs["w_proj_b"])[0], "w_out": f(inputs["w_out"])[0],
    }
    shared.update(_consts())
    in_maps = []
    for c in range(NCORES):
        sl = slice(c * BL, (c + 1) * BL)
        m = dict(shared)
        m["x_prompt"] = x_prompt[sl]
        m["x_sample"] = x_sample[sl].reshape(BL * TS, D)
        m["cache_k"] = cache_k[sl]
        m["cache_v"] = cache_v[sl]
        m["state_conv"] = state_conv[sl]
        m["state_lru"] = state_lru[sl]
        in_maps.append(m)
    nc = build_program()
    res = run_bass_kernel_spmd(nc, in_maps, core_ids=list(range(NCORES)))
    R = res.results
    cat = lambda name: np.concatenate([np.asarray(r[name], dtype=np.float32) for r in R], axis=0)
    y_prompt = cat("y_prompt")
    y_sample = cat("y_sample").reshape(32, TS, D)
    k_prompt = cat("k_prompt").reshape(1, 32, T, 8, 128)
    v_prompt = cat("v_prompt").reshape(1, 32, T, 8, 128)
    conv_prompt = cat("conv_prompt").reshape(1, 32, 3, 1024)
    lru_prompt = cat("lru_prompt").reshape(1, 32, 1024)
    k_sample = cat("k_sample").reshape(1, 32, TS, 8, 128)
    v_sample = cat("v_sample").reshape(1, 32, TS, 8, 128)
    conv_sample = cat("conv_sample").reshape(1, 32, 3, 1024)
    lru_sample = cat("lru_sample").reshape(1, 32, 1024)
    return (y_prompt, y_sample, k_prompt, v_prompt, conv_prompt, lru_prompt, k_sample, v_sample, conv_sample, lru_sample)
```
